# Optimizing a Trainium2 kernel written in Bass

```python
import math
import jax, jax.numpy as jnp
from jax import lax
import numpy as np

D_MODEL = 2048
BATCH = 2
SEQ = 4096
DEPTH = 1
DEC_BATCH = 16
DEC_SEQ = 16
PAST_LEN = 2048

CHUNK = 64
ATTN_WIDTH = D_MODEL // 2
LRU_WIDTH = D_MODEL - ATTN_WIDTH
HEAD_DIM = 128
HALF_DIM = HEAD_DIM // 2
N_HEADS = ATTN_WIDTH // HEAD_DIM
LRU_BLOCKS = 16
LRU_BLOCK_DIM = LRU_WIDTH // LRU_BLOCKS
CONV_WIDTH = 4
LRU_C = 8.0
D_FF = 4 * D_MODEL
ROPE_THETA = 10000.0
Q_BLOCK = 128
LN_EPS = 1e-5
RMS_EPS = 1e-5
NEG_INF = -1e30
DEEPNORM_ALPHA = (2.0 * DEPTH) ** 0.25
DEEPNORM_BETA = (8.0 * DEPTH) ** -0.25
IN_COLS = 3 * ATTN_WIDTH + 2 * LRU_WIDTH

kernel_name = "hymba_diffattn_rglru_deepnorm_stream_step"


def lambda_init(layer):
    return 0.8 - 0.6 * math.exp(-0.3 * layer)


def layer_norm(x, g, b):
    xf = x.astype(jnp.float32)
    mu = jnp.mean(xf, -1, keepdims=True)
    var = jnp.mean(jnp.square(xf - mu), -1, keepdims=True)
    y = (xf - mu) * lax.rsqrt(var + LN_EPS) * g.astype(jnp.float32) + b.astype(jnp.float32)
    return y.astype(x.dtype)


def rope(x, pos):
    inv = ROPE_THETA ** (-jnp.arange(0, HALF_DIM, 2, dtype=jnp.float32) / HALF_DIM)
    ang = pos.astype(jnp.float32)[:, None] * inv[None, :]
    cos = jnp.concatenate([jnp.cos(ang), jnp.cos(ang)], -1)[:, None, None, :]
    sin = jnp.concatenate([jnp.sin(ang), jnp.sin(ang)], -1)[:, None, None, :]
    x1, x2 = jnp.split(x.astype(jnp.float32), 2, axis=-1)
    rot = jnp.concatenate([-x2, x1], -1)
    return (x.astype(jnp.float32) * cos + rot * sin).astype(x.dtype)


def diff_attn_prompt(q, k, v, lam):
    B, S = q.shape[0], q.shape[1]
    nblk = S // Q_BLOCK
    k_chunk = jnp.arange(S) // CHUNK
    qb = q.reshape(B, nblk, Q_BLOCK, N_HEADS, 2, HALF_DIM).transpose(1, 0, 2, 3, 4, 5)
    scale = HALF_DIM ** -0.5

    def block(args):
        q_blk, i = args
        s = jnp.einsum('bqhcd,bkhcd->bhcqk', q_blk, k).astype(jnp.float32) * scale
        q_chunk = (i * Q_BLOCK + jnp.arange(Q_BLOCK)) // CHUNK
        mask = q_chunk[:, None] >= k_chunk[None, :]
        p = jax.nn.softmax(jnp.where(mask, s, NEG_INF), axis=-1)
        w = p[:, :, 0] - lam * p[:, :, 1]
        return jnp.einsum('bhqk,bkhe->bqhe', w.astype(v.dtype), v)

    o = lax.map(block, (qb, jnp.arange(nblk)))
    return o.transpose(1, 0, 2, 3, 4).reshape(B, S, N_HEADS, HEAD_DIM)


def diff_attn_sample(q, k, v, cache_k, cache_v, lam):
    B, P = cache_k.shape[0], cache_k.shape[1]
    k_all = jnp.concatenate([cache_k.reshape(B, P, N_HEADS, 2, HALF_DIM), k], axis=1)
    v_all = jnp.concatenate([cache_v, v], axis=1)
    s = jnp.einsum('bqhcd,bkhcd->bhcqk', q, k_all).astype(jnp.float32) * HALF_DIM ** -0.5
    p = jax.nn.softmax(s, axis=-1)
    w = p[:, :, 0] - lam * p[:, :, 1]
    return jnp.einsum('bhqk,bkhe->bqhe', w.astype(v_all.dtype), v_all)


def causal_conv(xb, buf, w, b):
    T = xb.shape[1]
    xp = jnp.concatenate([buf, xb], axis=1)
    y = b + sum(xp[:, j:j + T] * w[j] for j in range(CONV_WIDTH))
    return y, xp[:, -(CONV_WIDTH - 1):]


def rg_lru(xc, h0, w_a, b_a, w_i, b_i, lru_lambda):
    B, T = xc.shape[0], xc.shape[1]
    xr = xc.reshape(B, T, LRU_BLOCKS, LRU_BLOCK_DIM)
    r = jax.nn.sigmoid(jnp.einsum('bthi,hij->bthj', xr, w_a).reshape(B, T, LRU_WIDTH) + b_a)
    ig = jax.nn.sigmoid(jnp.einsum('bthi,hij->bthj', xr, w_i).reshape(B, T, LRU_WIDTH) + b_i)
    log_a = -LRU_C * r.astype(jnp.float32) * jax.nn.softplus(-lru_lambda.astype(jnp.float32))
    a = jnp.exp(log_a)
    u = jnp.sqrt(-jnp.expm1(2.0 * log_a)) * (ig * xc).astype(jnp.float32)

    def step(h, au):
        a_t, u_t = au
        h = a_t * h + u_t
        return h, h

    h_last, hs = lax.scan(step, h0.astype(jnp.float32), (a.transpose(1, 0, 2), u.transpose(1, 0, 2)))
    return hs.transpose(1, 0, 2).astype(xc.dtype), h_last.astype(h0.dtype)


def trunk_layer(x, pos, cache_k, cache_v, conv_buf, h0, p, lam_init):
    B, T = x.shape[0], x.shape[1]
    proj = x @ p['w_in']
    q, k, v, xb, g = jnp.split(proj, [ATTN_WIDTH, 2 * ATTN_WIDTH, 3 * ATTN_WIDTH, 3 * ATTN_WIDTH + LRU_WIDTH], axis=-1)
    q = rope(q.reshape(B, T, N_HEADS, 2, HALF_DIM), pos)
    k = rope(k.reshape(B, T, N_HEADS, 2, HALF_DIM), pos)
    v = v.reshape(B, T, N_HEADS, HEAD_DIM)
    lam = (jnp.exp(jnp.sum(p['lambda_q1'].astype(jnp.float32) * p['lambda_k1'].astype(jnp.float32)))
           - jnp.exp(jnp.sum(p['lambda_q2'].astype(jnp.float32) * p['lambda_k2'].astype(jnp.float32)))
           + lam_init)
    if cache_k is None:
        o = diff_attn_prompt(q, k, v, lam)
    else:
        o = diff_attn_sample(q, k, v, cache_k, cache_v, lam)
    of = o.astype(jnp.float32)
    of = of * lax.rsqrt(jnp.mean(of * of, -1, keepdims=True) + RMS_EPS) * p['subln_g'].astype(jnp.float32)
    o_attn = (of * (1.0 - lam_init)).astype(x.dtype).reshape(B, T, ATTN_WIDTH)
    xc, new_buf = causal_conv(xb, conv_buf, p['conv_w'], p['conv_b'])
    hs, h_last = rg_lru(xc, h0, p['w_rg_a'], p['b_rg_a'], p['w_rg_i'], p['b_rg_i'], p['lru_lambda'])
    o_lru = hs * jax.nn.gelu(g)
    mix = jnp.concatenate([o_attn, o_lru], axis=-1) @ p['w_out']
    x = layer_norm(DEEPNORM_ALPHA * x + mix, p['ln1_g'], p['ln1_b'])
    ffn = jnp.square(jax.nn.relu(x @ p['w_up'])) @ p['w_down']
    x = layer_norm(DEEPNORM_ALPHA * x + ffn, p['ln2_g'], p['ln2_b'])
    return x, k.reshape(B, T, N_HEADS, HEAD_DIM), v, h_last, new_buf


def setup_inputs(seed: int = 0) -> dict:
    key = jax.random.key(seed)
    ks = jax.random.split(key, 24)
    f32 = jnp.float32
    nrm = lambda k, shape, s: jax.random.normal(k, shape, f32) * s
    col_scale = jnp.concatenate([jnp.ones((2 * ATTN_WIDTH,), f32),
                                 jnp.full((ATTN_WIDTH,), DEEPNORM_BETA, f32),
                                 jnp.ones((2 * LRU_WIDTH,), f32)])
    a0 = jax.random.uniform(ks[13], (DEPTH, LRU_WIDTH), f32, 0.9, 0.999)
    return {
        'x_prompt': nrm(ks[0], (BATCH, SEQ, D_MODEL), 1.0),
        'x_sample': nrm(ks[1], (DEC_BATCH, DEC_SEQ, D_MODEL), 1.0),
        'cache_k': nrm(ks[2], (DEPTH, DEC_BATCH, PAST_LEN, N_HEADS, HEAD_DIM), 1.0),
        'cache_v': nrm(ks[3], (DEPTH, DEC_BATCH, PAST_LEN, N_HEADS, HEAD_DIM), 1.0),
        'state_h': nrm(ks[4], (DEPTH, DEC_BATCH, LRU_WIDTH), 0.5),
        'state_conv': nrm(ks[5], (DEPTH, DEC_BATCH, CONV_WIDTH - 1, LRU_WIDTH), 1.0),
        'w_in': nrm(ks[6], (DEPTH, D_MODEL, IN_COLS), D_MODEL ** -0.5) * col_scale,
        'lambda_q1': nrm(ks[7], (DEPTH, HALF_DIM), 0.1),
        'lambda_k1': nrm(ks[8], (DEPTH, HALF_DIM), 0.1),
        'lambda_q2': nrm(ks[9], (DEPTH, HALF_DIM), 0.1),
        'lambda_k2': nrm(ks[10], (DEPTH, HALF_DIM), 0.1),
        'subln_g': 1.0 + nrm(ks[11], (DEPTH, HEAD_DIM), 0.01),
        'conv_w': nrm(ks[12], (DEPTH, CONV_WIDTH, LRU_WIDTH), CONV_WIDTH ** -0.5),
        'conv_b': nrm(ks[14], (DEPTH, LRU_WIDTH), 0.01),
        'w_rg_a': nrm(ks[15], (DEPTH, LRU_BLOCKS, LRU_BLOCK_DIM, LRU_BLOCK_DIM), LRU_BLOCK_DIM ** -0.5),
        'b_rg_a': nrm(ks[16], (DEPTH, LRU_WIDTH), 0.01),
        'w_rg_i': nrm(ks[17], (DEPTH, LRU_BLOCKS, LRU_BLOCK_DIM, LRU_BLOCK_DIM), LRU_BLOCK_DIM ** -0.5),
        'b_rg_i': nrm(ks[18], (DEPTH, LRU_WIDTH), 0.01),
        'lru_lambda': jnp.log(a0 / (1.0 - a0)),
        'w_out': nrm(ks[19], (DEPTH, D_MODEL, D_MODEL), D_MODEL ** -0.5 * DEEPNORM_BETA),
        'ln1_g': 1.0 + nrm(ks[20], (DEPTH, D_MODEL), 0.01),
        'ln1_b': nrm(ks[21], (DEPTH, D_MODEL), 0.01),
        'w_up': nrm(ks[22], (DEPTH, D_MODEL, D_FF), D_MODEL ** -0.5 * DEEPNORM_BETA),
        'w_down': nrm(ks[23], (DEPTH, D_FF, D_MODEL), D_FF ** -0.5 * DEEPNORM_BETA),
        'ln2_g': 1.0 + nrm(jax.random.fold_in(ks[20], 1), (DEPTH, D_MODEL), 0.01),
        'ln2_b': nrm(jax.random.fold_in(ks[21], 1), (DEPTH, D_MODEL), 0.01),
    }


def reference(x_prompt, x_sample, cache_k, cache_v, state_h, state_conv,
              w_in, lambda_q1, lambda_k1, lambda_q2, lambda_k2, subln_g,
              conv_w, conv_b, w_rg_a, b_rg_a, w_rg_i, b_rg_i, lru_lambda,
              w_out, ln1_g, ln1_b, w_up, w_down, ln2_g, ln2_b):
    B, S = x_prompt.shape[0], x_prompt.shape[1]
    T = x_sample.shape[1]
    P = cache_k.shape[2]
    pos_prompt = jnp.arange(S, dtype=jnp.int32)
    pos_sample = P + jnp.arange(T, dtype=jnp.int32)
    yp, ys = x_prompt, x_sample
    kp_l, vp_l, hp_l, cp_l, ks_l, vs_l, hs_l, cs_l = [], [], [], [], [], [], [], []
    for l in range(DEPTH):
        p = {'w_in': w_in[l], 'lambda_q1': lambda_q1[l], 'lambda_k1': lambda_k1[l],
             'lambda_q2': lambda_q2[l], 'lambda_k2': lambda_k2[l], 'subln_g': subln_g[l],
             'conv_w': conv_w[l], 'conv_b': conv_b[l], 'w_rg_a': w_rg_a[l], 'b_rg_a': b_rg_a[l],
             'w_rg_i': w_rg_i[l], 'b_rg_i': b_rg_i[l], 'lru_lambda': lru_lambda[l],
             'w_out': w_out[l], 'ln1_g': ln1_g[l], 'ln1_b': ln1_b[l],
             'w_up': w_up[l], 'w_down': w_down[l], 'ln2_g': ln2_g[l], 'ln2_b': ln2_b[l]}
        lam0 = lambda_init(l)
        zero_buf = jnp.zeros((B, CONV_WIDTH - 1, LRU_WIDTH), x_prompt.dtype)
        zero_h = jnp.zeros((B, LRU_WIDTH), state_h.dtype)
        yp, kp, vp, hp, cp = trunk_layer(yp, pos_prompt, None, None, zero_buf, zero_h, p, lam0)
        ys, kn, vn, hn, cn = trunk_layer(ys, pos_sample, cache_k[l], cache_v[l], state_conv[l], state_h[l], p, lam0)
        kp_l.append(kp); vp_l.append(vp); hp_l.append(hp); cp_l.append(cp)
        ks_l.append(kn); vs_l.append(vn); hs_l.append(hn); cs_l.append(cn)
    k_prompt = jnp.stack(kp_l); v_prompt = jnp.stack(vp_l)
    h_prompt = jnp.stack(hp_l); conv_prompt = jnp.stack(cp_l)
    k_sample = jnp.stack(ks_l); v_sample = jnp.stack(vs_l)
    h_sample = jnp.stack(hs_l); conv_sample = jnp.stack(cs_l)
    return (yp, ys, k_prompt, v_prompt, h_prompt, conv_prompt, k_sample, v_sample, h_sample, conv_sample)
```

```python
import numpy as np
from contextlib import ExitStack
import concourse.bass as bass
import concourse.mybir as mybir
from concourse.bass_utils import run_bass_kernel_spmd

F32 = mybir.dt.float32
BF16 = mybir.dt.bfloat16
ALU = mybir.AluOpType
AF = mybir.ActivationFunctionType

D = 2048
NKC = 16
NSLOT = 4096
BLK = 512
NBLK = 8
OWN0 = 3072
TOWN = 1024
NSMP = 32
T2 = TOWN + NSMP
TB = 352
NTB = 3
LAM_INIT = 0.8 - 0.6 * 1.0
ALPHA = 2.0 ** 0.25
LN_EPS = 1e-5
RMS_EPS = 1e-5
QSCALE = 64 ** -0.5
NEG = -30000.0


class Op:
    __slots__ = ("eng", "fn", "dma", "deps", "signal", "semi", "count")


class Prog:
    CE = ("pe", "act", "dve", "pool")
    DQ = ("sp", "pool", "act")

    def __init__(self, nc, es, ndma=6):
        self.nc = nc
        self.ops = []
        self.lw = {}
        self.rd = {}
        self.sems = []
        self.esem = {}
        for e in self.CE:
            self.esem[e] = len(self.sems)
            self.sems.append(es.enter_context(nc.semaphore("sem_" + e)))
        self.ndma = ndma
        self.dsem = {}
        self.dcnt = {}
        self.dlast = {}
        self.dnext = {}
        for q in self.DQ:
            self.dsem[q] = []
            for i in range(ndma):
                self.dsem[q].append(len(self.sems))
                self.sems.append(es.enter_context(nc.semaphore("dq_%s_%d" % (q, i))))
            self.dcnt[q] = [0] * ndma
            self.dlast[q] = [None] * ndma
            self.dnext[q] = 0
        self.last_of = {e: None for e in self.CE}
        self.stopped = False

    def add(self, eng, meth, args=(), kw=None, reads=(), writes=(), dma=False):
        op = Op()
        op.eng = eng
        op.fn = (meth, tuple(args), dict(kw or {}))
        op.dma = dma
        op.signal = False
        op.semi = None
        op.count = 0
        op.deps = []
        if self.stopped:
            return op
        def _expand(keys):
            out = []
            for k in keys:
                out.append(k)
                if isinstance(k, tuple) and len(k) > 2:
                    out.append(k[:2])
            return out
        reads = _expand(reads)
        writes = _expand(writes)
        banks = set()
        for a in list(args) + list((kw or {}).values()):
            t = getattr(a, "tensor", None)
            if t is not None and type(t).__name__ == "PSumTensorHandle":
                banks.add(("BANK", t.name))
        if banks:
            writes = list(writes) + sorted(banks)
        raw = []
        other = []
        for k in reads:
            w = self.lw.get(k)
            if w is not None:
                raw.append(w)
        for k in writes:
            w = self.lw.get(k)
            if w is not None:
                other.append(w)
            r = self.rd.get(k)
            if r:
                other.extend(r[0].values())
                other.extend(r[1])
        deps = []
        seen = set()
        for lst, is_raw in ((raw, True), (other, False)):
            for d in lst:
                if id(d) in seen:
                    continue
                if (not dma) and (not d.dma) and d.eng == eng:
                    if eng == "pe":
                        continue
                seen.add(id(d))
                deps.append(d)
        if dma:
            i = self.dnext[eng]
            self.dnext[eng] = (i + 1) % self.ndma
            prev = self.dlast[eng][i]
            if prev is not None and id(prev) not in seen:
                deps.append(prev)
            self.dcnt[eng][i] += 16
            op.semi = self.dsem[eng][i]
            op.count = self.dcnt[eng][i]
            self.dlast[eng][i] = op
        for d in deps:
            d.signal = True
        op.deps = deps
        for k in writes:
            self.lw[k] = op
            self.rd[k] = [{}, []]
        for k in reads:
            if k in writes:
                continue
            r = self.rd.setdefault(k, [{}, []])
            if dma:
                r[1].append(op)
            else:
                r[0][eng] = op
        if not dma:
            self.last_of[eng] = op
        self.ops.append(op)
        return op

    def barrier(self):
        if self.stopped:
            return
        deps = [o for o in self.last_of.values() if o is not None]
        for q in self.DQ:
            deps.extend(o for o in self.dlast[q] if o is not None)
        for d in deps:
            d.signal = True
        for e in ("pe", "act", "dve", "pool", "sp"):
            op = Op()
            op.eng = e
            op.fn = None
            op.dma = False
            op.signal = False
            op.semi = None
            op.count = 0
            op.deps = list(deps)
            self.ops.append(op)
        self.lw = {}
        self.rd = {}

    def emit(self):
        nc = self.nc
        cnt = {e: 0 for e in self.CE}
        for op in self.ops:
            if (not op.dma) and op.signal and op.fn is not None:
                cnt[op.eng] += 1
                op.semi = self.esem[op.eng]
                op.count = cnt[op.eng]
        byeng = {e: [] for e in ("pe", "act", "dve", "pool", "sp")}
        for op in self.ops:
            byeng[op.eng].append(op)
        sems = self.sems

        def run(name, h):
            waited = {}
            for op in byeng[name]:
                for d in op.deps:
                    if d.semi is None:
                        continue
                    if waited.get(d.semi, 0) >= d.count:
                        continue
                    h.wait_ge(sems[d.semi], d.count)
                    waited[d.semi] = d.count
                if op.fn is None:
                    continue
                meth, args, kw = op.fn
                ins = getattr(h, meth)(*args, **kw)
                if op.dma:
                    ins.then_inc(sems[op.semi], 16)
                elif op.signal:
                    ins.then_inc(sems[op.semi], 1)
            if name == "sp":
                for q in self.DQ:
                    for i in range(self.ndma):
                        c = self.dcnt[q][i]
                        if c > 0 and waited.get(self.dsem[q][i], 0) < c:
                            h.wait_ge(sems[self.dsem[q][i]], c)

        with nc.Block() as block:
            @block.tensor
            def _(h):
                run("pe", h)

            @block.scalar
            def _(h):
                run("act", h)

            @block.vector
            def _(h):
                run("dve", h)

            @block.gpsimd
            def _(h):
                run("pool", h)

            @block.sync
            def _(h):
                run("sp", h)


class _CompView:
    def __init__(self, t, c):
        self.t = t
        self.c = c

    def __getitem__(self, idx):
        rows, cols = idx
        return self.t[rows, self.c, cols]


class Rot:
    def __init__(self, name, tiles, keys=None):
        self.name = name
        self.tiles = tiles
        self.keys = keys if keys is not None else [(name, j) for j in range(len(tiles))]
        self.i = 0

    def next(self):
        j = self.i % len(self.tiles)
        self.i += 1
        return self.tiles[j], self.keys[j]


_PROG = [None]


class _Stop(Exception):
    pass


def build_nc(stop=None):
    nc = bass.Bass("TRN2", target_bir_lowering=False)

    def din(name, shape, dt=F32):
        return nc.dram_tensor(name, list(shape), dt, kind="ExternalInput").ap()

    def dout(name, shape, dt=F32):
        return nc.dram_tensor(name, list(shape), dt, kind="ExternalOutput").ap()

    xrot = din("xrot", [D, NSLOT])
    xown = din("xown", [D, T2])
    wq = din("wq", [D, 1024]); wqp = din("wqp", [D, 1024])
    wk = din("wk", [D, 1024]); wkp = din("wkp", [D, 1024])
    wv = din("wv", [D, 1024]); wxb = din("wxb", [D, 1024]); wg = din("wg", [D, 1024])
    w_out = din("w_out", [D, D]); w_up = din("w_up", [D, 4 * D]); w_down = din("w_down", [4 * D, D])
    cosd = din("cosd", [128, NSLOT + NSMP]); sind = din("sind", [128, NSLOT + NSMP])
    vald = din("vald", [128, NSLOT + NSMP])
    kbiasd = din("kbiasd", [128, 32])
    wabd = din("wabd", [128, 8, 128]); wibd = din("wibd", [128, 8, 128])
    pvd = din("pvd", [128, 160])
    lamd = din("lamd", [128, 4, 64])
    subgd = din("subgd", [128, 128])
    identd = din("identd", [128, 128])
    permd = din("permd", [128, 128])
    ckT = din("ckT", [2, 8, 128, 2048])
    cvd = din("cvd", [2, 8, 128, 16, 128])
    shd = din("shd", [128, 8, 2])
    scd = din("scd", [128, 8, 2, 3])

    yT = dout("yT", [D, T2])
    koutT = dout("koutT", [8, 128, T2])
    voutT = dout("voutT", [8, 128, T2])
    hout = dout("hout", [128, 8, 3])
    cout = dout("cout", [128, 8, 3, 3])

    scr_k = nc.dram_tensor("scr_k", [8, 128, NSLOT], BF16).ap()
    scr_v = nc.dram_tensor("scr_v", [8, 128, 32, 129], BF16).ap()

    PV_CW = 0; PV_CB = 32; PV_BA = 40; PV_BI = 48; PV_LL = 56
    PV_G1 = 64; PV_B1 = 80; PV_G2 = 96; PV_B2 = 112
    PV_HBA = 128; PV_HBI = 136; PV_HC = 144
    AX = mybir.AxisListType.X

    with ExitStack() as es:
        P = Prog(nc, es)
        _PROG[0] = P

        def CHK(name):
            if stop == name:
                P.barrier()
                P.stopped = True

        def sb(name, shape, dt=F32, scope=es):
            return scope.enter_context(nc.sbuf_tensor(name, list(shape), dt))

        def OP(eng, meth, *args, r=(), w=(), **kw):
            return P.add(eng, meth, args, kw, list(r), list(w), False)

        def DV(meth, *args, r=(), w=(), **kw):
            return P.add("dve", meth, args, kw, list(r), list(w), False)

        def AC(out, in_, func, r=(), w=(), **kw):
            return P.add("act", "activation", (), dict(out=out, in_=in_, func=func, **kw), list(r), list(w), False)

        def PL(meth, *args, r=(), w=(), **kw):
            return P.add("pool", meth, args, kw, list(r), list(w), False)

        def MM(out, lhsT, rhs, r=(), w=(), **kw):
            return P.add("pe", "matmul", (out, lhsT, rhs), kw, list(r), list(w), False)

        def TR(out, in_, idn, r=(), w=()):
            return P.add("pe", "transpose", (out, in_, idn), {}, list(r), list(w), False)

        def DMA(q, out, in_, r=(), w=()):
            return P.add(q, "dma_start", (), dict(out=out, in_=in_), list(r), list(w), True)

        try:
            ident32 = sb("ident32", [128, 128])
            permT = sb("permT", [128, 128])
            mixT = sb("mixT", [128, 16, T2], BF16)
            pv = sb("pv", [128, 160])
            ident = sb("ident", [128, 128], BF16)
            ones32 = sb("ones32", [128, 128])
            lamt = sb("lamt", [128, 4])
            gsub = sb("gsub", [128, 128])
            zero_t = sb("zero_t", [128, 64])

            DMA("sp", pv[:, :], pvd[:, :], w=["pv"])
            DMA("pool", ident[:], identd[:, :], w=["ident"])
            DMA("sp", ident32[:], identd[:, :], w=["ident32"])
            DMA("sp", permT[:], permd[:, :], w=["permT"])
            PL("memset", ones32[:], 1.0, w=["ones32"])
            PL("memset", zero_t[:], 0.0, w=["zero"])
            DMA("sp", gsub[:], subgd[:, :], w=["gsub"])
            with ExitStack() as es0:
                lamin = sb("lamin", [128, 4, 64], scope=es0)
                lamp = sb("lamp", [128, 2, 64], scope=es0)
                lams = sb("lams", [128, 4], scope=es0)
                sp_t = sb("sp_t", [128, 16], scope=es0)
                DMA("sp", lamin[:], lamd[:, :, :], w=["lamin"])
                DV("tensor_tensor", lamp[:, 0, :], lamin[:, 0, :], lamin[:, 1, :], ALU.mult, r=["lamin"], w=["lamp0"])
                DV("tensor_tensor", lamp[:, 1, :], lamin[:, 2, :], lamin[:, 3, :], ALU.mult, r=["lamin"], w=["lamp1"])
                DV("reduce_sum", lams[:, 0:1], lamp[:, 0, :], AX, r=["lamp0"], w=["lams0"])
                DV("reduce_sum", lams[:, 1:2], lamp[:, 1, :], AX, r=["lamp1"], w=["lams1"])
                AC(lams[:, 2:4], lams[:, 0:2], AF.Exp, r=["lams0", "lams1"], w=["lams23"])
                DV("scalar_tensor_tensor", lamt[:, 0:1], lams[:, 2:3], LAM_INIT, lams[:, 3:4], ALU.add, ALU.subtract,
                   r=["lams23"], w=["lam0"])
                DV("tensor_scalar", lamt[:, 1:2], lamt[:, 0:1], -1.0, None, ALU.mult, r=["lam0"], w=["lam"])
                DV("tensor_scalar", gsub[:], gsub[:], 1.0 - LAM_INIT, None, ALU.mult, r=["gsub"], w=["gsub"])
                DV("tensor_scalar", pv[:, 152:153], pv[:, 152:153], 1.0 - LAM_INIT, None, ALU.mult, r=["pv"], w=["pvg"])
                DV("tensor_scalar", pv[:, PV_HBA:PV_HBA + 16], pv[:, PV_BA:PV_BA + 16], -1.0, None, ALU.mult,
                   r=["pv"], w=["pvh"])
                AC(sp_t[:, 0:8], pv[:, PV_LL:PV_LL + 8], AF.Exp, r=["pv"], w=["sp0"], scale=-1.0)
                DV("tensor_scalar", sp_t[:, 0:8], sp_t[:, 0:8], 1.0, None, ALU.add, r=["sp0"], w=["sp1"])
                AC(sp_t[:, 8:16], sp_t[:, 0:8], AF.Ln, r=["sp1"], w=["sp2"])
                DV("tensor_scalar", pv[:, PV_HC:PV_HC + 8], sp_t[:, 8:16], -8.0, None, ALU.mult, r=["sp2"], w=["pvc"])
                P.barrier()

            with ExitStack() as es1:
                ksmp = sb("ksmp", [128, 8, NSMP], BF16, scope=es1)
                vsmp = sb("vsmp", [16, 2, 8, 129], BF16, scope=es1)

                def run_passes(stage):
                    with ExitStack() as esp:
                        psf = [esp.enter_context(nc.psum_tensor("ps%d%s" % (i, stage), [128, 512], F32)) for i in range(7)]
                        pst = esp.enter_context(nc.psum_tensor("pst" + stage, [128, 4, 128], BF16))
                        wbuf = sb("wbuf" + stage, [128, NKC, 2048], BF16, scope=esp)
                        xrotr = Rot("xblk", [sb("xblk%d%s" % (i, stage), [128, NKC, BLK], BF16, scope=esp) for i in range(2)])
                        tabr = Rot("tab", [sb("tab%d%s" % (i, stage), [128, 3, BLK], scope=esp) for i in range(1)])
                        wab = sb("wab" + stage, [128, 8, 128], BF16, scope=esp)
                        wib = sb("wib" + stage, [128, 8, 128], BF16, scope=esp)
                        hcar = sb("hcar" + stage, [128, 8, 3], scope=esp)
                        xcar = sb("xcar" + stage, [128, 8, 3, 3], scope=esp)
                        _ft = [sb("f32t%d%s" % (i, stage), [128, BLK], scope=esp) for i in range(18 if stage == "A" else 6)]
                        _fk = [("f32t", i) for i in range(18 if stage == "A" else 6)]
                        f32r = Rot("f32t", _ft, _fk)
                        rXC = Rot("rXC", _ft[0:3], _fk[0:3])
                        rTR = Rot("rTR", _ft[3:6], _fk[3:6])
                        rTI = Rot("rTI", _ft[6:9], _fk[6:9])
                        rB2 = Rot("rB2", _ft[9:12], _fk[9:12])
                        rHB = Rot("rHB", _ft[12:15], _fk[12:15])
                        rG2 = Rot("rG2", _ft[15:18], _fk[15:18])
                        bfr = Rot("bft", [sb("bft%d%s" % (i, stage), [128, BLK], BF16, scope=esp) for i in range(4)])
                        xbtr = Rot("xbt", [sb("xbt%d%s" % (i, stage), [128, BLK + 8], scope=esp) for i in range(3)])
                        vaugr = Rot("vaug", [sb("vaug%d%s" % (i, stage), [128, 4, 129], BF16, scope=esp) for i in range(2)])
                        psA = Rot("psA", [psf[0], psf[1]])
                        psB = Rot("psB", [psf[2], psf[3]])
                        psG = Rot("psG", [psf[4], psf[5], psf[6]])
                        stgr = Rot("stg", [sb("stg%d%s" % (i, stage), [128, 2, BLK], scope=esp) for i in range(3)])

                        if stage == "A":
                            DMA("pool", wab[:], wabd[:, :, :], w=["wab"])
                            DMA("pool", wib[:], wibd[:, :, :], w=["wib"])
                            PL("memset", hcar[:], 0.0, w=["hcar"])
                            PL("memset", xcar[:], 0.0, w=["xcar"])
                            DMA("sp", hcar[:, :, 1:3], shd[:, :, :], r=["hcar"], w=["hcar"])
                            DMA("sp", xcar[:, :, 1:3, :], scd[:, :, :, :], r=["xcar"], w=["xcar"])
                            for vi, vt_ in enumerate(vaugr.tiles):
                                PL("memset", vt_[:, :, 128:129], 1.0, w=[("vaug", vi)])
                            PL("memset", vsmp[:, :, :, 128:129], 1.0, w=["vsmp"])

                        def load_w(half, src):
                            DMA("pool", wbuf[:, :, half * 1024:(half + 1) * 1024],
                                src.rearrange("(c p) n -> p c n", p=128), w=[("W", half)])

                        def blocks(own_only):
                            lst = []
                            for b in range(NBLK):
                                if own_only and b < 6:
                                    continue
                                lst.append((b, b * BLK, BLK))
                            lst.append((8, NSLOT, NSMP))
                            return lst

                        def load_x(b, c0, n):
                            xt, xk = xrotr.next()
                            tt, tk = tabr.next()
                            if b < 8:
                                for j in range(8):
                                    stg, sk = stgr.next()
                                    DMA("sp", stg[:, :, :],
                                        xrot[256 * j:256 * (j + 1), c0:c0 + n].rearrange("(c p) s -> p c s", p=128), w=[sk])
                                    PL("tensor_copy", xt[:, 2 * j:2 * j + 2, :], stg[:, :, :], r=[sk], w=[xk + (j,)])
                            else:
                                DMA("pool", xt[:, :, 0:n], xown[:, TOWN:T2].rearrange("(c p) s -> p c s", p=128), w=[xk])
                            DMA("sp", tt[:, 0, 0:n], cosd[:, c0:c0 + n], w=[tk + (0,)])
                            DMA("sp", tt[:, 1, 0:n], sind[:, c0:c0 + n], w=[tk + (1,)])
                            DMA("sp", tt[:, 2, 0:n], vald[:, c0:c0 + n], w=[tk + (2,)])
                            return xt, xk, tt, tk

                        def proj(ps, psk, half, chunk, xt, xk, n):
                            for kc in range(NKC):
                                MM(ps[:, 0:n], wbuf[:, kc, half * 1024 + chunk * 128: half * 1024 + (chunk + 1) * 128],
                                   xt[:, kc, 0:n], r=[("W", half), xk], w=[psk], start=(kc == 0), stop=(kc == NKC - 1))

                        def own_col(b):
                            return {6: 0, 7: 512, 8: 1024}.get(b)

                        def rope_pass(w_a, w_b, is_q):
                            load_w(0, w_a)
                            for (b, c0, n) in blocks(own_only=is_q):
                                xt, xk, tt, tk = load_x(b, c0, n)
                                oc = own_col(b)
                                for h in range(8):
                                    pa, pak = psA.next()
                                    pb, pbk = psB.next()
                                    proj(pa, pak, 0, h, xt, xk, n)
                                    ks, ksk = f32r.next()
                                    AC(ks[:, 0:n], pa[:, 0:n], AF.Copy, r=[pak], w=[ksk])
                                    MM(pb[:, 0:n], permT[:], ks[:, 0:n], r=["permT", ksk], w=[pbk], start=True, stop=True)
                                    t1, t1k = f32r.next()
                                    t2, t2k = f32r.next()
                                    DV("tensor_tensor", t1[:, 0:n], ks[:, 0:n], tt[:, 0, 0:n], ALU.mult, r=[ksk, tk + (0,)], w=[t1k])
                                    DV("tensor_tensor", t2[:, 0:n], pb[:, 0:n], tt[:, 1, 0:n], ALU.mult, r=[pbk, tk + (1,)], w=[t2k])
                                    if is_q:
                                        DV("tensor_tensor", qT[:, h, oc:oc + n], t1[:, 0:n], t2[:, 0:n], ALU.add,
                                           r=[t1k, t2k], w=[("qT", h, b)])
                                    else:
                                        kf, kfk = f32r.next()
                                        DV("tensor_tensor", kf[:, 0:n], t1[:, 0:n], t2[:, 0:n], ALU.add, r=[t1k, t2k], w=[kfk])
                                        if b < 8:
                                            kb_, kbk = bfr.next()
                                            AC(kb_[:, 0:n], kf[:, 0:n], AF.Copy, r=[kfk], w=[kbk])
                                            DMA("sp", scr_k[h, :, c0:c0 + n], kb_[:, 0:n], r=[kbk], w=[("scrk", h)])
                                        else:
                                            AC(ksmp[:, h, :], kf[:, 0:n], AF.Copy, r=[kfk], w=[("ksmp", h)])
                                        if oc is not None:
                                            DMA("sp", koutT[h, :, oc:oc + n], kf[:, 0:n], r=[kfk])

                        if stage == "B":
                            rope_pass(wq, wqp, True)
                            CHK("p3")
                            P.barrier()
                            return
                        rope_pass(wk, wkp, False)
                        CHK("p1")

                        load_w(0, wv)
                        for (b, c0, n) in blocks(own_only=False):
                            xt, xk, tt, tk = load_x(b, c0, n)
                            oc = own_col(b)
                            for h in range(8):
                                pa, pak = psA.next()
                                proj(pa, pak, 0, h, xt, xk, n)
                                vb, vbk = bfr.next()
                                AC(vb[:, 0:n], pa[:, 0:n], AF.Copy, r=[pak], w=[vbk])
                                if oc is not None:
                                    vf, vfk = f32r.next()
                                    DV("tensor_copy", vf[:, 0:n], pa[:, 0:n], r=[pak], w=[vfk])
                                    DMA("sp", voutT[h, :, oc:oc + n], vf[:, 0:n], r=[vfk])
                                if _DBG == "notr":
                                    continue
                                if b < 8:
                                    va, vak = vaugr.next()
                                    for s4 in range(4):
                                        TR(pst[:, s4, :], vb[:, s4 * 128:(s4 + 1) * 128], ident[:], r=[vbk, "ident"], w=["pst"])
                                    DV("tensor_copy", va[:, :, 0:128], pst[:, :, :], r=["pst"], w=[vak])
                                    DMA("sp", scr_v[h, :, b * 4:(b + 1) * 4, :], va[:, :, :], r=[vak], w=[("scrv", h)])
                                else:
                                    for bi in range(2):
                                        TR(pst[0:16, bi, :], vb[:, bi * 16:(bi + 1) * 16], ident[:], r=[vbk, "ident"], w=["pst"])
                                    DV("tensor_copy", vsmp[:, :, h, 0:128], pst[0:16, 0:2, :], r=["pst"], w=["vsmp"])
                            CHK("p2a_b%d" % b)
                        CHK("p2a")

                        load_w(0, wxb)
                        load_w(1, wg)
                        def lru_chain(b, n, ch, xt, xk, tt, tk, oc):
                            pa, pak = psA.next()
                            proj(pa, pak, 0, ch, xt, xk, n)
                            segs = [(0, 0, n)] if b < 8 else [(1, 0, 16), (2, 16, 16)]
                            xbt, xbk = xbtr.next()
                            xc, xck = rXC.next()
                            cwc = PV_CW + ch * 4
                            for (ci, s0, sn) in segs:
                                off = s0 + (3 if ci == 2 else 0)
                                kh = xbk + ("h", ci)
                                kd = xbk + ("d", ci)
                                DV("tensor_copy", xbt[:, off:off + 3], xcar[:, ch, ci, :], r=["xcar"], w=[kh])
                                AC(xbt[:, off + 3:off + 3 + sn], pa[:, s0:s0 + sn], AF.Copy, r=[pak], w=[kd])
                            yield
                            for (ci, s0, sn) in segs:
                                off = s0 + (3 if ci == 2 else 0)
                                kh = xbk + ("h", ci)
                                kd = xbk + ("d", ci)
                                DV("tensor_copy", xcar[:, ch, ci, :], xbt[:, off + sn:off + sn + 3], r=[kd, kh], w=["xcar"])
                                AC(xc[:, s0:s0 + sn], xbt[:, off + 3:off + 3 + sn], AF.Identity,
                                   r=[kd, kh, "pv"], w=[xck + (ci,)],
                                   scale=pv[:, cwc + 3:cwc + 4], bias=pv[:, PV_CB + ch:PV_CB + ch + 1])
                                yield
                                for j in range(3):
                                    DV("scalar_tensor_tensor", xc[:, s0:s0 + sn], xbt[:, off + j:off + j + sn],
                                       pv[:, cwc + j:cwc + j + 1], xc[:, s0:s0 + sn], ALU.mult, ALU.add,
                                       r=[kd, kh, xck + (ci,)], w=[xck + (ci,)])
                                    yield
                            xck_all = [xck + (ci,) for (ci, _, _) in segs]
                            xcb, xcbk = bfr.next()
                            AC(xcb[:, 0:n], xc[:, 0:n], AF.Copy, r=xck_all, w=[xcbk])
                            yield
                            pr, prk = psG.next()
                            pi, pik = psG.next()
                            MM(pr[:, 0:n], wab[:, ch, :], xcb[:, 0:n], r=["wab", xcbk], w=[prk], start=True, stop=True)
                            MM(pi[:, 0:n], wib[:, ch, :], xcb[:, 0:n], r=["wib", xcbk], w=[pik], start=True, stop=True)
                            tr, trk = rTR.next()
                            ti, tik = rTI.next()
                            AC(tr[:, 0:n], pr[:, 0:n], AF.Exp, r=[prk, "pvh"], w=[trk],
                               bias=pv[:, PV_HBA + ch:PV_HBA + ch + 1], scale=-1.0)
                            AC(ti[:, 0:n], pi[:, 0:n], AF.Exp, r=[pik, "pvh"], w=[tik],
                               bias=pv[:, PV_HBI + ch:PV_HBI + ch + 1], scale=-1.0)
                            yield
                            AC(tr[:, 0:n], tr[:, 0:n], AF.Ln, r=[trk], w=[trk], bias=1.0)
                            AC(ti[:, 0:n], ti[:, 0:n], AF.Ln, r=[tik], w=[tik], bias=1.0)
                            yield
                            AC(tr[:, 0:n], tr[:, 0:n], AF.Exp, r=[trk], w=[trk], scale=-1.0)
                            AC(ti[:, 0:n], ti[:, 0:n], AF.Exp, r=[tik], w=[tik], scale=-1.0)
                            yield
                            AC(tr[:, 0:n], tr[:, 0:n], AF.Exp, r=[trk, "pvc"], w=[trk],
                               scale=pv[:, PV_HC + ch:PV_HC + ch + 1])
                            DV("tensor_tensor", ti[:, 0:n], ti[:, 0:n], xc[:, 0:n], ALU.mult, r=[tik] + xck_all, w=[tik])
                            yield
                            b2, b2k = rB2.next()
                            AC(b2[:, 0:n], tr[:, 0:n], AF.Square, r=[trk], w=[b2k])
                            if b < 6:
                                DV("tensor_tensor", ti[:, 0:n], ti[:, 0:n], tt[:, 2, 0:n], ALU.mult, r=[tik, tk + (2,)], w=[tik])
                            yield
                            AC(b2[:, 0:n], b2[:, 0:n], AF.Ln, r=[b2k], w=[b2k], scale=-1.0, bias=1.0)
                            yield
                            AC(b2[:, 0:n], b2[:, 0:n], AF.Exp, r=[b2k], w=[b2k], scale=0.5)
                            yield
                            DV("tensor_tensor", b2[:, 0:n], b2[:, 0:n], ti[:, 0:n], ALU.mult, r=[b2k, tik], w=[b2k])
                            yield
                            hb, hbk = rHB.next()
                            for (ci, s0, sn) in segs:
                                DV("tensor_tensor_scan", hb[:, s0:s0 + sn], tr[:, s0:s0 + sn], b2[:, s0:s0 + sn],
                                   hcar[:, ch, ci:ci + 1], ALU.mult, ALU.add, r=[trk, b2k, "hcar"], w=[hbk + (ci,)])
                                yield
                                DV("tensor_copy", hcar[:, ch, ci:ci + 1], hb[:, s0 + sn - 1:s0 + sn], r=[hbk + (ci,)], w=["hcar"])
                            if oc is not None:
                                pg, pgk = psB.next()
                                proj(pg, pgk, 1, ch, xt, xk, n)
                                gs = xc
                                AC(gs[:, 0:n], pg[:, 0:n], AF.Copy, r=[pgk], w=[xck])
                                yield
                                g2, g2k = rG2.next()
                                AC(g2[:, 0:n], gs[:, 0:n], AF.Square, r=[xck], w=[g2k])
                                yield
                                DV("tensor_scalar", g2[:, 0:n], g2[:, 0:n], 0.044715, 1.0, ALU.mult, ALU.add, r=[g2k], w=[g2k])
                                yield
                                DV("tensor_tensor", g2[:, 0:n], g2[:, 0:n], gs[:, 0:n], ALU.mult, r=[g2k, xck], w=[g2k])
                                yield
                                AC(g2[:, 0:n], g2[:, 0:n], AF.Exp, r=[g2k], w=[g2k], scale=-1.5957691216057308)
                                yield
                                DV("tensor_scalar", g2[:, 0:n], g2[:, 0:n], 1.0, None, ALU.add, r=[g2k], w=[g2k])
                                yield
                                DV("reciprocal", g2[:, 0:n], g2[:, 0:n], r=[g2k], w=[g2k])
                                yield
                                DV("tensor_tensor", g2[:, 0:n], g2[:, 0:n], gs[:, 0:n], ALU.mult, r=[g2k, xck], w=[g2k])
                                yield
                                DV("tensor_tensor", mixT[:, 8 + ch, oc:oc + n], g2[:, 0:n], hb[:, 0:n], ALU.mult,
                                   r=[g2k] + [hbk + (ci,) for (ci, _, _) in segs], w=[("mix", 8 + ch, b)])

                        IL = 3
                        for (b, c0, n) in blocks(own_only=False):
                            xt, xk, tt, tk = load_x(b, c0, n)
                            oc = own_col(b)
                            for g0 in range(0, 8, IL):
                                gens = [lru_chain(b, n, ch, xt, xk, tt, tk, oc) for ch in range(g0, min(8, g0 + IL))]
                                while gens:
                                    for g in list(gens):
                                        try:
                                            next(g)
                                        except StopIteration:
                                            gens.remove(g)
                        DMA("sp", hout[:, :, :], hcar[:], r=["hcar"])
                        DMA("sp", cout[:, :, :, :], xcar[:], r=["xcar"])
                        CHK("p2b")

                        P.barrier()

                qT = None
                run_passes("A")
                qT = sb("qT", [128, 8, T2], BF16, scope=es1)
                run_passes("B")

                with ExitStack() as esa:
                    psf = [esa.enter_context(nc.psum_tensor("pa%d" % i, [128, 512], F32)) for i in range(8)]
                    kTh = [sb("kTh%d" % i, [128, NSLOT], BF16, scope=esa) for i in range(2)]
                    Vh = [sb("Vh%d" % i, [128, 32, 129], BF16, scope=esa) for i in range(2)]
                    ptr = Rot("PT", [sb("PT%d" % i, [128, 2, BLK], BF16, scope=esa) for i in range(3)])
                    kbias = sb("kbias", [128, 32], scope=esa)
                    ckt = [sb("ckt%d" % i, [128, 2048], BF16, scope=esa) for i in range(2)]
                    cvt = [sb("cvt%d" % i, [128, 16, 129], BF16, scope=esa) for i in range(2)]
                    ep = Rot("ep", [sb("ep%d" % i, [128, 128], scope=esa) for i in range(4)])
                    epb = Rot("epb", [sb("epb%d" % i, [128, 128], scope=esa) for i in range(2)])
                    sm = Rot("sm", [sb("sm%d" % i, [128, 8], scope=esa) for i in range(4)])
                    junk = sb("junk", [128, 128], scope=esa)
                    onesb = sb("onesb", [128, 128], BF16, scope=esa)
                    epf = Rot("epf", [sb("epf%d" % i, [128, BLK], scope=esa) for i in range(7)])
                    PL("memset", onesb[:], 1.0, w=["onesb"])
                    DMA("sp", kbias[:], kbiasd[:, :], w=["kbias"])
                    for i in range(2):
                        PL("memset", cvt[i][:, :, 128:129], 1.0, w=[("cvt", i)])
                    psS = [[psf[0], psf[1]], [psf[2], psf[3]]]
                    accs = {}
                    order = [(s, c) for s in range(4) for c in range(2)]
                    for idx, sc in enumerate(order):
                        accs[sc] = (psf[4 + sc[0]], sc[1] * 129, 4 + sc[0])

                    def epilogue(npart, bank1, o1, bank2, o2, okeys, dst, dkey):
                        ps_ = slice(0, npart)
                        s_, sk = sm.next()
                        t_, tk_ = ep.next()
                        o_, ok_ = ep.next()
                        ob, obk = epb.next()
                        DV("reciprocal", s_[ps_, 0:1], bank1[ps_, o1 + 128:o1 + 129], r=okeys, w=[sk + (0,)])
                        DV("reciprocal", s_[ps_, 1:2], bank2[ps_, o2 + 128:o2 + 129], r=okeys, w=[sk + (1,)])
                        DV("tensor_tensor", s_[ps_, 2:3], s_[ps_, 1:2], lamt[ps_, 1:2], ALU.mult, r=[sk + (1,), "lam"], w=[sk + (2,)])
                        DV("tensor_scalar", t_[ps_, :], bank1[ps_, o1:o1 + 128], s_[ps_, 0:1], None, ALU.mult,
                           r=list(okeys) + [sk + (0,)], w=[tk_])
                        DV("scalar_tensor_tensor", o_[ps_, :], bank2[ps_, o2:o2 + 128], s_[ps_, 2:3], t_[ps_, :],
                           ALU.mult, ALU.add, r=list(okeys) + [sk + (2,), tk_], w=[ok_])
                        AC(junk[ps_, :], o_[ps_, :], AF.Square, r=[ok_], w=["junk", sk + (3,)], accum_out=s_[ps_, 3:4])
                        DV("tensor_scalar", s_[ps_, 4:5], s_[ps_, 3:4], 1.0 / 128.0, RMS_EPS, ALU.mult, ALU.add,
                           r=[sk + (3,)], w=[sk + (4,)])
                        AC(s_[ps_, 6:7], s_[ps_, 4:5], AF.Ln, r=[sk + (4,)], w=[sk + (6,)])
                        AC(s_[ps_, 5:6], s_[ps_, 6:7], AF.Exp, r=[sk + (6,)], w=[sk + (5,)], scale=-0.5)
                        DV("scalar_tensor_tensor", ob[ps_, :], o_[ps_, :], s_[ps_, 5:6], gsub[ps_, :], ALU.mult, ALU.mult,
                           r=[ok_, sk + (5,), "gsub"], w=[obk])
                        TR(bank1[:, 0:npart], ob[ps_, :], ident32[ps_, 0:npart], r=[obk, "ident32"] + list(okeys), w=list(okeys))
                        AC(dst, bank1[:, 0:npart], AF.Copy, r=list(okeys), w=[dkey])

                    def load_head(h):
                        DMA("sp", kTh[h % 2][:], scr_k[h, :, :], w=[("kTh", h % 2)])
                        DMA("sp", Vh[h % 2][:], scr_v[h, :, :, :], w=[("Vh", h % 2)])

                    steps = []
                    for h in range(8):
                        for sbk in range(2):
                            first_kb = 24 + 4 * sbk
                            for kb in range(first_kb + 4):
                                steps.append((h, sbk, kb, first_kb))
                    st_state = {}

                    def emit_qk(si):
                        h, sbk, kb, first_kb = steps[si]
                        kt = kTh[h % 2]
                        i = kb - first_kb
                        c0 = max(0, i) * 128
                        pss = psS[si % 2]
                        psk = [("psS", si % 2, 0), ("psS", si % 2, 1)]
                        for c in range(2):
                            MM(pss[c][:, c0:BLK], kt[64 * c:64 * c + 64, kb * 128:(kb + 1) * 128],
                               qT[64 * c:64 * c + 64, h, sbk * BLK + c0:(sbk + 1) * BLK],
                               r=[("kTh", h % 2)], w=[psk[c]], start=True, stop=True)

                    def emit_exp(si):
                        h, sbk, kb, first_kb = steps[si]
                        i = kb - first_kb
                        c0 = max(0, i) * 128
                        pss = psS[si % 2]
                        psk = [("psS", si % 2, 0), ("psS", si % 2, 1)]
                        pt, ptk = ptr.next()
                        st_state[si] = (pt, ptk)
                        if i < 0:
                            for c in range(2):
                                AC(pt[:, c, :], pss[c][:, :], AF.Exp, r=[psk[c], "kbias"], w=[ptk + (c,)],
                                   bias=kbias[:, kb:kb + 1], scale=QSCALE)
                        else:
                            for c in range(2):
                                AC(pt[0:64, c, c0:BLK], pss[c][0:64, c0:BLK], AF.Exp, r=[psk[c]], w=[ptk + (c, "a")], scale=QSCALE)
                                AC(pt[64:128, c, c0 + 64:BLK], pss[c][64:128, c0 + 64:BLK], AF.Exp, r=[psk[c]],
                                   w=[ptk + (c, "b")], scale=QSCALE)
                                AC(pt[64:128, c, c0:c0 + 64], zero_t[64:128, 0:64], AF.Copy, r=["zero"], w=[ptk + (c, "z")])

                    OT = [psf[4], psf[5]]
                    SM = [psf[6], psf[7]]

                    def emit_pv(si):
                        h, sbk, kb, first_kb = steps[si]
                        vt = Vh[h % 2]
                        vk = ("Vh", h % 2)
                        i = kb - first_kb
                        c0 = max(0, i) * 128
                        last = first_kb + 3
                        pt, ptk = st_state.pop(si)
                        for c in range(2):
                            rkeys = [ptk + (c,)] if i < 0 else [ptk + (c, "a"), ptk + (c, "b"), ptk + (c, "z")]
                            MM(OT[c][:, c0:BLK], vt[:, kb, 0:128], pt[:, c, c0:BLK], r=[vk] + rkeys, w=[("OT", c)],
                               start=(kb == 0), stop=(kb == last))
                            MM(SM[c][:, c0:BLK], onesb[:], pt[:, c, c0:BLK], r=["onesb"] + rkeys, w=[("SM", c)],
                               start=(kb == 0), stop=(kb == last))
                        if kb == last:
                            rs1, rs1k = epf.next()
                            rs2, rs2k = epf.next()
                            t_, tk_ = epf.next()
                            u_, uk_ = epf.next()
                            DV("reciprocal", rs1[:], SM[0][:, :], r=[("SM", 0)], w=[rs1k])
                            DV("reciprocal", rs2[:], SM[1][:, :], r=[("SM", 1)], w=[rs2k])
                            DV("tensor_tensor", t_[:], OT[0][:, :], rs1[:], ALU.mult, r=[("OT", 0), rs1k], w=[tk_])
                            DV("tensor_tensor", u_[:], OT[1][:, :], rs2[:], ALU.mult, r=[("OT", 1), rs2k], w=[uk_])
                            DV("scalar_tensor_tensor", t_[:], u_[:], lamt[:, 1:2], t_[:], ALU.mult, ALU.add,
                               r=[uk_, tk_, "lam"], w=[tk_])
                            AC(u_[:], t_[:], AF.Square, r=[tk_], w=[uk_])
                            MM(SM[0][:, :], ones32[:], u_[:], r=["ones32", uk_], w=[("SM", 0)], start=True, stop=True)
                            AC(rs1[:], SM[0][:, :], AF.Ln, r=[("SM", 0)], w=[rs1k], scale=1.0 / 128.0, bias=RMS_EPS)
                            AC(rs1[:], rs1[:], AF.Exp, r=[rs1k], w=[rs1k], scale=-0.5)
                            DV("scalar_tensor_tensor", mixT[:, h, sbk * BLK:(sbk + 1) * BLK], t_[:], pv[:, 152:153], rs1[:],
                               ALU.mult, ALU.mult, r=[tk_, rs1k, "pvg"], w=[("mix", h, 6 + sbk)])

                    load_head(0)
                    emit_qk(0)
                    for si in range(len(steps)):
                        h, sbk, kb, first_kb = steps[si]
                        if sbk == 0 and kb == 0 and h + 1 < 8:
                            load_head(h + 1)
                        if si + 1 < len(steps):
                            emit_qk(si + 1)
                        emit_exp(si)
                        emit_pv(si)
                    CHK("attn")

                    def load_cache(j):
                        bi, h = j // 8, j % 8
                        DMA("pool", ckt[j % 2][:], ckT[bi, h, :, :], w=[("ckt", j % 2)])
                        DMA("pool", cvt[j % 2][:, :, 0:128], cvd[bi, h, :, :, :], r=[("cvt", j % 2)], w=[("cvt", j % 2)])

                    ssteps = [(j, kb) for j in range(16) for kb in range(17)]
                    bankO = psf[4]
                    sst = {}

                    def s_qk(si):
                        j, kb = ssteps[si]
                        bi, h = j // 8, j % 8
                        q0 = TOWN + bi * 16
                        ck = ckt[j % 2]
                        pss = psS[si % 2]
                        np_ = 128 if kb < 16 else 16
                        for c in range(2):
                            if kb < 16:
                                lhs = ck[64 * c:64 * c + 64, kb * 128:(kb + 1) * 128]
                                rk_ = [("ckt", j % 2)]
                            else:
                                lhs = ksmp[64 * c:64 * c + 64, h, bi * 16:(bi + 1) * 16]
                                rk_ = []
                            MM(pss[c][0:np_, 0:16], lhs, qT[64 * c:64 * c + 64, h, q0:q0 + 16],
                               r=rk_, w=[("psS", si % 2, c)], start=True, stop=True)

                    def s_exp(si):
                        j, kb = ssteps[si]
                        pss = psS[si % 2]
                        np_ = 128 if kb < 16 else 16
                        pt, ptk = ptr.next()
                        sst[si] = (pt, ptk)
                        for c in range(2):
                            AC(pt[0:np_, c, 0:16], pss[c][0:np_, 0:16], AF.Exp, r=[("psS", si % 2, c)], w=[ptk + (c,)], scale=QSCALE)

                    def s_pv(si):
                        j, kb = ssteps[si]
                        bi, h = j // 8, j % 8
                        q0 = TOWN + bi * 16
                        cv = cvt[j % 2]
                        np_ = 128 if kb < 16 else 16
                        pt, ptk = sst.pop(si)
                        for c in range(2):
                            if kb < 16:
                                rhs = cv[:, kb, :]
                                rk_ = [("cvt", j % 2)]
                            else:
                                rhs = vsmp[0:16, bi, h, :]
                                rk_ = []
                            MM(bankO[0:16, c * 129:(c + 1) * 129], pt[0:np_, c, 0:16], rhs,
                               r=rk_ + [ptk + (c,)], w=[("psO", 0, c)],
                               start=(kb == 0 and c == 0), stop=(kb == 16), skip_group_check=True)
                        if kb == 16:
                            epilogue(16, bankO, 0, bankO, 129, [("psO", 0, 0), ("psO", 0, 1)],
                                     mixT[:, h, q0:q0 + 16], ("mix", h, 8, bi))

                    load_cache(0)
                    s_qk(0)
                    for si in range(len(ssteps)):
                        j, kb = ssteps[si]
                        if kb == 0 and j + 1 < 16:
                            load_cache(j + 1)
                        if si + 1 < len(ssteps):
                            s_qk(si + 1)
                        s_exp(si)
                        s_pv(si)
                    CHK("sattn")
                    P.barrier()

            with ExitStack() as es2:
                psf = [es2.enter_context(nc.psum_tensor("pm%d" % i, [128, 512], F32)) for i in range(6)]
                R = sb("R", [128, NKC, T2], scope=es2)
                XB = sb("XB", [128, NKC, T2], BF16, scope=es2)
                wtr = Rot("WT", [sb("WT%d" % i, [128, NKC, 512], BF16, scope=es2) for i in range(2)])
                sqr = Rot("sq", [sb("sq%d" % i, [128, TB], scope=es2) for i in range(4)])
                relr = Rot("rel", [sb("rel%d" % i, [128, TB], scope=es2) for i in range(4)])
                mean_t = sb("mean_t", [128, TB], scope=es2)
                rstd_t = sb("rstd_t", [128, TB], scope=es2)
                msq_t = sb("msq_t", [128, TB], scope=es2)
                lnr = Rot("lnt", [sb("lnt%d" % i, [128, TB], scope=es2) for i in range(4)])
                psR = Rot("psR", [psf[0], psf[1], psf[2], psf[3]])
                Hq = mixT

                for j in range(4):
                    DMA("sp", R[:, 4 * j:4 * j + 4, :], xown[512 * j:512 * (j + 1), :].rearrange("(c p) t -> p c t", p=128),
                        w=[("R", m) for m in range(4 * j, 4 * j + 4)])

                def load_wt(src, r0, c0):
                    wt, wtk = wtr.next()
                    DMA("pool", wt[:], src[r0:r0 + 2048, c0:c0 + 512].rearrange("(c p) n -> p c n", p=128), w=[wtk])
                    return wt, wtk

                def mm_group(wt, wtk, mp, act, akey, tb):
                    ps, psk = psR.next()
                    for kc in range(NKC):
                        MM(ps[:, 0:TB], wt[:, kc, mp * 128:(mp + 1) * 128], act[:, kc, tb * TB:(tb + 1) * TB],
                           r=[wtk, (akey, kc)], w=[psk], start=(kc == 0), stop=(kc == NKC - 1))
                    return ps, psk

                def layer_norm(gcol, bcol, write_bf):
                    for tb in range(NTB):
                        cs = slice(tb * TB, (tb + 1) * TB)
                        p1, p1k = psf[4], ("psLN", 0)
                        p2, p2k = psf[5], ("psLN", 1)
                        for m in range(NKC):
                            sq, sqk = sqr.next()
                            AC(sq[:], R[:, m, cs], AF.Square, r=[("R", m)], w=[sqk])
                            MM(p1[:, 0:TB], ones32[:], R[:, m, cs], r=["ones32", ("R", m)], w=[p1k],
                               start=(m == 0), stop=(m == NKC - 1))
                            MM(p2[:, 0:TB], ones32[:], sq[:], r=["ones32", sqk], w=[p2k],
                               start=(m == 0), stop=(m == NKC - 1))
                        DV("tensor_scalar", mean_t[:], p1[:, 0:TB], 1.0 / D, None, ALU.mult, r=[p1k], w=["mean"])
                        DV("tensor_tensor", msq_t[:], mean_t[:], mean_t[:], ALU.mult, r=["mean"], w=["msq"])
                        DV("scalar_tensor_tensor", rstd_t[:], p2[:, 0:TB], 1.0 / D, msq_t[:], ALU.mult, ALU.subtract,
                           r=[p2k, "msq"], w=["rstd"])
                        DV("tensor_scalar", rstd_t[:], rstd_t[:], LN_EPS, None, ALU.add, r=["rstd"], w=["rstd"])
                        AC(rstd_t[:], rstd_t[:], AF.Ln, r=["rstd"], w=["rstd"])
                        AC(rstd_t[:], rstd_t[:], AF.Exp, r=["rstd"], w=["rstd"], scale=-0.5)
                        for m in range(NKC):
                            t_, tk_ = lnr.next()
                            DV("tensor_tensor", t_[:], R[:, m, cs], mean_t[:], ALU.subtract, r=[("R", m), "mean"], w=[tk_])
                            DV("tensor_tensor", t_[:], t_[:], rstd_t[:], ALU.mult, r=[tk_, "rstd"], w=[tk_])
                            DV("tensor_scalar", R[:, m, cs], t_[:], pv[:, gcol + m:gcol + m + 1], pv[:, bcol + m:bcol + m + 1],
                               ALU.mult, ALU.add, r=[tk_, "pv"], w=[("R", m)])
                            if write_bf:
                                AC(XB[:, m, cs], R[:, m, cs], AF.Copy, r=[("R", m)], w=[("XB", m)])

                for j in range(4):
                    wt, wtk = load_wt(w_out, 0, 512 * j)
                    for mp in range(4):
                        m = 4 * j + mp
                        for tb in range(NTB):
                            ps, psk = mm_group(wt, wtk, mp, mixT, "H", tb)
                            cs = slice(tb * TB, (tb + 1) * TB)
                            DV("scalar_tensor_tensor", R[:, m, cs], R[:, m, cs], ALPHA, ps[:, 0:TB], ALU.mult, ALU.add,
                               r=[("R", m), psk], w=[("R", m)])
                layer_norm(PV_G1, PV_B1, True)

                for Q in range(4):
                    for j in range(4):
                        wt, wtk = load_wt(w_up, 0, Q * 2048 + 512 * j)
                        for mp in range(4):
                            fc = 4 * j + mp
                            for tb in range(NTB):
                                ps, psk = mm_group(wt, wtk, mp, XB, "XB", tb)
                                cs = slice(tb * TB, (tb + 1) * TB)
                                rl, rlk = relr.next()
                                AC(rl[:], ps[:, 0:TB], AF.Relu, r=[psk], w=[rlk])
                                DV("tensor_tensor", Hq[:, fc, cs], rl[:], rl[:], ALU.mult, r=[rlk], w=[("H", fc)])
                    for j in range(4):
                        wt, wtk = load_wt(w_down, Q * 2048, 512 * j)
                        for mp in range(4):
                            m = 4 * j + mp
                            for tb in range(NTB):
                                ps, psk = mm_group(wt, wtk, mp, Hq, "H", tb)
                                cs = slice(tb * TB, (tb + 1) * TB)
                                if Q == 0:
                                    DV("scalar_tensor_tensor", R[:, m, cs], R[:, m, cs], ALPHA, ps[:, 0:TB], ALU.mult, ALU.add,
                                       r=[("R", m), psk], w=[("R", m)])
                                else:
                                    DV("tensor_tensor", R[:, m, cs], R[:, m, cs], ps[:, 0:TB], ALU.add,
                                       r=[("R", m), psk], w=[("R", m)])
                layer_norm(PV_G2, PV_B2, False)
                for j in range(4):
                    DMA("sp", yT[512 * j:512 * (j + 1), :].rearrange("(c p) t -> p c t", p=128), R[:, 4 * j:4 * j + 4, :],
                        r=[("R", m) for m in range(4 * j, 4 * j + 4)])
                P.barrier()

        except _Stop:
            pass

        P.emit()
    return nc


def _perm_matrix():
    m = np.arange(128)
    d = m % 64
    partner = np.where(d < 32, m + 32, m - 32)
    P = np.zeros((128, 128), np.float32)
    P[partner, m] = 1.0
    return P


def _rope_tables(pos):
    inv = (np.float32(10000.0) ** (-(np.arange(0, 64, 2, dtype=np.float32) / np.float32(64)))).astype(np.float32)
    ang = pos.astype(np.float32)[:, None] * inv[None, :]
    cos = np.cos(ang).astype(np.float32)
    sin = np.sin(ang).astype(np.float32)
    d = np.arange(128) % 64
    f = d % 32
    sgn = np.where(d < 32, -1.0, 1.0).astype(np.float32)
    cosT = np.ascontiguousarray(cos[:, f].T)
    sinT = np.ascontiguousarray((sin[:, f] * sgn[None, :]).T)
    return cosT, sinT


_NC_CACHE = {}
import os as _os
_DBG = _os.environ.get('KDBG', '')


def _prep(x_prompt, x_sample, cache_k, cache_v, state_h, state_conv,
           w_in, lambda_q1, lambda_k1, lambda_q2, lambda_k2, subln_g,
           conv_w, conv_b, w_rg_a, b_rg_a, w_rg_i, b_rg_i, lru_lambda,
           w_out, ln1_g, ln1_b, w_up, w_down, ln2_g, ln2_b):
    f32 = np.float32
    x_prompt = np.asarray(x_prompt, f32); x_sample = np.asarray(x_sample, f32)
    cache_k = np.asarray(cache_k, f32); cache_v = np.asarray(cache_v, f32)
    w_in0 = np.asarray(w_in, f32)[0]
    wq_ = w_in0[:, 0:1024]; wk_ = w_in0[:, 1024:2048]; wv_ = w_in0[:, 2048:3072]
    wxb_ = w_in0[:, 3072:4096]; wg_ = w_in0[:, 4096:5120]
    col = np.arange(1024)
    dd = col % 64
    partner = np.where(dd < 32, col + 32, col - 32)
    shared = {
        "wq": np.ascontiguousarray(wq_), "wqp": np.ascontiguousarray(wq_[:, partner]),
        "wk": np.ascontiguousarray(wk_), "wkp": np.ascontiguousarray(wk_[:, partner]),
        "wv": np.ascontiguousarray(wv_), "wxb": np.ascontiguousarray(wxb_), "wg": np.ascontiguousarray(wg_),
        "w_out": np.ascontiguousarray(np.asarray(w_out, f32)[0]),
        "w_up": np.ascontiguousarray(np.asarray(w_up, f32)[0]),
        "w_down": np.ascontiguousarray(np.asarray(w_down, f32)[0]),
        "identd": np.eye(128, dtype=f32),
        "permd": _perm_matrix(),
        "subgd": np.ascontiguousarray(np.broadcast_to(np.asarray(subln_g, f32)[0][None, :], (128, 128))),
    }
    lam4 = np.stack([np.asarray(v, f32)[0] for v in (lambda_q1, lambda_k1, lambda_q2, lambda_k2)], 0)
    shared["lamd"] = np.ascontiguousarray(np.broadcast_to(lam4[None], (128, 4, 64)))
    wa = np.asarray(w_rg_a, f32)[0]; wi = np.asarray(w_rg_i, f32)[0]
    wabd = np.zeros((128, 8, 128), f32); wibd = np.zeros((128, 8, 128), f32)
    for ch in range(8):
        for t in range(2):
            wabd[64 * t:64 * t + 64, ch, 64 * t:64 * t + 64] = wa[2 * ch + t]
            wibd[64 * t:64 * t + 64, ch, 64 * t:64 * t + 64] = wi[2 * ch + t]
    shared["wabd"] = wabd; shared["wibd"] = wibd
    pvd = np.zeros((128, 160), f32)

    def pc(v, n):
        return np.asarray(v, f32).reshape(n, 128).T

    cw = np.asarray(conv_w, f32)[0]
    for ch in range(8):
        for j in range(4):
            pvd[:, ch * 4 + j] = cw[j, ch * 128:(ch + 1) * 128]
    pvd[:, 32:40] = pc(np.asarray(conv_b)[0], 8)
    pvd[:, 40:48] = pc(np.asarray(b_rg_a)[0], 8)
    pvd[:, 48:56] = pc(np.asarray(b_rg_i)[0], 8)
    pvd[:, 56:64] = pc(np.asarray(lru_lambda)[0], 8)
    pvd[:, 64:80] = pc(np.asarray(ln1_g)[0], 16)
    pvd[:, 80:96] = pc(np.asarray(ln1_b)[0], 16)
    pvd[:, 96:112] = pc(np.asarray(ln2_g)[0], 16)
    pvd[:, 112:128] = pc(np.asarray(ln2_b)[0], 16)
    pvd[:, 152] = np.asarray(subln_g, f32)[0]
    shared["pvd"] = pvd

    in_maps = []
    for c in range(8):
        b, qi = c // 4, c % 4
        shift = 3072 - 1024 * qi
        m = dict(shared)
        xr = np.zeros((D, NSLOT), f32)
        xr[:, shift:] = x_prompt[b, :1024 * (qi + 1), :].T
        m["xrot"] = xr
        xo = np.empty((D, T2), f32)
        xo[:, :TOWN] = x_prompt[b, 1024 * qi:1024 * (qi + 1), :].T
        xo[:, TOWN:] = x_sample[2 * c:2 * c + 2].reshape(NSMP, D).T
        m["xown"] = xo
        pos = np.concatenate([np.maximum(np.arange(NSLOT) - shift, 0), 2048 + np.arange(16), 2048 + np.arange(16)])
        cosT, sinT = _rope_tables(pos)
        m["cosd"] = cosT; m["sind"] = sinT
        valid = np.concatenate([(np.arange(NSLOT) >= shift).astype(f32), np.ones(NSMP, f32)])
        m["vald"] = np.ascontiguousarray(np.broadcast_to(valid[None, :], (128, NSLOT + NSMP)))
        kb_valid = (np.arange(32) * 128 >= shift)
        m["kbiasd"] = np.ascontiguousarray(np.broadcast_to(np.where(kb_valid, 0.0, NEG).astype(f32)[None, :], (128, 32)))
        ck = cache_k[0, 2 * c:2 * c + 2]
        m["ckT"] = np.ascontiguousarray(ck.transpose(0, 2, 3, 1))
        cv = cache_v[0, 2 * c:2 * c + 2].reshape(2, 16, 128, 8, 128)
        m["cvd"] = np.ascontiguousarray(cv.transpose(0, 3, 2, 1, 4))
        sh = np.asarray(state_h, f32)[0, 2 * c:2 * c + 2]
        m["shd"] = np.ascontiguousarray(sh.reshape(2, 8, 128).transpose(2, 1, 0))
        sc = np.asarray(state_conv, f32)[0, 2 * c:2 * c + 2]
        m["scd"] = np.ascontiguousarray(sc.reshape(2, 3, 8, 128).transpose(3, 2, 0, 1))
        in_maps.append(m)

    return in_maps


def _assemble(R):
    f32 = np.float32
    y_prompt = np.empty((2, 4096, D), f32); y_sample = np.empty((16, 16, D), f32)
    k_prompt = np.empty((1, 2, 4096, 8, 128), f32); v_prompt = np.empty((1, 2, 4096, 8, 128), f32)
    h_prompt = np.empty((1, 2, 1024), f32); conv_prompt = np.empty((1, 2, 3, 1024), f32)
    k_sample = np.empty((1, 16, 16, 8, 128), f32); v_sample = np.empty((1, 16, 16, 8, 128), f32)
    h_sample = np.empty((1, 16, 1024), f32); conv_sample = np.empty((1, 16, 3, 1024), f32)
    for c in range(8):
        b, qi = c // 4, c % 4
        r = R[c]
        yT = np.asarray(r["yT"], f32)
        y_prompt[b, 1024 * qi:1024 * (qi + 1)] = yT[:, :TOWN].T
        y_sample[2 * c:2 * c + 2] = yT[:, TOWN:].T.reshape(2, 16, D)
        kT = np.asarray(r["koutT"], f32); vT = np.asarray(r["voutT"], f32)
        k_prompt[0, b, 1024 * qi:1024 * (qi + 1)] = kT[:, :, :TOWN].transpose(2, 0, 1)
        v_prompt[0, b, 1024 * qi:1024 * (qi + 1)] = vT[:, :, :TOWN].transpose(2, 0, 1)
        k_sample[0, 2 * c:2 * c + 2] = kT[:, :, TOWN:].transpose(2, 0, 1).reshape(2, 16, 8, 128)
        v_sample[0, 2 * c:2 * c + 2] = vT[:, :, TOWN:].transpose(2, 0, 1).reshape(2, 16, 8, 128)
        ho = np.asarray(r["hout"], f32)
        co = np.asarray(r["cout"], f32)
        if qi == 3:
            h_prompt[0, b] = ho[:, :, 0].T.reshape(1024)
            conv_prompt[0, b] = co[:, :, 0, :].transpose(2, 1, 0).reshape(3, 1024)
        for bi in range(2):
            h_sample[0, 2 * c + bi] = ho[:, :, 1 + bi].T.reshape(1024)
            conv_sample[0, 2 * c + bi] = co[:, :, 1 + bi, :].transpose(2, 1, 0).reshape(3, 1024)
    return (y_prompt, y_sample, k_prompt, v_prompt, h_prompt, conv_prompt,
            k_sample, v_sample, h_sample, conv_sample)


def kernel(x_prompt, x_sample, cache_k, cache_v, state_h, state_conv,
           w_in, lambda_q1, lambda_k1, lambda_q2, lambda_k2, subln_g,
           conv_w, conv_b, w_rg_a, b_rg_a, w_rg_i, b_rg_i, lru_lambda,
           w_out, ln1_g, ln1_b, w_up, w_down, ln2_g, ln2_b):
    in_maps = _prep(x_prompt, x_sample, cache_k, cache_v, state_h, state_conv,
                    w_in, lambda_q1, lambda_k1, lambda_q2, lambda_k2, subln_g,
                    conv_w, conv_b, w_rg_a, b_rg_a, w_rg_i, b_rg_i, lru_lambda,
                    w_out, ln1_g, ln1_b, w_up, w_down, ln2_g, ln2_b)
    if "nc" not in _NC_CACHE:
        _NC_CACHE["nc"] = build_nc()
    nc = _NC_CACHE["nc"]
    res = run_bass_kernel_spmd(nc, in_maps, core_ids=list(range(8)))
    return _assemble(res.results)
```

```python
import numpy as np
from contextlib import ExitStack
import concourse.bass as bass
import concourse.mybir as mybir
from concourse.bass_utils import run_bass_kernel_spmd

F32 = mybir.dt.float32
BF16 = mybir.dt.bfloat16
ALU = mybir.AluOpType
AF = mybir.ActivationFunctionType

D = 2048
NKC = 16
NSLOT = 4096
BLK = 512
NBLK = 8
OWN0 = 3072
TOWN = 1024
NSMP = 32
T2 = TOWN + NSMP
TB = 352
NTB = 3
LAM_INIT = 0.8 - 0.6 * 1.0
ALPHA = 2.0 ** 0.25
LN_EPS = 1e-5
RMS_EPS = 1e-5
QSCALE = 64 ** -0.5
NEG = -30000.0


class Op:
    __slots__ = ("eng", "fn", "dma", "deps", "signal", "semi", "count")


class Prog:
    CE = ("pe", "act", "dve", "pool")
    DQ = ("sp", "pool", "act")

    def __init__(self, nc, es, ndma=6):
        self.nc = nc
        self.ops = []
        self.lw = {}
        self.rd = {}
        self.sems = []
        self.esem = {}
        for e in self.CE:
            self.esem[e] = len(self.sems)
            self.sems.append(es.enter_context(nc.semaphore("sem_" + e)))
        self.ndma = ndma
        self.dsem = {}
        self.dcnt = {}
        self.dlast = {}
        self.dnext = {}
        for q in self.DQ:
            self.dsem[q] = []
            for i in range(ndma):
                self.dsem[q].append(len(self.sems))
                self.sems.append(es.enter_context(nc.semaphore("dq_%s_%d" % (q, i))))
            self.dcnt[q] = [0] * ndma
            self.dlast[q] = [None] * ndma
            self.dnext[q] = 0
        self.last_of = {e: None for e in self.CE}
        self.stopped = False

    def add(self, eng, meth, args=(), kw=None, reads=(), writes=(), dma=False):
        op = Op()
        op.eng = eng
        op.fn = (meth, tuple(args), dict(kw or {}))
        op.dma = dma
        op.signal = False
        op.semi = None
        op.count = 0
        op.deps = []
        if self.stopped:
            return op
        def _expand(keys):
            out = []
            for k in keys:
                out.append(k)
                if isinstance(k, tuple) and len(k) > 2:
                    out.append(k[:2])
            return out
        reads = _expand(reads)
        writes = _expand(writes)
        banks = set()
        for a in list(args) + list((kw or {}).values()):
            t = getattr(a, "tensor", None)
            if t is not None and type(t).__name__ == "PSumTensorHandle":
                banks.add(("BANK", t.name))
        if banks:
            writes = list(writes) + sorted(banks)
        raw = []
        other = []
        for k in reads:
            w = self.lw.get(k)
            if w is not None:
                raw.append(w)
        for k in writes:
            w = self.lw.get(k)
            if w is not None:
                other.append(w)
            r = self.rd.get(k)
            if r:
                other.extend(r[0].values())
                other.extend(r[1])
        deps = []
        seen = set()
        for lst, is_raw in ((raw, True), (other, False)):
            for d in lst:
                if id(d) in seen:
                    continue
                if (not dma) and (not d.dma) and d.eng == eng:
                    if eng == "pe":
                        continue
                seen.add(id(d))
                deps.append(d)
        if dma:
            i = self.dnext[eng]
            self.dnext[eng] = (i + 1) % self.ndma
            prev = self.dlast[eng][i]
            if prev is not None and id(prev) not in seen:
                deps.append(prev)
            self.dcnt[eng][i] += 16
            op.semi = self.dsem[eng][i]
            op.count = self.dcnt[eng][i]
            self.dlast[eng][i] = op
        for d in deps:
            d.signal = True
        op.deps = deps
        for k in writes:
            self.lw[k] = op
            self.rd[k] = [{}, []]
        for k in reads:
            if k in writes:
                continue
            r = self.rd.setdefault(k, [{}, []])
            if dma:
                r[1].append(op)
            else:
                r[0][eng] = op
        if not dma:
            self.last_of[eng] = op
        self.ops.append(op)
        return op

    def barrier(self):
        if self.stopped:
            return
        deps = [o for o in self.last_of.values() if o is not None]
        for q in self.DQ:
            deps.extend(o for o in self.dlast[q] if o is not None)
        for d in deps:
            d.signal = True
        for e in ("pe", "act", "dve", "pool", "sp"):
            op = Op()
            op.eng = e
            op.fn = None
            op.dma = False
            op.signal = False
            op.semi = None
            op.count = 0
            op.deps = list(deps)
            self.ops.append(op)
        self.lw = {}
        self.rd = {}

    def emit(self):
        nc = self.nc
        cnt = {e: 0 for e in self.CE}
        for op in self.ops:
            if (not op.dma) and op.signal and op.fn is not None:
                cnt[op.eng] += 1
                op.semi = self.esem[op.eng]
                op.count = cnt[op.eng]
        byeng = {e: [] for e in ("pe", "act", "dve", "pool", "sp")}
        for op in self.ops:
            byeng[op.eng].append(op)
        sems = self.sems

        def run(name, h):
            waited = {}
            for op in byeng[name]:
                for d in op.deps:
                    if d.semi is None:
                        continue
                    if waited.get(d.semi, 0) >= d.count:
                        continue
                    h.wait_ge(sems[d.semi], d.count)
                    waited[d.semi] = d.count
                if op.fn is None:
                    continue
                meth, args, kw = op.fn
                ins = getattr(h, meth)(*args, **kw)
                if op.dma:
                    ins.then_inc(sems[op.semi], 16)
                elif op.signal:
                    ins.then_inc(sems[op.semi], 1)
            if name == "sp":
                for q in self.DQ:
                    for i in range(self.ndma):
                        c = self.dcnt[q][i]
                        if c > 0 and waited.get(self.dsem[q][i], 0) < c:
                            h.wait_ge(sems[self.dsem[q][i]], c)

        with nc.Block() as block:
            @block.tensor
            def _(h):
                run("pe", h)

            @block.scalar
            def _(h):
                run("act", h)

            @block.vector
            def _(h):
                run("dve", h)

            @block.gpsimd
            def _(h):
                run("pool", h)

            @block.sync
            def _(h):
                run("sp", h)


class _CompView:
    def __init__(self, t, c):
        self.t = t
        self.c = c

    def __getitem__(self, idx):
        rows, cols = idx
        return self.t[rows, self.c, cols]


class Rot:
    def __init__(self, name, tiles, keys=None):
        self.name = name
        self.tiles = tiles
        self.keys = keys if keys is not None else [(name, j) for j in range(len(tiles))]
        self.i = 0

    def next(self):
        j = self.i % len(self.tiles)
        self.i += 1
        return self.tiles[j], self.keys[j]


_PROG = [None]


class _Stop(Exception):
    pass


def build_nc(stop=None):
    nc = bass.Bass("TRN2", target_bir_lowering=False)

    def din(name, shape, dt=F32):
        return nc.dram_tensor(name, list(shape), dt, kind="ExternalInput").ap()

    def dout(name, shape, dt=F32):
        return nc.dram_tensor(name, list(shape), dt, kind="ExternalOutput").ap()

    xrot = din("xrot", [D, NSLOT])
    xown = din("xown", [D, T2])
    wq = din("wq", [D, 1024]); wqp = din("wqp", [D, 1024])
    wk = din("wk", [D, 1024]); wkp = din("wkp", [D, 1024])
    wv = din("wv", [D, 1024]); wxb = din("wxb", [D, 1024]); wg = din("wg", [D, 1024])
    w_out = din("w_out", [D, D]); w_up = din("w_up", [D, 4 * D]); w_down = din("w_down", [4 * D, D])
    cosd = din("cosd", [128, NSLOT + NSMP]); sind = din("sind", [128, NSLOT + NSMP])
    vald = din("vald", [128, NSLOT + NSMP])
    kbiasd = din("kbiasd", [128, 32])
    wabd = din("wabd", [128, 8, 128]); wibd = din("wibd", [128, 8, 128])
    pvd = din("pvd", [128, 160])
    lamd = din("lamd", [128, 4, 64])
    subgd = din("subgd", [128, 128])
    identd = din("identd", [128, 128])
    permd = din("permd", [128, 128])
    ckT = din("ckT", [2, 8, 128, 2048])
    cvd = din("cvd", [2, 8, 128, 16, 128])
    shd = din("shd", [128, 8, 2])
    scd = din("scd", [128, 8, 2, 3])

    yT = dout("yT", [D, T2])
    koutT = dout("koutT", [8, 128, T2])
    voutT = dout("voutT", [8, 128, T2])
    hout = dout("hout", [128, 8, 3])
    cout = dout("cout", [128, 8, 3, 3])

    scr_k = nc.dram_tensor("scr_k", [8, 128, NSLOT], BF16).ap()
    scr_v = nc.dram_tensor("scr_v", [8, 128, 32, 129], BF16).ap()

    PV_CW = 0; PV_CB = 32; PV_BA = 40; PV_BI = 48; PV_LL = 56
    PV_G1 = 64; PV_B1 = 80; PV_G2 = 96; PV_B2 = 112
    PV_HBA = 128; PV_HBI = 136; PV_HC = 144
    AX = mybir.AxisListType.X

    with ExitStack() as es:
        P = Prog(nc, es)
        _PROG[0] = P

        def CHK(name):
            if stop == name:
                P.barrier()
                P.stopped = True

        def sb(name, shape, dt=F32, scope=es):
            return scope.enter_context(nc.sbuf_tensor(name, list(shape), dt))

        def OP(eng, meth, *args, r=(), w=(), **kw):
            return P.add(eng, meth, args, kw, list(r), list(w), False)

        def DV(meth, *args, r=(), w=(), **kw):
            return P.add("dve", meth, args, kw, list(r), list(w), False)

        def AC(out, in_, func, r=(), w=(), **kw):
            return P.add("act", "activation", (), dict(out=out, in_=in_, func=func, **kw), list(r), list(w), False)

        def PL(meth, *args, r=(), w=(), **kw):
            return P.add("pool", meth, args, kw, list(r), list(w), False)

        def MM(out, lhsT, rhs, r=(), w=(), **kw):
            return P.add("pe", "matmul", (out, lhsT, rhs), kw, list(r), list(w), False)

        def TR(out, in_, idn, r=(), w=()):
            return P.add("pe", "transpose", (out, in_, idn), {}, list(r), list(w), False)

        def DMA(q, out, in_, r=(), w=()):
            return P.add(q, "dma_start", (), dict(out=out, in_=in_), list(r), list(w), True)

        try:
            ident32 = sb("ident32", [128, 128])
            permT = sb("permT", [128, 128])
            mixT = sb("mixT", [128, 16, T2], BF16)
            pv = sb("pv", [128, 160])
            ident = sb("ident", [128, 128], BF16)
            ones32 = sb("ones32", [128, 128])
            lamt = sb("lamt", [128, 4])
            gsub = sb("gsub", [128, 128])
            zero_t = sb("zero_t", [128, 64])

            DMA("sp", pv[:, :], pvd[:, :], w=["pv"])
            DMA("pool", ident[:], identd[:, :], w=["ident"])
            DMA("sp", ident32[:], identd[:, :], w=["ident32"])
            DMA("sp", permT[:], permd[:, :], w=["permT"])
            PL("memset", ones32[:], 1.0, w=["ones32"])
            PL("memset", zero_t[:], 0.0, w=["zero"])
            DMA("sp", gsub[:], subgd[:, :], w=["gsub"])
            with ExitStack() as es0:
                lamin = sb("lamin", [128, 4, 64], scope=es0)
                lamp = sb("lamp", [128, 2, 64], scope=es0)
                lams = sb("lams", [128, 4], scope=es0)
                sp_t = sb("sp_t", [128, 16], scope=es0)
                DMA("sp", lamin[:], lamd[:, :, :], w=["lamin"])
                DV("tensor_tensor", lamp[:, 0, :], lamin[:, 0, :], lamin[:, 1, :], ALU.mult, r=["lamin"], w=["lamp0"])
                DV("tensor_tensor", lamp[:, 1, :], lamin[:, 2, :], lamin[:, 3, :], ALU.mult, r=["lamin"], w=["lamp1"])
                DV("reduce_sum", lams[:, 0:1], lamp[:, 0, :], AX, r=["lamp0"], w=["lams0"])
                DV("reduce_sum", lams[:, 1:2], lamp[:, 1, :], AX, r=["lamp1"], w=["lams1"])
                AC(lams[:, 2:4], lams[:, 0:2], AF.Exp, r=["lams0", "lams1"], w=["lams23"])
                DV("scalar_tensor_tensor", lamt[:, 0:1], lams[:, 2:3], LAM_INIT, lams[:, 3:4], ALU.add, ALU.subtract,
                   r=["lams23"], w=["lam0"])
                DV("tensor_scalar", lamt[:, 1:2], lamt[:, 0:1], -1.0, None, ALU.mult, r=["lam0"], w=["lam"])
                DV("tensor_scalar", gsub[:], gsub[:], 1.0 - LAM_INIT, None, ALU.mult, r=["gsub"], w=["gsub"])
                DV("tensor_scalar", pv[:, 152:153], pv[:, 152:153], 1.0 - LAM_INIT, None, ALU.mult, r=["pv"], w=["pvg"])
                DV("tensor_scalar", pv[:, PV_HBA:PV_HBA + 16], pv[:, PV_BA:PV_BA + 16], -1.0, None, ALU.mult,
                   r=["pv"], w=["pvh"])
                AC(sp_t[:, 0:8], pv[:, PV_LL:PV_LL + 8], AF.Exp, r=["pv"], w=["sp0"], scale=-1.0)
                DV("tensor_scalar", sp_t[:, 0:8], sp_t[:, 0:8], 1.0, None, ALU.add, r=["sp0"], w=["sp1"])
                AC(sp_t[:, 8:16], sp_t[:, 0:8], AF.Ln, r=["sp1"], w=["sp2"])
                DV("tensor_scalar", pv[:, PV_HC:PV_HC + 8], sp_t[:, 8:16], -8.0, None, ALU.mult, r=["sp2"], w=["pvc"])
                P.barrier()

            with ExitStack() as es1:
                ksmp = sb("ksmp", [128, 8, NSMP], BF16, scope=es1)
                vsmp = sb("vsmp", [16, 2, 8, 129], BF16, scope=es1)

                def run_passes(stage):
                    with ExitStack() as esp:
                        psf = [esp.enter_context(nc.psum_tensor("ps%d%s" % (i, stage), [128, 512], F32)) for i in range(7)]
                        pst = esp.enter_context(nc.psum_tensor("pst" + stage, [128, 4, 128], BF16))
                        wbuf = sb("wbuf" + stage, [128, NKC, 2048], BF16, scope=esp)
                        xrotr = Rot("xblk", [sb("xblk%d%s" % (i, stage), [128, NKC, BLK], BF16, scope=esp) for i in range(2)])
                        tabr = Rot("tab", [sb("tab%d%s" % (i, stage), [128, 3, BLK], scope=esp) for i in range(2)])
                        wab = sb("wab" + stage, [128, 8, 128], BF16, scope=esp)
                        wib = sb("wib" + stage, [128, 8, 128], BF16, scope=esp)
                        hcar = sb("hcar" + stage, [128, 8, 3], scope=esp)
                        xcar = sb("xcar" + stage, [128, 8, 3, 3], scope=esp)
                        _ft = [sb("f32t%d%s" % (i, stage), [128, BLK], scope=esp) for i in range(18 if stage == "A" else 12)]
                        _fk = [("f32t", i) for i in range(18 if stage == "A" else 12)]
                        _nw = 8 if stage == "A" else 6
                        f32r = Rot("f32t", _ft[0:_nw], _fk[0:_nw])
                        stgr = Rot("stg", _ft[_nw:], _fk[_nw:])
                        rXC = Rot("rXC", _ft[0:3], _fk[0:3])
                        rTR = Rot("rTR", _ft[3:6], _fk[3:6])
                        rTI = Rot("rTI", _ft[6:9], _fk[6:9])
                        rB2 = Rot("rB2", _ft[9:12], _fk[9:12])
                        rHB = Rot("rHB", _ft[12:15], _fk[12:15])
                        rG2 = Rot("rG2", _ft[15:18], _fk[15:18])
                        bfr = Rot("bft", [sb("bft%d%s" % (i, stage), [128, BLK], BF16, scope=esp) for i in range(4)])
                        xbtr = Rot("xbt", [sb("xbt%d%s" % (i, stage), [128, BLK + 8], scope=esp) for i in range(3)])
                        vaugr = Rot("vaug", [sb("vaug%d%s" % (i, stage), [128, 4, 129], BF16, scope=esp) for i in range(2)])
                        psA = Rot("psA", [psf[0], psf[1]])
                        psB = Rot("psB", [psf[2], psf[3]])
                        psG = Rot("psG", [psf[4], psf[5], psf[6]])

                        if stage == "A":
                            DMA("pool", wab[:], wabd[:, :, :], w=["wab"])
                            DMA("pool", wib[:], wibd[:, :, :], w=["wib"])
                            PL("memset", hcar[:], 0.0, w=["hcar"])
                            PL("memset", xcar[:], 0.0, w=["xcar"])
                            DMA("sp", hcar[:, :, 1:3], shd[:, :, :], r=["hcar"], w=["hcar"])
                            DMA("sp", xcar[:, :, 1:3, :], scd[:, :, :, :], r=["xcar"], w=["xcar"])
                            for vi, vt_ in enumerate(vaugr.tiles):
                                PL("memset", vt_[:, :, 128:129], 1.0, w=[("vaug", vi)])
                            PL("memset", vsmp[:, :, :, 128:129], 1.0, w=["vsmp"])

                        def load_w(half, src):
                            DMA("pool", wbuf[:, :, half * 1024:(half + 1) * 1024],
                                src.rearrange("(c p) n -> p c n", p=128), w=[("W", half)])

                        def blocks(own_only):
                            lst = []
                            for b in range(NBLK):
                                if own_only and b < 6:
                                    continue
                                lst.append((b, b * BLK, BLK))
                            lst.append((8, NSLOT, NSMP))
                            return lst

                        def load_x(b, c0, n, fast=True):
                            xt, xk = xrotr.next()
                            tt, tk = tabr.next()
                            if b < 8 and fast:
                                for j in range(NKC):
                                    stg, sk = stgr.next()
                                    DMA("sp", stg[:, 0:n], xrot[128 * j:128 * (j + 1), c0:c0 + n], w=[sk])
                                    AC(xt[:, j, 0:n], stg[:, 0:n], AF.Copy, r=[sk], w=[xk + (j,)])
                            else:
                                src = xrot[:, c0:c0 + n] if b < 8 else xown[:, TOWN:T2]
                                DMA("pool", xt[:, :, 0:n], src.rearrange("(c p) s -> p c s", p=128), w=[xk])
                            DMA("sp", tt[:, 0, 0:n], cosd[:, c0:c0 + n], w=[tk + (0,)])
                            DMA("sp", tt[:, 1, 0:n], sind[:, c0:c0 + n], w=[tk + (1,)])
                            DMA("sp", tt[:, 2, 0:n], vald[:, c0:c0 + n], w=[tk + (2,)])
                            return xt, xk, tt, tk

                        def proj(ps, psk, half, chunk, xt, xk, n):
                            for kc in range(NKC):
                                MM(ps[:, 0:n], wbuf[:, kc, half * 1024 + chunk * 128: half * 1024 + (chunk + 1) * 128],
                                   xt[:, kc, 0:n], r=[("W", half), xk], w=[psk], start=(kc == 0), stop=(kc == NKC - 1))

                        def own_col(b):
                            return {6: 0, 7: 512, 8: 1024}.get(b)

                        def rope_pass(w_a, w_b, is_q):
                            load_w(0, w_a)
                            for (b, c0, n) in blocks(own_only=is_q):
                                xt, xk, tt, tk = load_x(b, c0, n)
                                oc = own_col(b)
                                for h in range(8):
                                    pa, pak = psA.next()
                                    pb, pbk = psB.next()
                                    proj(pa, pak, 0, h, xt, xk, n)
                                    ks, ksk = f32r.next()
                                    AC(ks[:, 0:n], pa[:, 0:n], AF.Copy, r=[pak], w=[ksk])
                                    MM(pb[:, 0:n], permT[:], ks[:, 0:n], r=["permT", ksk], w=[pbk], start=True, stop=True)
                                    t1, t1k = f32r.next()
                                    t2, t2k = f32r.next()
                                    DV("tensor_tensor", t1[:, 0:n], ks[:, 0:n], tt[:, 0, 0:n], ALU.mult, r=[ksk, tk + (0,)], w=[t1k])
                                    DV("tensor_tensor", t2[:, 0:n], pb[:, 0:n], tt[:, 1, 0:n], ALU.mult, r=[pbk, tk + (1,)], w=[t2k])
                                    if is_q:
                                        DV("tensor_tensor", qT[:, h, oc:oc + n], t1[:, 0:n], t2[:, 0:n], ALU.add,
                                           r=[t1k, t2k], w=[("qT", h, b)])
                                    else:
                                        kf, kfk = f32r.next()
                                        DV("tensor_tensor", kf[:, 0:n], t1[:, 0:n], t2[:, 0:n], ALU.add, r=[t1k, t2k], w=[kfk])
                                        if b < 8:
                                            kb_, kbk = bfr.next()
                                            AC(kb_[:, 0:n], kf[:, 0:n], AF.Copy, r=[kfk], w=[kbk])
                                            DMA("sp", scr_k[h, :, c0:c0 + n], kb_[:, 0:n], r=[kbk], w=[("scrk", h)])
                                        else:
                                            AC(ksmp[:, h, :], kf[:, 0:n], AF.Copy, r=[kfk], w=[("ksmp", h)])
                                        if oc is not None:
                                            DMA("sp", koutT[h, :, oc:oc + n], kf[:, 0:n], r=[kfk])

                        if stage == "B":
                            rope_pass(wq, wqp, True)
                            CHK("p3")
                            P.barrier()
                            return
                        rope_pass(wk, wkp, False)
                        CHK("p1")

                        load_w(0, wv)
                        for (b, c0, n) in blocks(own_only=False):
                            xt, xk, tt, tk = load_x(b, c0, n)
                            oc = own_col(b)
                            for h in range(8):
                                pa, pak = psA.next()
                                proj(pa, pak, 0, h, xt, xk, n)
                                vb, vbk = bfr.next()
                                AC(vb[:, 0:n], pa[:, 0:n], AF.Copy, r=[pak], w=[vbk])
                                if oc is not None:
                                    vf, vfk = f32r.next()
                                    DV("tensor_copy", vf[:, 0:n], pa[:, 0:n], r=[pak], w=[vfk])
                                    DMA("sp", voutT[h, :, oc:oc + n], vf[:, 0:n], r=[vfk])
                                if _DBG == "notr":
                                    continue
                                if b < 8:
                                    va, vak = vaugr.next()
                                    for s4 in range(4):
                                        TR(pst[:, s4, :], vb[:, s4 * 128:(s4 + 1) * 128], ident[:], r=[vbk, "ident"], w=["pst"])
                                    DV("tensor_copy", va[:, :, 0:128], pst[:, :, :], r=["pst"], w=[vak])
                                    DMA("sp", scr_v[h, :, b * 4:(b + 1) * 4, :], va[:, :, :], r=[vak], w=[("scrv", h)])
                                else:
                                    for bi in range(2):
                                        TR(pst[0:16, bi, :], vb[:, bi * 16:(bi + 1) * 16], ident[:], r=[vbk, "ident"], w=["pst"])
                                    DV("tensor_copy", vsmp[:, :, h, 0:128], pst[0:16, 0:2, :], r=["pst"], w=["vsmp"])
                            CHK("p2a_b%d" % b)
                        CHK("p2a")

                        load_w(0, wxb)
                        load_w(1, wg)
                        def lru_chain(b, n, ch, xt, xk, tt, tk, oc):
                            pa, pak = psA.next()
                            proj(pa, pak, 0, ch, xt, xk, n)
                            segs = [(0, 0, n)] if b < 8 else [(1, 0, 16), (2, 16, 16)]
                            xbt, xbk = xbtr.next()
                            xc, xck = rXC.next()
                            cwc = PV_CW + ch * 4
                            for (ci, s0, sn) in segs:
                                off = s0 + (3 if ci == 2 else 0)
                                kh = xbk + ("h", ci)
                                kd = xbk + ("d", ci)
                                DV("tensor_copy", xbt[:, off:off + 3], xcar[:, ch, ci, :], r=["xcar"], w=[kh])
                                AC(xbt[:, off + 3:off + 3 + sn], pa[:, s0:s0 + sn], AF.Copy, r=[pak], w=[kd])
                            yield
                            for (ci, s0, sn) in segs:
                                off = s0 + (3 if ci == 2 else 0)
                                kh = xbk + ("h", ci)
                                kd = xbk + ("d", ci)
                                DV("tensor_copy", xcar[:, ch, ci, :], xbt[:, off + sn:off + sn + 3], r=[kd, kh], w=["xcar"])
                                AC(xc[:, s0:s0 + sn], xbt[:, off + 3:off + 3 + sn], AF.Identity,
                                   r=[kd, kh, "pv"], w=[xck + (ci,)],
                                   scale=pv[:, cwc + 3:cwc + 4], bias=pv[:, PV_CB + ch:PV_CB + ch + 1])
                                yield
                                for j in range(3):
                                    DV("scalar_tensor_tensor", xc[:, s0:s0 + sn], xbt[:, off + j:off + j + sn],
                                       pv[:, cwc + j:cwc + j + 1], xc[:, s0:s0 + sn], ALU.mult, ALU.add,
                                       r=[kd, kh, xck + (ci,)], w=[xck + (ci,)])
                                    yield
                            xck_all = [xck + (ci,) for (ci, _, _) in segs]
                            xcb, xcbk = bfr.next()
                            AC(xcb[:, 0:n], xc[:, 0:n], AF.Copy, r=xck_all, w=[xcbk])
                            yield
                            pr, prk = psG.next()
                            pi, pik = psG.next()
                            MM(pr[:, 0:n], wab[:, ch, :], xcb[:, 0:n], r=["wab", xcbk], w=[prk], start=True, stop=True)
                            MM(pi[:, 0:n], wib[:, ch, :], xcb[:, 0:n], r=["wib", xcbk], w=[pik], start=True, stop=True)
                            tr, trk = rTR.next()
                            ti, tik = rTI.next()
                            AC(tr[:, 0:n], pr[:, 0:n], AF.Exp, r=[prk, "pvh"], w=[trk],
                               bias=pv[:, PV_HBA + ch:PV_HBA + ch + 1], scale=-1.0)
                            AC(ti[:, 0:n], pi[:, 0:n], AF.Exp, r=[pik, "pvh"], w=[tik],
                               bias=pv[:, PV_HBI + ch:PV_HBI + ch + 1], scale=-1.0)
                            yield
                            AC(tr[:, 0:n], tr[:, 0:n], AF.Ln, r=[trk], w=[trk], bias=1.0)
                            AC(ti[:, 0:n], ti[:, 0:n], AF.Ln, r=[tik], w=[tik], bias=1.0)
                            yield
                            AC(tr[:, 0:n], tr[:, 0:n], AF.Exp, r=[trk], w=[trk], scale=-1.0)
                            AC(ti[:, 0:n], ti[:, 0:n], AF.Exp, r=[tik], w=[tik], scale=-1.0)
                            yield
                            AC(tr[:, 0:n], tr[:, 0:n], AF.Exp, r=[trk, "pvc"], w=[trk],
                               scale=pv[:, PV_HC + ch:PV_HC + ch + 1])
                            DV("tensor_tensor", ti[:, 0:n], ti[:, 0:n], xc[:, 0:n], ALU.mult, r=[tik] + xck_all, w=[tik])
                            yield
                            b2, b2k = rB2.next()
                            AC(b2[:, 0:n], tr[:, 0:n], AF.Square, r=[trk], w=[b2k])
                            if b < 6:
                                DV("tensor_tensor", ti[:, 0:n], ti[:, 0:n], tt[:, 2, 0:n], ALU.mult, r=[tik, tk + (2,)], w=[tik])
                            yield
                            AC(b2[:, 0:n], b2[:, 0:n], AF.Ln, r=[b2k], w=[b2k], scale=-1.0, bias=1.0)
                            yield
                            AC(b2[:, 0:n], b2[:, 0:n], AF.Exp, r=[b2k], w=[b2k], scale=0.5)
                            yield
                            DV("tensor_tensor", b2[:, 0:n], b2[:, 0:n], ti[:, 0:n], ALU.mult, r=[b2k, tik], w=[b2k])
                            yield
                            hb, hbk = rHB.next()
                            for (ci, s0, sn) in segs:
                                DV("tensor_tensor_scan", hb[:, s0:s0 + sn], tr[:, s0:s0 + sn], b2[:, s0:s0 + sn],
                                   hcar[:, ch, ci:ci + 1], ALU.mult, ALU.add, r=[trk, b2k, "hcar"], w=[hbk + (ci,)])
                                yield
                                DV("tensor_copy", hcar[:, ch, ci:ci + 1], hb[:, s0 + sn - 1:s0 + sn], r=[hbk + (ci,)], w=["hcar"])
                            if oc is not None:
                                pg, pgk = psB.next()
                                proj(pg, pgk, 1, ch, xt, xk, n)
                                gs = xc
                                AC(gs[:, 0:n], pg[:, 0:n], AF.Copy, r=[pgk], w=[xck])
                                yield
                                g2, g2k = rG2.next()
                                AC(g2[:, 0:n], gs[:, 0:n], AF.Square, r=[xck], w=[g2k])
                                yield
                                DV("tensor_scalar", g2[:, 0:n], g2[:, 0:n], 0.044715, 1.0, ALU.mult, ALU.add, r=[g2k], w=[g2k])
                                yield
                                DV("tensor_tensor", g2[:, 0:n], g2[:, 0:n], gs[:, 0:n], ALU.mult, r=[g2k, xck], w=[g2k])
                                yield
                                AC(g2[:, 0:n], g2[:, 0:n], AF.Exp, r=[g2k], w=[g2k], scale=-1.5957691216057308)
                                yield
                                DV("tensor_scalar", g2[:, 0:n], g2[:, 0:n], 1.0, None, ALU.add, r=[g2k], w=[g2k])
                                yield
                                DV("reciprocal", g2[:, 0:n], g2[:, 0:n], r=[g2k], w=[g2k])
                                yield
                                DV("tensor_tensor", g2[:, 0:n], g2[:, 0:n], gs[:, 0:n], ALU.mult, r=[g2k, xck], w=[g2k])
                                yield
                                DV("tensor_tensor", mixT[:, 8 + ch, oc:oc + n], g2[:, 0:n], hb[:, 0:n], ALU.mult,
                                   r=[g2k] + [hbk + (ci,) for (ci, _, _) in segs], w=[("mix", 8 + ch, b)])

                        IL = 3
                        for (b, c0, n) in blocks(own_only=False):
                            xt, xk, tt, tk = load_x(b, c0, n, fast=False)
                            oc = own_col(b)
                            for g0 in range(0, 8, IL):
                                gens = [lru_chain(b, n, ch, xt, xk, tt, tk, oc) for ch in range(g0, min(8, g0 + IL))]
                                while gens:
                                    for g in list(gens):
                                        try:
                                            next(g)
                                        except StopIteration:
                                            gens.remove(g)
                        DMA("sp", hout[:, :, :], hcar[:], r=["hcar"])
                        DMA("sp", cout[:, :, :, :], xcar[:], r=["xcar"])
                        CHK("p2b")

                        P.barrier()

                qT = None
                run_passes("A")
                qT = sb("qT", [128, 8, T2], BF16, scope=es1)
                run_passes("B")

                with ExitStack() as esa:
                    psf = [esa.enter_context(nc.psum_tensor("pa%d" % i, [128, 512], F32)) for i in range(8)]
                    kTh = [sb("kTh%d" % i, [128, NSLOT], BF16, scope=esa) for i in range(2)]
                    Vh = [sb("Vh%d" % i, [128, 32, 129], BF16, scope=esa) for i in range(2)]
                    ptr = Rot("PT", [sb("PT%d" % i, [128, 2, BLK], BF16, scope=esa) for i in range(3)])
                    kbias = sb("kbias", [128, 32], scope=esa)
                    ckt = [sb("ckt%d" % i, [128, 2048], BF16, scope=esa) for i in range(2)]
                    cvt = [sb("cvt%d" % i, [128, 16, 129], BF16, scope=esa) for i in range(2)]
                    ep = Rot("ep", [sb("ep%d" % i, [128, 128], scope=esa) for i in range(4)])
                    epb = Rot("epb", [sb("epb%d" % i, [128, 128], scope=esa) for i in range(2)])
                    sm = Rot("sm", [sb("sm%d" % i, [128, 8], scope=esa) for i in range(4)])
                    junk = sb("junk", [128, 128], scope=esa)
                    onesb = sb("onesb", [128, 128], BF16, scope=esa)
                    epf = Rot("epf", [sb("epf%d" % i, [128, BLK], scope=esa) for i in range(7)])
                    PL("memset", onesb[:], 1.0, w=["onesb"])
                    DMA("sp", kbias[:], kbiasd[:, :], w=["kbias"])
                    for i in range(2):
                        PL("memset", cvt[i][:, :, 128:129], 1.0, w=[("cvt", i)])
                    psS = [[psf[0], psf[1]], [psf[2], psf[3]]]
                    accs = {}
                    order = [(s, c) for s in range(4) for c in range(2)]
                    for idx, sc in enumerate(order):
                        accs[sc] = (psf[4 + sc[0]], sc[1] * 129, 4 + sc[0])

                    def epilogue(npart, bank1, o1, bank2, o2, okeys, dst, dkey):
                        ps_ = slice(0, npart)
                        s_, sk = sm.next()
                        t_, tk_ = ep.next()
                        o_, ok_ = ep.next()
                        ob, obk = epb.next()
                        DV("reciprocal", s_[ps_, 0:1], bank1[ps_, o1 + 128:o1 + 129], r=okeys, w=[sk + (0,)])
                        DV("reciprocal", s_[ps_, 1:2], bank2[ps_, o2 + 128:o2 + 129], r=okeys, w=[sk + (1,)])
                        DV("tensor_tensor", s_[ps_, 2:3], s_[ps_, 1:2], lamt[ps_, 1:2], ALU.mult, r=[sk + (1,), "lam"], w=[sk + (2,)])
                        DV("tensor_scalar", t_[ps_, :], bank1[ps_, o1:o1 + 128], s_[ps_, 0:1], None, ALU.mult,
                           r=list(okeys) + [sk + (0,)], w=[tk_])
                        DV("scalar_tensor_tensor", o_[ps_, :], bank2[ps_, o2:o2 + 128], s_[ps_, 2:3], t_[ps_, :],
                           ALU.mult, ALU.add, r=list(okeys) + [sk + (2,), tk_], w=[ok_])
                        AC(junk[ps_, :], o_[ps_, :], AF.Square, r=[ok_], w=["junk", sk + (3,)], accum_out=s_[ps_, 3:4])
                        DV("tensor_scalar", s_[ps_, 4:5], s_[ps_, 3:4], 1.0 / 128.0, RMS_EPS, ALU.mult, ALU.add,
                           r=[sk + (3,)], w=[sk + (4,)])
                        AC(s_[ps_, 6:7], s_[ps_, 4:5], AF.Ln, r=[sk + (4,)], w=[sk + (6,)])
                        AC(s_[ps_, 5:6], s_[ps_, 6:7], AF.Exp, r=[sk + (6,)], w=[sk + (5,)], scale=-0.5)
                        DV("scalar_tensor_tensor", ob[ps_, :], o_[ps_, :], s_[ps_, 5:6], gsub[ps_, :], ALU.mult, ALU.mult,
                           r=[ok_, sk + (5,), "gsub"], w=[obk])
                        TR(bank1[:, 0:npart], ob[ps_, :], ident32[ps_, 0:npart], r=[obk, "ident32"] + list(okeys), w=list(okeys))
                        AC(dst, bank1[:, 0:npart], AF.Copy, r=list(okeys), w=[dkey])

                    def load_head(h):
                        DMA("sp", kTh[h % 2][:], scr_k[h, :, :], w=[("kTh", h % 2)])
                        DMA("sp", Vh[h % 2][:], scr_v[h, :, :, :], w=[("Vh", h % 2)])

                    steps = []
                    for h in range(8):
                        for sbk in range(2):
                            first_kb = 24 + 4 * sbk
                            for kb in range(first_kb + 4):
                                steps.append((h, sbk, kb, first_kb))
                    st_state = {}

                    def emit_qk(si):
                        h, sbk, kb, first_kb = steps[si]
                        kt = kTh[h % 2]
                        i = kb - first_kb
                        c0 = max(0, i) * 128
                        pss = psS[si % 2]
                        psk = [("psS", si % 2, 0), ("psS", si % 2, 1)]
                        for c in range(2):
                            MM(pss[c][:, c0:BLK], kt[64 * c:64 * c + 64, kb * 128:(kb + 1) * 128],
                               qT[64 * c:64 * c + 64, h, sbk * BLK + c0:(sbk + 1) * BLK],
                               r=[("kTh", h % 2)], w=[psk[c]], start=True, stop=True)

                    def emit_exp(si):
                        h, sbk, kb, first_kb = steps[si]
                        i = kb - first_kb
                        c0 = max(0, i) * 128
                        pss = psS[si % 2]
                        psk = [("psS", si % 2, 0), ("psS", si % 2, 1)]
                        pt, ptk = ptr.next()
                        st_state[si] = (pt, ptk)
                        if i < 0:
                            for c in range(2):
                                AC(pt[:, c, :], pss[c][:, :], AF.Exp, r=[psk[c], "kbias"], w=[ptk + (c,)],
                                   bias=kbias[:, kb:kb + 1], scale=QSCALE)
                        else:
                            for c in range(2):
                                AC(pt[0:64, c, c0:BLK], pss[c][0:64, c0:BLK], AF.Exp, r=[psk[c]], w=[ptk + (c, "a")], scale=QSCALE)
                                AC(pt[64:128, c, c0 + 64:BLK], pss[c][64:128, c0 + 64:BLK], AF.Exp, r=[psk[c]],
                                   w=[ptk + (c, "b")], scale=QSCALE)
                                AC(pt[64:128, c, c0:c0 + 64], zero_t[64:128, 0:64], AF.Copy, r=["zero"], w=[ptk + (c, "z")])

                    OT = [psf[4], psf[5]]
                    SM = [psf[6], psf[7]]

                    def emit_pv(si):
                        h, sbk, kb, first_kb = steps[si]
                        vt = Vh[h % 2]
                        vk = ("Vh", h % 2)
                        i = kb - first_kb
                        c0 = max(0, i) * 128
                        last = first_kb + 3
                        pt, ptk = st_state.pop(si)
                        for c in range(2):
                            rkeys = [ptk + (c,)] if i < 0 else [ptk + (c, "a"), ptk + (c, "b"), ptk + (c, "z")]
                            MM(OT[c][:, c0:BLK], vt[:, kb, 0:128], pt[:, c, c0:BLK], r=[vk] + rkeys, w=[("OT", c)],
                               start=(kb == 0), stop=(kb == last))
                            MM(SM[c][:, c0:BLK], onesb[:], pt[:, c, c0:BLK], r=["onesb"] + rkeys, w=[("SM", c)],
                               start=(kb == 0), stop=(kb == last))
                        if kb == last:
                            rs1, rs1k = epf.next()
                            rs2, rs2k = epf.next()
                            t_, tk_ = epf.next()
                            u_, uk_ = epf.next()
                            DV("reciprocal", rs1[:], SM[0][:, :], r=[("SM", 0)], w=[rs1k])
                            DV("reciprocal", rs2[:], SM[1][:, :], r=[("SM", 1)], w=[rs2k])
                            DV("tensor_tensor", t_[:], OT[0][:, :], rs1[:], ALU.mult, r=[("OT", 0), rs1k], w=[tk_])
                            DV("tensor_tensor", u_[:], OT[1][:, :], rs2[:], ALU.mult, r=[("OT", 1), rs2k], w=[uk_])
                            DV("scalar_tensor_tensor", t_[:], u_[:], lamt[:, 1:2], t_[:], ALU.mult, ALU.add,
                               r=[uk_, tk_, "lam"], w=[tk_])
                            AC(u_[:], t_[:], AF.Square, r=[tk_], w=[uk_])
                            MM(SM[0][:, :], ones32[:], u_[:], r=["ones32", uk_], w=[("SM", 0)], start=True, stop=True)
                            AC(rs1[:], SM[0][:, :], AF.Ln, r=[("SM", 0)], w=[rs1k], scale=1.0 / 128.0, bias=RMS_EPS)
                            AC(rs1[:], rs1[:], AF.Exp, r=[rs1k], w=[rs1k], scale=-0.5)
                            DV("scalar_tensor_tensor", mixT[:, h, sbk * BLK:(sbk + 1) * BLK], t_[:], pv[:, 152:153], rs1[:],
                               ALU.mult, ALU.mult, r=[tk_, rs1k, "pvg"], w=[("mix", h, 6 + sbk)])

                    load_head(0)
                    emit_qk(0)
                    for si in range(len(steps)):
                        h, sbk, kb, first_kb = steps[si]
                        if sbk == 0 and kb == 0 and h + 1 < 8:
                            load_head(h + 1)
                        if si + 1 < len(steps):
                            emit_qk(si + 1)
                        emit_exp(si)
                        emit_pv(si)
                    CHK("attn")

                    def load_cache(j):
                        bi, h = j // 8, j % 8
                        DMA("pool", ckt[j % 2][:], ckT[bi, h, :, :], w=[("ckt", j % 2)])
                        DMA("pool", cvt[j % 2][:, :, 0:128], cvd[bi, h, :, :, :], r=[("cvt", j % 2)], w=[("cvt", j % 2)])

                    ssteps = [(j, kb) for j in range(16) for kb in range(17)]
                    bankO = psf[4]
                    sst = {}

                    def s_qk(si):
                        j, kb = ssteps[si]
                        bi, h = j // 8, j % 8
                        q0 = TOWN + bi * 16
                        ck = ckt[j % 2]
                        pss = psS[si % 2]
                        np_ = 128 if kb < 16 else 16
                        for c in range(2):
                            if kb < 16:
                                lhs = ck[64 * c:64 * c + 64, kb * 128:(kb + 1) * 128]
                                rk_ = [("ckt", j % 2)]
                            else:
                                lhs = ksmp[64 * c:64 * c + 64, h, bi * 16:(bi + 1) * 16]
                                rk_ = []
                            MM(pss[c][0:np_, 0:16], lhs, qT[64 * c:64 * c + 64, h, q0:q0 + 16],
                               r=rk_, w=[("psS", si % 2, c)], start=True, stop=True)

                    def s_exp(si):
                        j, kb = ssteps[si]
                        pss = psS[si % 2]
                        np_ = 128 if kb < 16 else 16
                        pt, ptk = ptr.next()
                        sst[si] = (pt, ptk)
                        for c in range(2):
                            AC(pt[0:np_, c, 0:16], pss[c][0:np_, 0:16], AF.Exp, r=[("psS", si % 2, c)], w=[ptk + (c,)], scale=QSCALE)

                    def s_pv(si):
                        j, kb = ssteps[si]
                        bi, h = j // 8, j % 8
                        q0 = TOWN + bi * 16
                        cv = cvt[j % 2]
                        np_ = 128 if kb < 16 else 16
                        pt, ptk = sst.pop(si)
                        for c in range(2):
                            if kb < 16:
                                rhs = cv[:, kb, :]
                                rk_ = [("cvt", j % 2)]
                            else:
                                rhs = vsmp[0:16, bi, h, :]
                                rk_ = []
                            MM(bankO[0:16, c * 129:(c + 1) * 129], pt[0:np_, c, 0:16], rhs,
                               r=rk_ + [ptk + (c,)], w=[("psO", 0, c)],
                               start=(kb == 0 and c == 0), stop=(kb == 16), skip_group_check=True)
                        if kb == 16:
                            epilogue(16, bankO, 0, bankO, 129, [("psO", 0, 0), ("psO", 0, 1)],
                                     mixT[:, h, q0:q0 + 16], ("mix", h, 8, bi))

                    load_cache(0)
                    s_qk(0)
                    for si in range(len(ssteps)):
                        j, kb = ssteps[si]
                        if kb == 0 and j + 1 < 16:
                            load_cache(j + 1)
                        if si + 1 < len(ssteps):
                            s_qk(si + 1)
                        s_exp(si)
                        s_pv(si)
                    CHK("sattn")
                    P.barrier()

            with ExitStack() as es2:
                psf = [es2.enter_context(nc.psum_tensor("pm%d" % i, [128, 512], F32)) for i in range(6)]
                R = sb("R", [128, NKC, T2], scope=es2)
                XB = sb("XB", [128, NKC, T2], BF16, scope=es2)
                wtr = Rot("WT", [sb("WT%d" % i, [128, NKC, 512], BF16, scope=es2) for i in range(2)])
                sqr = Rot("sq", [sb("sq%d" % i, [128, TB], scope=es2) for i in range(4)])
                relr = Rot("rel", [sb("rel%d" % i, [128, TB], scope=es2) for i in range(4)])
                mean_t = sb("mean_t", [128, TB], scope=es2)
                rstd_t = sb("rstd_t", [128, TB], scope=es2)
                msq_t = sb("msq_t", [128, TB], scope=es2)
                lnr = Rot("lnt", [sb("lnt%d" % i, [128, TB], scope=es2) for i in range(4)])
                psR = Rot("psR", [psf[0], psf[1], psf[2], psf[3]])
                Hq = mixT

                for j in range(4):
                    DMA("sp", R[:, 4 * j:4 * j + 4, :], xown[512 * j:512 * (j + 1), :].rearrange("(c p) t -> p c t", p=128),
                        w=[("R", m) for m in range(4 * j, 4 * j + 4)])

                def load_wt(src, r0, c0):
                    wt, wtk = wtr.next()
                    DMA("pool", wt[:], src[r0:r0 + 2048, c0:c0 + 512].rearrange("(c p) n -> p c n", p=128), w=[wtk])
                    return wt, wtk

                def mm_group(wt, wtk, mp, act, akey, tb):
                    ps, psk = psR.next()
                    for kc in range(NKC):
                        MM(ps[:, 0:TB], wt[:, kc, mp * 128:(mp + 1) * 128], act[:, kc, tb * TB:(tb + 1) * TB],
                           r=[wtk, (akey, kc)], w=[psk], start=(kc == 0), stop=(kc == NKC - 1))
                    return ps, psk

                def layer_norm(gcol, bcol, write_bf):
                    for tb in range(NTB):
                        cs = slice(tb * TB, (tb + 1) * TB)
                        p1, p1k = psf[4], ("psLN", 0)
                        p2, p2k = psf[5], ("psLN", 1)
                        for m in range(NKC):
                            sq, sqk = sqr.next()
                            AC(sq[:], R[:, m, cs], AF.Square, r=[("R", m)], w=[sqk])
                            MM(p1[:, 0:TB], ones32[:], R[:, m, cs], r=["ones32", ("R", m)], w=[p1k],
                               start=(m == 0), stop=(m == NKC - 1))
                            MM(p2[:, 0:TB], ones32[:], sq[:], r=["ones32", sqk], w=[p2k],
                               start=(m == 0), stop=(m == NKC - 1))
                        DV("tensor_scalar", mean_t[:], p1[:, 0:TB], 1.0 / D, None, ALU.mult, r=[p1k], w=["mean"])
                        DV("tensor_tensor", msq_t[:], mean_t[:], mean_t[:], ALU.mult, r=["mean"], w=["msq"])
                        DV("scalar_tensor_tensor", rstd_t[:], p2[:, 0:TB], 1.0 / D, msq_t[:], ALU.mult, ALU.subtract,
                           r=[p2k, "msq"], w=["rstd"])
                        DV("tensor_scalar", rstd_t[:], rstd_t[:], LN_EPS, None, ALU.add, r=["rstd"], w=["rstd"])
                        AC(rstd_t[:], rstd_t[:], AF.Ln, r=["rstd"], w=["rstd"])
                        AC(rstd_t[:], rstd_t[:], AF.Exp, r=["rstd"], w=["rstd"], scale=-0.5)
                        for m in range(NKC):
                            t_, tk_ = lnr.next()
                            DV("tensor_tensor", t_[:], R[:, m, cs], mean_t[:], ALU.subtract, r=[("R", m), "mean"], w=[tk_])
                            DV("tensor_tensor", t_[:], t_[:], rstd_t[:], ALU.mult, r=[tk_, "rstd"], w=[tk_])
                            DV("tensor_scalar", R[:, m, cs], t_[:], pv[:, gcol + m:gcol + m + 1], pv[:, bcol + m:bcol + m + 1],
                               ALU.mult, ALU.add, r=[tk_, "pv"], w=[("R", m)])
                            if write_bf:
                                AC(XB[:, m, cs], R[:, m, cs], AF.Copy, r=[("R", m)], w=[("XB", m)])

                for j in range(4):
                    wt, wtk = load_wt(w_out, 0, 512 * j)
                    for mp in range(4):
                        m = 4 * j + mp
                        for tb in range(NTB):
                            ps, psk = mm_group(wt, wtk, mp, mixT, "H", tb)
                            cs = slice(tb * TB, (tb + 1) * TB)
                            DV("scalar_tensor_tensor", R[:, m, cs], R[:, m, cs], ALPHA, ps[:, 0:TB], ALU.mult, ALU.add,
                               r=[("R", m), psk], w=[("R", m)])
                layer_norm(PV_G1, PV_B1, True)

                for Q in range(4):
                    for j in range(4):
                        wt, wtk = load_wt(w_up, 0, Q * 2048 + 512 * j)
                        for mp in range(4):
                            fc = 4 * j + mp
                            for tb in range(NTB):
                                ps, psk = mm_group(wt, wtk, mp, XB, "XB", tb)
                                cs = slice(tb * TB, (tb + 1) * TB)
                                rl, rlk = relr.next()
                                AC(rl[:], ps[:, 0:TB], AF.Relu, r=[psk], w=[rlk])
                                DV("tensor_tensor", Hq[:, fc, cs], rl[:], rl[:], ALU.mult, r=[rlk], w=[("H", fc)])
                    for j in range(4):
                        wt, wtk = load_wt(w_down, Q * 2048, 512 * j)
                        for mp in range(4):
                            m = 4 * j + mp
                            for tb in range(NTB):
                                ps, psk = mm_group(wt, wtk, mp, Hq, "H", tb)
                                cs = slice(tb * TB, (tb + 1) * TB)
                                if Q == 0:
                                    DV("scalar_tensor_tensor", R[:, m, cs], R[:, m, cs], ALPHA, ps[:, 0:TB], ALU.mult, ALU.add,
                                       r=[("R", m), psk], w=[("R", m)])
                                else:
                                    DV("tensor_tensor", R[:, m, cs], R[:, m, cs], ps[:, 0:TB], ALU.add,
                                       r=[("R", m), psk], w=[("R", m)])
                layer_norm(PV_G2, PV_B2, False)
                for j in range(4):
                    DMA("sp", yT[512 * j:512 * (j + 1), :].rearrange("(c p) t -> p c t", p=128), R[:, 4 * j:4 * j + 4, :],
                        r=[("R", m) for m in range(4 * j, 4 * j + 4)])
                P.barrier()

        except _Stop:
            pass

        P.emit()
    return nc


def _perm_matrix():
    m = np.arange(128)
    d = m % 64
    partner = np.where(d < 32, m + 32, m - 32)
    P = np.zeros((128, 128), np.float32)
    P[partner, m] = 1.0
    return P


def _rope_tables(pos):
    inv = (np.float32(10000.0) ** (-(np.arange(0, 64, 2, dtype=np.float32) / np.float32(64)))).astype(np.float32)
    ang = pos.astype(np.float32)[:, None] * inv[None, :]
    cos = np.cos(ang).astype(np.float32)
    sin = np.sin(ang).astype(np.float32)
    d = np.arange(128) % 64
    f = d % 32
    sgn = np.where(d < 32, -1.0, 1.0).astype(np.float32)
    cosT = np.ascontiguousarray(cos[:, f].T)
    sinT = np.ascontiguousarray((sin[:, f] * sgn[None, :]).T)
    return cosT, sinT


_NC_CACHE = {}
import os as _os
_DBG = _os.environ.get('KDBG', '')


def _prep(x_prompt, x_sample, cache_k, cache_v, state_h, state_conv,
           w_in, lambda_q1, lambda_k1, lambda_q2, lambda_k2, subln_g,
           conv_w, conv_b, w_rg_a, b_rg_a, w_rg_i, b_rg_i, lru_lambda,
           w_out, ln1_g, ln1_b, w_up, w_down, ln2_g, ln2_b):
    f32 = np.float32
    x_prompt = np.asarray(x_prompt, f32); x_sample = np.asarray(x_sample, f32)
    cache_k = np.asarray(cache_k, f32); cache_v = np.asarray(cache_v, f32)
    w_in0 = np.asarray(w_in, f32)[0]
    wq_ = w_in0[:, 0:1024]; wk_ = w_in0[:, 1024:2048]; wv_ = w_in0[:, 2048:3072]
    wxb_ = w_in0[:, 3072:4096]; wg_ = w_in0[:, 4096:5120]
    col = np.arange(1024)
    dd = col % 64
    partner = np.where(dd < 32, col + 32, col - 32)
    shared = {
        "wq": np.ascontiguousarray(wq_), "wqp": np.ascontiguousarray(wq_[:, partner]),
        "wk": np.ascontiguousarray(wk_), "wkp": np.ascontiguousarray(wk_[:, partner]),
        "wv": np.ascontiguousarray(wv_), "wxb": np.ascontiguousarray(wxb_), "wg": np.ascontiguousarray(wg_),
        "w_out": np.ascontiguousarray(np.asarray(w_out, f32)[0]),
        "w_up": np.ascontiguousarray(np.asarray(w_up, f32)[0]),
        "w_down": np.ascontiguousarray(np.asarray(w_down, f32)[0]),
        "identd": np.eye(128, dtype=f32),
        "permd": _perm_matrix(),
        "subgd": np.ascontiguousarray(np.broadcast_to(np.asarray(subln_g, f32)[0][None, :], (128, 128))),
    }
    lam4 = np.stack([np.asarray(v, f32)[0] for v in (lambda_q1, lambda_k1, lambda_q2, lambda_k2)], 0)
    shared["lamd"] = np.ascontiguousarray(np.broadcast_to(lam4[None], (128, 4, 64)))
    wa = np.asarray(w_rg_a, f32)[0]; wi = np.asarray(w_rg_i, f32)[0]
    wabd = np.zeros((128, 8, 128), f32); wibd = np.zeros((128, 8, 128), f32)
    for ch in range(8):
        for t in range(2):
            wabd[64 * t:64 * t + 64, ch, 64 * t:64 * t + 64] = wa[2 * ch + t]
            wibd[64 * t:64 * t + 64, ch, 64 * t:64 * t + 64] = wi[2 * ch + t]
    shared["wabd"] = wabd; shared["wibd"] = wibd
    pvd = np.zeros((128, 160), f32)

    def pc(v, n):
        return np.asarray(v, f32).reshape(n, 128).T

    cw = np.asarray(conv_w, f32)[0]
    for ch in range(8):
        for j in range(4):
            pvd[:, ch * 4 + j] = cw[j, ch * 128:(ch + 1) * 128]
    pvd[:, 32:40] = pc(np.asarray(conv_b)[0], 8)
    pvd[:, 40:48] = pc(np.asarray(b_rg_a)[0], 8)
    pvd[:, 48:56] = pc(np.asarray(b_rg_i)[0], 8)
    pvd[:, 56:64] = pc(np.asarray(lru_lambda)[0], 8)
    pvd[:, 64:80] = pc(np.asarray(ln1_g)[0], 16)
    pvd[:, 80:96] = pc(np.asarray(ln1_b)[0], 16)
    pvd[:, 96:112] = pc(np.asarray(ln2_g)[0], 16)
    pvd[:, 112:128] = pc(np.asarray(ln2_b)[0], 16)
    pvd[:, 152] = np.asarray(subln_g, f32)[0]
    shared["pvd"] = pvd

    in_maps = []
    for c in range(8):
        b, qi = c // 4, c % 4
        shift = 3072 - 1024 * qi
        m = dict(shared)
        xr = np.zeros((D, NSLOT), f32)
        xr[:, shift:] = x_prompt[b, :1024 * (qi + 1), :].T
        m["xrot"] = xr
        xo = np.empty((D, T2), f32)
        xo[:, :TOWN] = x_prompt[b, 1024 * qi:1024 * (qi + 1), :].T
        xo[:, TOWN:] = x_sample[2 * c:2 * c + 2].reshape(NSMP, D).T
        m["xown"] = xo
        pos = np.concatenate([np.maximum(np.arange(NSLOT) - shift, 0), 2048 + np.arange(16), 2048 + np.arange(16)])
        cosT, sinT = _rope_tables(pos)
        m["cosd"] = cosT; m["sind"] = sinT
        valid = np.concatenate([(np.arange(NSLOT) >= shift).astype(f32), np.ones(NSMP, f32)])
        m["vald"] = np.ascontiguousarray(np.broadcast_to(valid[None, :], (128, NSLOT + NSMP)))
        kb_valid = (np.arange(32) * 128 >= shift)
        m["kbiasd"] = np.ascontiguousarray(np.broadcast_to(np.where(kb_valid, 0.0, NEG).astype(f32)[None, :], (128, 32)))
        ck = cache_k[0, 2 * c:2 * c + 2]
        m["ckT"] = np.ascontiguousarray(ck.transpose(0, 2, 3, 1))
        cv = cache_v[0, 2 * c:2 * c + 2].reshape(2, 16, 128, 8, 128)
        m["cvd"] = np.ascontiguousarray(cv.transpose(0, 3, 2, 1, 4))
        sh = np.asarray(state_h, f32)[0, 2 * c:2 * c + 2]
        m["shd"] = np.ascontiguousarray(sh.reshape(2, 8, 128).transpose(2, 1, 0))
        sc = np.asarray(state_conv, f32)[0, 2 * c:2 * c + 2]
        m["scd"] = np.ascontiguousarray(sc.reshape(2, 3, 8, 128).transpose(3, 2, 0, 1))
        in_maps.append(m)

    return in_maps


def _assemble(R):
    f32 = np.float32
    y_prompt = np.empty((2, 4096, D), f32); y_sample = np.empty((16, 16, D), f32)
    k_prompt = np.empty((1, 2, 4096, 8, 128), f32); v_prompt = np.empty((1, 2, 4096, 8, 128), f32)
    h_prompt = np.empty((1, 2, 1024), f32); conv_prompt = np.empty((1, 2, 3, 1024), f32)
    k_sample = np.empty((1, 16, 16, 8, 128), f32); v_sample = np.empty((1, 16, 16, 8, 128), f32)
    h_sample = np.empty((1, 16, 1024), f32); conv_sample = np.empty((1, 16, 3, 1024), f32)
    for c in range(8):
        b, qi = c // 4, c % 4
        r = R[c]
        yT = np.asarray(r["yT"], f32)
        y_prompt[b, 1024 * qi:1024 * (qi + 1)] = yT[:, :TOWN].T
        y_sample[2 * c:2 * c + 2] = yT[:, TOWN:].T.reshape(2, 16, D)
        kT = np.asarray(r["koutT"], f32); vT = np.asarray(r["voutT"], f32)
        k_prompt[0, b, 1024 * qi:1024 * (qi + 1)] = kT[:, :, :TOWN].transpose(2, 0, 1)
        v_prompt[0, b, 1024 * qi:1024 * (qi + 1)] = vT[:, :, :TOWN].transpose(2, 0, 1)
        k_sample[0, 2 * c:2 * c + 2] = kT[:, :, TOWN:].transpose(2, 0, 1).reshape(2, 16, 8, 128)
        v_sample[0, 2 * c:2 * c + 2] = vT[:, :, TOWN:].transpose(2, 0, 1).reshape(2, 16, 8, 128)
        ho = np.asarray(r["hout"], f32)
        co = np.asarray(r["cout"], f32)
        if qi == 3:
            h_prompt[0, b] = ho[:, :, 0].T.reshape(1024)
            conv_prompt[0, b] = co[:, :, 0, :].transpose(2, 1, 0).reshape(3, 1024)
        for bi in range(2):
            h_sample[0, 2 * c + bi] = ho[:, :, 1 + bi].T.reshape(1024)
            conv_sample[0, 2 * c + bi] = co[:, :, 1 + bi, :].transpose(2, 1, 0).reshape(3, 1024)
    return (y_prompt, y_sample, k_prompt, v_prompt, h_prompt, conv_prompt,
            k_sample, v_sample, h_sample, conv_sample)


def kernel(x_prompt, x_sample, cache_k, cache_v, state_h, state_conv,
           w_in, lambda_q1, lambda_k1, lambda_q2, lambda_k2, subln_g,
           conv_w, conv_b, w_rg_a, b_rg_a, w_rg_i, b_rg_i, lru_lambda,
           w_out, ln1_g, ln1_b, w_up, w_down, ln2_g, ln2_b):
    in_maps = _prep(x_prompt, x_sample, cache_k, cache_v, state_h, state_conv,
                    w_in, lambda_q1, lambda_k1, lambda_q2, lambda_k2, subln_g,
                    conv_w, conv_b, w_rg_a, b_rg_a, w_rg_i, b_rg_i, lru_lambda,
                    w_out, ln1_g, ln1_b, w_up, w_down, ln2_g, ln2_b)
    if "nc" not in _NC_CACHE:
        _NC_CACHE["nc"] = build_nc()
    nc = _NC_CACHE["nc"]
    res = run_bass_kernel_spmd(nc, in_maps, core_ids=list(range(8)))
    return _assemble(res.results)
```

```python
import numpy as np
from contextlib import ExitStack
import concourse.bass as bass
import concourse.mybir as mybir
from concourse.bass_utils import run_bass_kernel_spmd

F32 = mybir.dt.float32
BF16 = mybir.dt.bfloat16
ALU = mybir.AluOpType
AF = mybir.ActivationFunctionType

D = 2048
NKC = 16
NSLOT = 4096
BLK = 512
NBLK = 8
OWN0 = 3072
TOWN = 1024
NSMP = 32
T2 = TOWN + NSMP
TB = 352
NTB = 3
LAM_INIT = 0.8 - 0.6 * 1.0
ALPHA = 2.0 ** 0.25
LN_EPS = 1e-5
RMS_EPS = 1e-5
QSCALE = 64 ** -0.5
NEG = -30000.0


class Op:
    __slots__ = ("eng", "fn", "dma", "deps", "signal", "semi", "count")


class Prog:
    CE = ("pe", "act", "dve", "pool")
    DQ = ("sp", "pool", "act")

    def __init__(self, nc, es, ndma=6):
        self.nc = nc
        self.ops = []
        self.lw = {}
        self.rd = {}
        self.sems = []
        self.esem = {}
        for e in self.CE:
            self.esem[e] = len(self.sems)
            self.sems.append(es.enter_context(nc.semaphore("sem_" + e)))
        self.ndma = ndma
        self.dsem = {}
        self.dcnt = {}
        self.dlast = {}
        self.dnext = {}
        for q in self.DQ:
            self.dsem[q] = []
            for i in range(ndma):
                self.dsem[q].append(len(self.sems))
                self.sems.append(es.enter_context(nc.semaphore("dq_%s_%d" % (q, i))))
            self.dcnt[q] = [0] * ndma
            self.dlast[q] = [None] * ndma
            self.dnext[q] = 0
        self.last_of = {e: None for e in self.CE}
        self.stopped = False

    def add(self, eng, meth, args=(), kw=None, reads=(), writes=(), dma=False):
        op = Op()
        op.eng = eng
        op.fn = (meth, tuple(args), dict(kw or {}))
        op.dma = dma
        op.signal = False
        op.semi = None
        op.count = 0
        op.deps = []
        if self.stopped:
            return op
        def _expand(keys):
            out = []
            for k in keys:
                out.append(k)
                if isinstance(k, tuple) and len(k) > 2:
                    out.append(k[:2])
            return out
        reads = _expand(reads)
        writes = _expand(writes)
        banks = set()
        for a in list(args) + list((kw or {}).values()):
            t = getattr(a, "tensor", None)
            if t is not None and type(t).__name__ == "PSumTensorHandle":
                banks.add(("BANK", t.name))
        if banks:
            writes = list(writes) + sorted(banks)
        raw = []
        other = []
        for k in reads:
            w = self.lw.get(k)
            if w is not None:
                raw.append(w)
        for k in writes:
            w = self.lw.get(k)
            if w is not None:
                other.append(w)
            r = self.rd.get(k)
            if r:
                other.extend(r[0].values())
                other.extend(r[1])
        deps = []
        seen = set()
        for lst, is_raw in ((raw, True), (other, False)):
            for d in lst:
                if id(d) in seen:
                    continue
                if (not dma) and (not d.dma) and d.eng == eng:
                    if eng == "pe":
                        continue
                seen.add(id(d))
                deps.append(d)
        if dma:
            i = self.dnext[eng]
            self.dnext[eng] = (i + 1) % self.ndma
            prev = self.dlast[eng][i]
            if prev is not None and id(prev) not in seen:
                deps.append(prev)
            self.dcnt[eng][i] += 16
            op.semi = self.dsem[eng][i]
            op.count = self.dcnt[eng][i]
            self.dlast[eng][i] = op
        for d in deps:
            d.signal = True
        op.deps = deps
        for k in writes:
            self.lw[k] = op
            self.rd[k] = [{}, []]
        for k in reads:
            if k in writes:
                continue
            r = self.rd.setdefault(k, [{}, []])
            if dma:
                r[1].append(op)
            else:
                r[0][eng] = op
        if not dma:
            self.last_of[eng] = op
        self.ops.append(op)
        return op

    def barrier(self):
        if self.stopped:
            return
        deps = [o for o in self.last_of.values() if o is not None]
        for q in self.DQ:
            deps.extend(o for o in self.dlast[q] if o is not None)
        for d in deps:
            d.signal = True
        for e in ("pe", "act", "dve", "pool", "sp"):
            op = Op()
            op.eng = e
            op.fn = None
            op.dma = False
            op.signal = False
            op.semi = None
            op.count = 0
            op.deps = list(deps)
            self.ops.append(op)
        self.lw = {}
        self.rd = {}

    def emit(self):
        nc = self.nc
        cnt = {e: 0 for e in self.CE}
        for op in self.ops:
            if (not op.dma) and op.signal and op.fn is not None:
                cnt[op.eng] += 1
                op.semi = self.esem[op.eng]
                op.count = cnt[op.eng]
        byeng = {e: [] for e in ("pe", "act", "dve", "pool", "sp")}
        for op in self.ops:
            byeng[op.eng].append(op)
        sems = self.sems

        def run(name, h):
            waited = {}
            for op in byeng[name]:
                for d in op.deps:
                    if d.semi is None:
                        continue
                    if waited.get(d.semi, 0) >= d.count:
                        continue
                    h.wait_ge(sems[d.semi], d.count)
                    waited[d.semi] = d.count
                if op.fn is None:
                    continue
                meth, args, kw = op.fn
                ins = getattr(h, meth)(*args, **kw)
                if op.dma:
                    ins.then_inc(sems[op.semi], 16)
                elif op.signal:
                    ins.then_inc(sems[op.semi], 1)
            if name == "sp":
                for q in self.DQ:
                    for i in range(self.ndma):
                        c = self.dcnt[q][i]
                        if c > 0 and waited.get(self.dsem[q][i], 0) < c:
                            h.wait_ge(sems[self.dsem[q][i]], c)

        with nc.Block() as block:
            @block.tensor
            def _(h):
                run("pe", h)

            @block.scalar
            def _(h):
                run("act", h)

            @block.vector
            def _(h):
                run("dve", h)

            @block.gpsimd
            def _(h):
                run("pool", h)

            @block.sync
            def _(h):
                run("sp", h)


class _CompView:
    def __init__(self, t, c):
        self.t = t
        self.c = c

    def __getitem__(self, idx):
        rows, cols = idx
        return self.t[rows, self.c, cols]


class Rot:
    def __init__(self, name, tiles, keys=None):
        self.name = name
        self.tiles = tiles
        self.keys = keys if keys is not None else [(name, j) for j in range(len(tiles))]
        self.i = 0

    def next(self):
        j = self.i % len(self.tiles)
        self.i += 1
        return self.tiles[j], self.keys[j]


_PROG = [None]


class _Stop(Exception):
    pass


def build_nc(stop=None):
    nc = bass.Bass("TRN2", target_bir_lowering=False)

    def din(name, shape, dt=F32):
        return nc.dram_tensor(name, list(shape), dt, kind="ExternalInput").ap()

    def dout(name, shape, dt=F32):
        return nc.dram_tensor(name, list(shape), dt, kind="ExternalOutput").ap()

    xrot = din("xrot", [D, NSLOT])
    xown = din("xown", [D, T2])
    wq = din("wq", [D, 1024]); wqp = din("wqp", [D, 1024])
    wk = din("wk", [D, 1024]); wkp = din("wkp", [D, 1024])
    wv = din("wv", [D, 1024]); wxb = din("wxb", [D, 1024]); wg = din("wg", [D, 1024])
    w_out = din("w_out", [D, D]); w_up = din("w_up", [D, 4 * D]); w_down = din("w_down", [4 * D, D])
    cosd = din("cosd", [128, NSLOT + NSMP]); sind = din("sind", [128, NSLOT + NSMP])
    vald = din("vald", [128, NSLOT + NSMP])
    kbiasd = din("kbiasd", [128, 32])
    wabd = din("wabd", [128, 8, 128]); wibd = din("wibd", [128, 8, 128])
    pvd = din("pvd", [128, 160])
    lamd = din("lamd", [128, 4, 64])
    subgd = din("subgd", [128, 128])
    identd = din("identd", [128, 128])
    permd = din("permd", [128, 128])
    ckT = din("ckT", [2, 8, 128, 2048])
    cvd = din("cvd", [2, 8, 128, 16, 128])
    shd = din("shd", [128, 8, 2])
    scd = din("scd", [128, 8, 2, 3])

    yT = dout("yT", [D, T2])
    koutT = dout("koutT", [8, 128, T2])
    voutT = dout("voutT", [8, 128, T2])
    hout = dout("hout", [128, 8, 3])
    cout = dout("cout", [128, 8, 3, 3])

    scr_k = nc.dram_tensor("scr_k", [8, 128, NSLOT], BF16).ap()
    scr_v = nc.dram_tensor("scr_v", [8, 128, 32, 129], BF16).ap()

    PV_CW = 0; PV_CB = 32; PV_BA = 40; PV_BI = 48; PV_LL = 56
    PV_G1 = 64; PV_B1 = 80; PV_G2 = 96; PV_B2 = 112
    PV_HBA = 128; PV_HBI = 136; PV_HC = 144
    AX = mybir.AxisListType.X

    with ExitStack() as es:
        P = Prog(nc, es)
        _PROG[0] = P

        def CHK(name):
            if stop == name:
                P.barrier()
                P.stopped = True

        def sb(name, shape, dt=F32, scope=es):
            return scope.enter_context(nc.sbuf_tensor(name, list(shape), dt))

        def OP(eng, meth, *args, r=(), w=(), **kw):
            return P.add(eng, meth, args, kw, list(r), list(w), False)

        def DV(meth, *args, r=(), w=(), **kw):
            return P.add("dve", meth, args, kw, list(r), list(w), False)

        def AC(out, in_, func, r=(), w=(), **kw):
            return P.add("act", "activation", (), dict(out=out, in_=in_, func=func, **kw), list(r), list(w), False)

        def PL(meth, *args, r=(), w=(), **kw):
            return P.add("pool", meth, args, kw, list(r), list(w), False)

        def MM(out, lhsT, rhs, r=(), w=(), **kw):
            return P.add("pe", "matmul", (out, lhsT, rhs), kw, list(r), list(w), False)

        def TR(out, in_, idn, r=(), w=()):
            return P.add("pe", "transpose", (out, in_, idn), {}, list(r), list(w), False)

        def DMA(q, out, in_, r=(), w=()):
            return P.add(q, "dma_start", (), dict(out=out, in_=in_), list(r), list(w), True)

        try:
            ident32 = sb("ident32", [128, 128])
            permT = sb("permT", [128, 128])
            mixT = sb("mixT", [128, 16, T2], BF16)
            pv = sb("pv", [128, 160])
            ident = sb("ident", [128, 128], BF16)
            ones32 = sb("ones32", [128, 128])
            lamt = sb("lamt", [128, 4])
            gsub = sb("gsub", [128, 128])
            zero_t = sb("zero_t", [128, 64])

            DMA("sp", pv[:, :], pvd[:, :], w=["pv"])
            DMA("pool", ident[:], identd[:, :], w=["ident"])
            DMA("sp", ident32[:], identd[:, :], w=["ident32"])
            DMA("sp", permT[:], permd[:, :], w=["permT"])
            PL("memset", ones32[:], 1.0, w=["ones32"])
            PL("memset", zero_t[:], 0.0, w=["zero"])
            DMA("sp", gsub[:], subgd[:, :], w=["gsub"])
            with ExitStack() as es0:
                lamin = sb("lamin", [128, 4, 64], scope=es0)
                lamp = sb("lamp", [128, 2, 64], scope=es0)
                lams = sb("lams", [128, 4], scope=es0)
                sp_t = sb("sp_t", [128, 16], scope=es0)
                DMA("sp", lamin[:], lamd[:, :, :], w=["lamin"])
                DV("tensor_tensor", lamp[:, 0, :], lamin[:, 0, :], lamin[:, 1, :], ALU.mult, r=["lamin"], w=["lamp0"])
                DV("tensor_tensor", lamp[:, 1, :], lamin[:, 2, :], lamin[:, 3, :], ALU.mult, r=["lamin"], w=["lamp1"])
                DV("reduce_sum", lams[:, 0:1], lamp[:, 0, :], AX, r=["lamp0"], w=["lams0"])
                DV("reduce_sum", lams[:, 1:2], lamp[:, 1, :], AX, r=["lamp1"], w=["lams1"])
                AC(lams[:, 2:4], lams[:, 0:2], AF.Exp, r=["lams0", "lams1"], w=["lams23"])
                DV("scalar_tensor_tensor", lamt[:, 0:1], lams[:, 2:3], LAM_INIT, lams[:, 3:4], ALU.add, ALU.subtract,
                   r=["lams23"], w=["lam0"])
                DV("tensor_scalar", lamt[:, 1:2], lamt[:, 0:1], -1.0, None, ALU.mult, r=["lam0"], w=["lam"])
                DV("tensor_scalar", gsub[:], gsub[:], 1.0 - LAM_INIT, None, ALU.mult, r=["gsub"], w=["gsub"])
                DV("tensor_scalar", pv[:, 152:153], pv[:, 152:153], 1.0 - LAM_INIT, None, ALU.mult, r=["pv"], w=["pvg"])
                DV("tensor_scalar", pv[:, PV_HBA:PV_HBA + 16], pv[:, PV_BA:PV_BA + 16], -1.0, None, ALU.mult,
                   r=["pv"], w=["pvh"])
                AC(sp_t[:, 0:8], pv[:, PV_LL:PV_LL + 8], AF.Exp, r=["pv"], w=["sp0"], scale=-1.0)
                DV("tensor_scalar", sp_t[:, 0:8], sp_t[:, 0:8], 1.0, None, ALU.add, r=["sp0"], w=["sp1"])
                AC(sp_t[:, 8:16], sp_t[:, 0:8], AF.Ln, r=["sp1"], w=["sp2"])
                DV("tensor_scalar", pv[:, PV_HC:PV_HC + 8], sp_t[:, 8:16], -8.0, None, ALU.mult, r=["sp2"], w=["pvc"])
                P.barrier()

            with ExitStack() as es1:
                ksmp = sb("ksmp", [128, 8, NSMP], BF16, scope=es1)
                vsmp = sb("vsmp", [16, 2, 8, 129], BF16, scope=es1)

                def run_passes(stage):
                    with ExitStack() as esp:
                        psf = [esp.enter_context(nc.psum_tensor("ps%d%s" % (i, stage), [128, 512], F32)) for i in range(7)]
                        pst = esp.enter_context(nc.psum_tensor("pst" + stage, [128, 4, 128], BF16))
                        wbuf = sb("wbuf" + stage, [128, NKC, 2048], BF16, scope=esp)
                        xrotr = Rot("xblk", [sb("xblk%d%s" % (i, stage), [128, NKC, BLK], BF16, scope=esp) for i in range(2)])
                        tabr = Rot("tab", [sb("tab%d%s" % (i, stage), [128, 3, BLK], scope=esp) for i in range(2)])
                        wab = sb("wab" + stage, [128, 8, 128], BF16, scope=esp)
                        wib = sb("wib" + stage, [128, 8, 128], BF16, scope=esp)
                        hcar = sb("hcar" + stage, [128, 8, 3], scope=esp)
                        xcar = sb("xcar" + stage, [128, 8, 3, 3], scope=esp)
                        _ft = [sb("f32t%d%s" % (i, stage), [128, BLK], scope=esp) for i in range(18 if stage == "A" else 12)]
                        _fk = [("f32t", i) for i in range(18 if stage == "A" else 12)]
                        _nw = 8 if stage == "A" else 6
                        f32r = Rot("f32t", _ft[0:_nw], _fk[0:_nw])
                        stgr = Rot("stg", _ft[_nw:], _fk[_nw:])
                        rXC = Rot("rXC", _ft[0:3], _fk[0:3])
                        rTR = Rot("rTR", _ft[3:6], _fk[3:6])
                        rTI = Rot("rTI", _ft[6:9], _fk[6:9])
                        rB2 = Rot("rB2", _ft[9:12], _fk[9:12])
                        rHB = Rot("rHB", _ft[12:15], _fk[12:15])
                        rG2 = Rot("rG2", _ft[15:18], _fk[15:18])
                        bfr = Rot("bft", [sb("bft%d%s" % (i, stage), [128, BLK], BF16, scope=esp) for i in range(4)])
                        xbtr = Rot("xbt", [sb("xbt%d%s" % (i, stage), [128, BLK + 8], scope=esp) for i in range(3)])
                        vaugr = Rot("vaug", [sb("vaug%d%s" % (i, stage), [128, 4, 129], BF16, scope=esp) for i in range(2)])
                        psA = Rot("psA", [psf[0], psf[1]])
                        psB = Rot("psB", [psf[2], psf[3]])
                        psG = Rot("psG", [psf[4], psf[5], psf[6]])

                        if stage == "A":
                            DMA("pool", wab[:], wabd[:, :, :], w=["wab"])
                            DMA("pool", wib[:], wibd[:, :, :], w=["wib"])
                            PL("memset", hcar[:], 0.0, w=["hcar"])
                            PL("memset", xcar[:], 0.0, w=["xcar"])
                            DMA("sp", hcar[:, :, 1:3], shd[:, :, :], r=["hcar"], w=["hcar"])
                            DMA("sp", xcar[:, :, 1:3, :], scd[:, :, :, :], r=["xcar"], w=["xcar"])
                            for vi, vt_ in enumerate(vaugr.tiles):
                                PL("memset", vt_[:, :, 128:129], 1.0, w=[("vaug", vi)])
                            PL("memset", vsmp[:, :, :, 128:129], 1.0, w=["vsmp"])

                        def load_w(half, src):
                            DMA("pool", wbuf[:, :, half * 1024:(half + 1) * 1024],
                                src.rearrange("(c p) n -> p c n", p=128), w=[("W", half)])

                        def blocks(own_only):
                            lst = []
                            for b in range(NBLK):
                                if own_only and b < 6:
                                    continue
                                lst.append((b, b * BLK, BLK))
                            lst.append((8, NSLOT, NSMP))
                            return lst

                        def load_x(b, c0, n, fast=True):
                            xt, xk = xrotr.next()
                            tt, tk = tabr.next()
                            if False:
                                for j in range(NKC):
                                    stg, sk = stgr.next()
                                    DMA("sp", stg[:, 0:n], xrot[128 * j:128 * (j + 1), c0:c0 + n], w=[sk])
                                    AC(xt[:, j, 0:n], stg[:, 0:n], AF.Copy, r=[sk], w=[xk + (j,)])
                            else:
                                src = xrot[:, c0:c0 + n] if b < 8 else xown[:, TOWN:T2]
                                DMA("pool", xt[:, :, 0:n], src.rearrange("(c p) s -> p c s", p=128), w=[xk])
                            DMA("sp", tt[:, 0, 0:n], cosd[:, c0:c0 + n], w=[tk + (0,)])
                            DMA("sp", tt[:, 1, 0:n], sind[:, c0:c0 + n], w=[tk + (1,)])
                            DMA("sp", tt[:, 2, 0:n], vald[:, c0:c0 + n], w=[tk + (2,)])
                            return xt, xk, tt, tk

                        def proj(ps, psk, half, chunk, xt, xk, n):
                            for kc in range(NKC):
                                MM(ps[:, 0:n], wbuf[:, kc, half * 1024 + chunk * 128: half * 1024 + (chunk + 1) * 128],
                                   xt[:, kc, 0:n], r=[("W", half), xk], w=[psk], start=(kc == 0), stop=(kc == NKC - 1))

                        def own_col(b):
                            return {6: 0, 7: 512, 8: 1024}.get(b)

                        def rope_pass(w_a, w_b, is_q):
                            load_w(0, w_a)
                            for (b, c0, n) in blocks(own_only=is_q):
                                xt, xk, tt, tk = load_x(b, c0, n)
                                oc = own_col(b)
                                pend = None
                                for h in range(9):
                                    cur = None
                                    if h < 8:
                                        pa, pak = psA.next()
                                        proj(pa, pak, 0, h, xt, xk, n)
                                        ks, ksk = f32r.next()
                                        AC(ks[:, 0:n], pa[:, 0:n], AF.Copy, r=[pak], w=[ksk])
                                        cur = (h, ks, ksk)
                                    if pend is not None:
                                        rope_tail(pend[0], pend[1], pend[2], b, c0, n, oc, tt, tk, is_q)
                                    pend = cur

                        def rope_tail(h, ks, ksk, b, c0, n, oc, tt, tk, is_q):
                            pb, pbk = psB.next()
                            MM(pb[:, 0:n], permT[:], ks[:, 0:n], r=["permT", ksk], w=[pbk], start=True, stop=True)
                            t1, t1k = f32r.next()
                            t2, t2k = f32r.next()
                            DV("tensor_tensor", t1[:, 0:n], ks[:, 0:n], tt[:, 0, 0:n], ALU.mult, r=[ksk, tk + (0,)], w=[t1k])
                            DV("tensor_tensor", t2[:, 0:n], pb[:, 0:n], tt[:, 1, 0:n], ALU.mult, r=[pbk, tk + (1,)], w=[t2k])
                            if is_q:
                                DV("tensor_tensor", qT[:, h, oc:oc + n], t1[:, 0:n], t2[:, 0:n], ALU.add,
                                   r=[t1k, t2k], w=[("qT", h, b)])
                            else:
                                kf, kfk = f32r.next()
                                DV("tensor_tensor", kf[:, 0:n], t1[:, 0:n], t2[:, 0:n], ALU.add, r=[t1k, t2k], w=[kfk])
                                if b < 8:
                                    kb_, kbk = bfr.next()
                                    AC(kb_[:, 0:n], kf[:, 0:n], AF.Copy, r=[kfk], w=[kbk])
                                    DMA("sp", scr_k[h, :, c0:c0 + n], kb_[:, 0:n], r=[kbk], w=[("scrk", h)])
                                else:
                                    AC(ksmp[:, h, :], kf[:, 0:n], AF.Copy, r=[kfk], w=[("ksmp", h)])
                                if oc is not None:
                                    DMA("sp", koutT[h, :, oc:oc + n], kf[:, 0:n], r=[kfk])

                        if stage == "B":
                            rope_pass(wq, wqp, True)
                            CHK("p3")
                            P.barrier()
                            return
                        rope_pass(wk, wkp, False)
                        CHK("p1")

                        load_w(0, wv)
                        for (b, c0, n) in blocks(own_only=False):
                            xt, xk, tt, tk = load_x(b, c0, n)
                            oc = own_col(b)
                            for h in range(8):
                                pa, pak = psA.next()
                                proj(pa, pak, 0, h, xt, xk, n)
                                vb, vbk = bfr.next()
                                AC(vb[:, 0:n], pa[:, 0:n], AF.Copy, r=[pak], w=[vbk])
                                if oc is not None:
                                    vf, vfk = f32r.next()
                                    DV("tensor_copy", vf[:, 0:n], pa[:, 0:n], r=[pak], w=[vfk])
                                    DMA("sp", voutT[h, :, oc:oc + n], vf[:, 0:n], r=[vfk])
                                if _DBG == "notr":
                                    continue
                                if b < 8:
                                    va, vak = vaugr.next()
                                    for s4 in range(4):
                                        TR(pst[:, s4, :], vb[:, s4 * 128:(s4 + 1) * 128], ident[:], r=[vbk, "ident"], w=["pst"])
                                    DV("tensor_copy", va[:, :, 0:128], pst[:, :, :], r=["pst"], w=[vak])
                                    DMA("sp", scr_v[h, :, b * 4:(b + 1) * 4, :], va[:, :, :], r=[vak], w=[("scrv", h)])
                                else:
                                    for bi in range(2):
                                        TR(pst[0:16, bi, :], vb[:, bi * 16:(bi + 1) * 16], ident[:], r=[vbk, "ident"], w=["pst"])
                                    DV("tensor_copy", vsmp[:, :, h, 0:128], pst[0:16, 0:2, :], r=["pst"], w=["vsmp"])
                            CHK("p2a_b%d" % b)
                        CHK("p2a")

                        load_w(0, wxb)
                        load_w(1, wg)
                        def lru_chain(b, n, ch, xt, xk, tt, tk, oc):
                            pa, pak = psA.next()
                            proj(pa, pak, 0, ch, xt, xk, n)
                            segs = [(0, 0, n)] if b < 8 else [(1, 0, 16), (2, 16, 16)]
                            xbt, xbk = xbtr.next()
                            xc, xck = rXC.next()
                            cwc = PV_CW + ch * 4
                            for (ci, s0, sn) in segs:
                                off = s0 + (3 if ci == 2 else 0)
                                kh = xbk + ("h", ci)
                                kd = xbk + ("d", ci)
                                DV("tensor_copy", xbt[:, off:off + 3], xcar[:, ch, ci, :], r=["xcar"], w=[kh])
                                AC(xbt[:, off + 3:off + 3 + sn], pa[:, s0:s0 + sn], AF.Copy, r=[pak], w=[kd])
                            yield
                            for (ci, s0, sn) in segs:
                                off = s0 + (3 if ci == 2 else 0)
                                kh = xbk + ("h", ci)
                                kd = xbk + ("d", ci)
                                DV("tensor_copy", xcar[:, ch, ci, :], xbt[:, off + sn:off + sn + 3], r=[kd, kh], w=["xcar"])
                                AC(xc[:, s0:s0 + sn], xbt[:, off + 3:off + 3 + sn], AF.Identity,
                                   r=[kd, kh, "pv"], w=[xck + (ci,)],
                                   scale=pv[:, cwc + 3:cwc + 4], bias=pv[:, PV_CB + ch:PV_CB + ch + 1])
                                yield
                                for j in range(3):
                                    DV("scalar_tensor_tensor", xc[:, s0:s0 + sn], xbt[:, off + j:off + j + sn],
                                       pv[:, cwc + j:cwc + j + 1], xc[:, s0:s0 + sn], ALU.mult, ALU.add,
                                       r=[kd, kh, xck + (ci,)], w=[xck + (ci,)])
                                    yield
                            xck_all = [xck + (ci,) for (ci, _, _) in segs]
                            xcb, xcbk = bfr.next()
                            AC(xcb[:, 0:n], xc[:, 0:n], AF.Copy, r=xck_all, w=[xcbk])
                            yield
                            pr, prk = psG.next()
                            pi, pik = psG.next()
                            MM(pr[:, 0:n], wab[:, ch, :], xcb[:, 0:n], r=["wab", xcbk], w=[prk], start=True, stop=True)
                            MM(pi[:, 0:n], wib[:, ch, :], xcb[:, 0:n], r=["wib", xcbk], w=[pik], start=True, stop=True)
                            tr, trk = rTR.next()
                            ti, tik = rTI.next()
                            AC(tr[:, 0:n], pr[:, 0:n], AF.Exp, r=[prk, "pvh"], w=[trk],
                               bias=pv[:, PV_HBA + ch:PV_HBA + ch + 1], scale=-1.0)
                            AC(ti[:, 0:n], pi[:, 0:n], AF.Exp, r=[pik, "pvh"], w=[tik],
                               bias=pv[:, PV_HBI + ch:PV_HBI + ch + 1], scale=-1.0)
                            yield
                            AC(tr[:, 0:n], tr[:, 0:n], AF.Ln, r=[trk], w=[trk], bias=1.0)
                            AC(ti[:, 0:n], ti[:, 0:n], AF.Ln, r=[tik], w=[tik], bias=1.0)
                            yield
                            AC(tr[:, 0:n], tr[:, 0:n], AF.Exp, r=[trk], w=[trk], scale=-1.0)
                            AC(ti[:, 0:n], ti[:, 0:n], AF.Exp, r=[tik], w=[tik], scale=-1.0)
                            yield
                            AC(tr[:, 0:n], tr[:, 0:n], AF.Exp, r=[trk, "pvc"], w=[trk],
                               scale=pv[:, PV_HC + ch:PV_HC + ch + 1])
                            DV("tensor_tensor", ti[:, 0:n], ti[:, 0:n], xc[:, 0:n], ALU.mult, r=[tik] + xck_all, w=[tik])
                            yield
                            b2, b2k = rB2.next()
                            AC(b2[:, 0:n], tr[:, 0:n], AF.Square, r=[trk], w=[b2k])
                            if b < 6:
                                DV("tensor_tensor", ti[:, 0:n], ti[:, 0:n], tt[:, 2, 0:n], ALU.mult, r=[tik, tk + (2,)], w=[tik])
                            yield
                            AC(b2[:, 0:n], b2[:, 0:n], AF.Ln, r=[b2k], w=[b2k], scale=-1.0, bias=1.0)
                            yield
                            AC(b2[:, 0:n], b2[:, 0:n], AF.Exp, r=[b2k], w=[b2k], scale=0.5)
                            yield
                            DV("tensor_tensor", b2[:, 0:n], b2[:, 0:n], ti[:, 0:n], ALU.mult, r=[b2k, tik], w=[b2k])
                            yield
                            hb, hbk = rHB.next()
                            for (ci, s0, sn) in segs:
                                DV("tensor_tensor_scan", hb[:, s0:s0 + sn], tr[:, s0:s0 + sn], b2[:, s0:s0 + sn],
                                   hcar[:, ch, ci:ci + 1], ALU.mult, ALU.add, r=[trk, b2k, "hcar"], w=[hbk + (ci,)])
                                yield
                                DV("tensor_copy", hcar[:, ch, ci:ci + 1], hb[:, s0 + sn - 1:s0 + sn], r=[hbk + (ci,)], w=["hcar"])
                            if oc is not None:
                                pg, pgk = psB.next()
                                proj(pg, pgk, 1, ch, xt, xk, n)
                                gs = xc
                                AC(gs[:, 0:n], pg[:, 0:n], AF.Copy, r=[pgk], w=[xck])
                                yield
                                g2, g2k = rG2.next()
                                AC(g2[:, 0:n], gs[:, 0:n], AF.Square, r=[xck], w=[g2k])
                                yield
                                DV("tensor_scalar", g2[:, 0:n], g2[:, 0:n], 0.044715, 1.0, ALU.mult, ALU.add, r=[g2k], w=[g2k])
                                yield
                                DV("tensor_tensor", g2[:, 0:n], g2[:, 0:n], gs[:, 0:n], ALU.mult, r=[g2k, xck], w=[g2k])
                                yield
                                AC(g2[:, 0:n], g2[:, 0:n], AF.Exp, r=[g2k], w=[g2k], scale=-1.5957691216057308)
                                yield
                                DV("tensor_scalar", g2[:, 0:n], g2[:, 0:n], 1.0, None, ALU.add, r=[g2k], w=[g2k])
                                yield
                                DV("reciprocal", g2[:, 0:n], g2[:, 0:n], r=[g2k], w=[g2k])
                                yield
                                DV("tensor_tensor", g2[:, 0:n], g2[:, 0:n], gs[:, 0:n], ALU.mult, r=[g2k, xck], w=[g2k])
                                yield
                                DV("tensor_tensor", mixT[:, 8 + ch, oc:oc + n], g2[:, 0:n], hb[:, 0:n], ALU.mult,
                                   r=[g2k] + [hbk + (ci,) for (ci, _, _) in segs], w=[("mix", 8 + ch, b)])

                        IL = 3
                        for (b, c0, n) in blocks(own_only=False):
                            xt, xk, tt, tk = load_x(b, c0, n, fast=False)
                            oc = own_col(b)
                            for g0 in range(0, 8, IL):
                                gens = [lru_chain(b, n, ch, xt, xk, tt, tk, oc) for ch in range(g0, min(8, g0 + IL))]
                                while gens:
                                    for g in list(gens):
                                        try:
                                            next(g)
                                        except StopIteration:
                                            gens.remove(g)
                        DMA("sp", hout[:, :, :], hcar[:], r=["hcar"])
                        DMA("sp", cout[:, :, :, :], xcar[:], r=["xcar"])
                        CHK("p2b")

                        P.barrier()

                qT = None
                run_passes("A")
                qT = sb("qT", [128, 8, T2], BF16, scope=es1)
                run_passes("B")

                with ExitStack() as esa:
                    psf = [esa.enter_context(nc.psum_tensor("pa%d" % i, [128, 512], F32)) for i in range(8)]
                    kTh = [sb("kTh%d" % i, [128, NSLOT], BF16, scope=esa) for i in range(2)]
                    Vh = [sb("Vh%d" % i, [128, 32, 129], BF16, scope=esa) for i in range(2)]
                    ptr = Rot("PT", [sb("PT%d" % i, [128, 2, BLK], BF16, scope=esa) for i in range(3)])
                    kbias = sb("kbias", [128, 32], scope=esa)
                    ckt = [sb("ckt%d" % i, [128, 2048], BF16, scope=esa) for i in range(2)]
                    cvt = [sb("cvt%d" % i, [128, 16, 129], BF16, scope=esa) for i in range(2)]
                    ep = Rot("ep", [sb("ep%d" % i, [128, 128], scope=esa) for i in range(4)])
                    epb = Rot("epb", [sb("epb%d" % i, [128, 128], scope=esa) for i in range(2)])
                    sm = Rot("sm", [sb("sm%d" % i, [128, 8], scope=esa) for i in range(4)])
                    junk = sb("junk", [128, 128], scope=esa)
                    onesb = sb("onesb", [128, 128], BF16, scope=esa)
                    epf = Rot("epf", [sb("epf%d" % i, [128, BLK], scope=esa) for i in range(7)])
                    PL("memset", onesb[:], 1.0, w=["onesb"])
                    DMA("sp", kbias[:], kbiasd[:, :], w=["kbias"])
                    for i in range(2):
                        PL("memset", cvt[i][:, :, 128:129], 1.0, w=[("cvt", i)])
                    psS = [[psf[0], psf[1]], [psf[2], psf[3]]]
                    accs = {}
                    order = [(s, c) for s in range(4) for c in range(2)]
                    for idx, sc in enumerate(order):
                        accs[sc] = (psf[4 + sc[0]], sc[1] * 129, 4 + sc[0])

                    def epilogue(npart, bank1, o1, bank2, o2, okeys, dst, dkey):
                        ps_ = slice(0, npart)
                        s_, sk = sm.next()
                        t_, tk_ = ep.next()
                        o_, ok_ = ep.next()
                        ob, obk = epb.next()
                        DV("reciprocal", s_[ps_, 0:1], bank1[ps_, o1 + 128:o1 + 129], r=okeys, w=[sk + (0,)])
                        DV("reciprocal", s_[ps_, 1:2], bank2[ps_, o2 + 128:o2 + 129], r=okeys, w=[sk + (1,)])
                        DV("tensor_tensor", s_[ps_, 2:3], s_[ps_, 1:2], lamt[ps_, 1:2], ALU.mult, r=[sk + (1,), "lam"], w=[sk + (2,)])
                        DV("tensor_scalar", t_[ps_, :], bank1[ps_, o1:o1 + 128], s_[ps_, 0:1], None, ALU.mult,
                           r=list(okeys) + [sk + (0,)], w=[tk_])
                        DV("scalar_tensor_tensor", o_[ps_, :], bank2[ps_, o2:o2 + 128], s_[ps_, 2:3], t_[ps_, :],
                           ALU.mult, ALU.add, r=list(okeys) + [sk + (2,), tk_], w=[ok_])
                        AC(junk[ps_, :], o_[ps_, :], AF.Square, r=[ok_], w=["junk", sk + (3,)], accum_out=s_[ps_, 3:4])
                        DV("tensor_scalar", s_[ps_, 4:5], s_[ps_, 3:4], 1.0 / 128.0, RMS_EPS, ALU.mult, ALU.add,
                           r=[sk + (3,)], w=[sk + (4,)])
                        AC(s_[ps_, 6:7], s_[ps_, 4:5], AF.Ln, r=[sk + (4,)], w=[sk + (6,)])
                        AC(s_[ps_, 5:6], s_[ps_, 6:7], AF.Exp, r=[sk + (6,)], w=[sk + (5,)], scale=-0.5)
                        DV("scalar_tensor_tensor", ob[ps_, :], o_[ps_, :], s_[ps_, 5:6], gsub[ps_, :], ALU.mult, ALU.mult,
                           r=[ok_, sk + (5,), "gsub"], w=[obk])
                        TR(bank1[:, 0:npart], ob[ps_, :], ident32[ps_, 0:npart], r=[obk, "ident32"] + list(okeys), w=list(okeys))
                        AC(dst, bank1[:, 0:npart], AF.Copy, r=list(okeys), w=[dkey])

                    def load_head(h):
                        DMA("sp", kTh[h % 2][:], scr_k[h, :, :], w=[("kTh", h % 2)])
                        DMA("sp", Vh[h % 2][:], scr_v[h, :, :, :], w=[("Vh", h % 2)])

                    steps = []
                    for h in range(8):
                        for sbk in range(2):
                            first_kb = 24 + 4 * sbk
                            for kb in range(first_kb + 4):
                                steps.append((h, sbk, kb, first_kb))
                    st_state = {}

                    def emit_qk(si):
                        h, sbk, kb, first_kb = steps[si]
                        kt = kTh[h % 2]
                        i = kb - first_kb
                        c0 = max(0, i) * 128
                        pss = psS[si % 2]
                        psk = [("psS", si % 2, 0), ("psS", si % 2, 1)]
                        for c in range(2):
                            MM(pss[c][:, c0:BLK], kt[64 * c:64 * c + 64, kb * 128:(kb + 1) * 128],
                               qT[64 * c:64 * c + 64, h, sbk * BLK + c0:(sbk + 1) * BLK],
                               r=[("kTh", h % 2)], w=[psk[c]], start=True, stop=True)

                    def emit_exp(si):
                        h, sbk, kb, first_kb = steps[si]
                        i = kb - first_kb
                        c0 = max(0, i) * 128
                        pss = psS[si % 2]
                        psk = [("psS", si % 2, 0), ("psS", si % 2, 1)]
                        pt, ptk = ptr.next()
                        st_state[si] = (pt, ptk)
                        if i < 0:
                            for c in range(2):
                                AC(pt[:, c, :], pss[c][:, :], AF.Exp, r=[psk[c], "kbias"], w=[ptk + (c,)],
                                   bias=kbias[:, kb:kb + 1], scale=QSCALE)
                        else:
                            for c in range(2):
                                AC(pt[0:64, c, c0:BLK], pss[c][0:64, c0:BLK], AF.Exp, r=[psk[c]], w=[ptk + (c, "a")], scale=QSCALE)
                                AC(pt[64:128, c, c0 + 64:BLK], pss[c][64:128, c0 + 64:BLK], AF.Exp, r=[psk[c]],
                                   w=[ptk + (c, "b")], scale=QSCALE)
                                AC(pt[64:128, c, c0:c0 + 64], zero_t[64:128, 0:64], AF.Copy, r=["zero"], w=[ptk + (c, "z")])

                    OT = [psf[4], psf[5]]
                    SM = [psf[6], psf[7]]

                    def emit_pv(si):
                        h, sbk, kb, first_kb = steps[si]
                        vt = Vh[h % 2]
                        vk = ("Vh", h % 2)
                        i = kb - first_kb
                        c0 = max(0, i) * 128
                        last = first_kb + 3
                        pt, ptk = st_state.pop(si)
                        for c in range(2):
                            rkeys = [ptk + (c,)] if i < 0 else [ptk + (c, "a"), ptk + (c, "b"), ptk + (c, "z")]
                            MM(OT[c][:, c0:BLK], vt[:, kb, 0:128], pt[:, c, c0:BLK], r=[vk] + rkeys, w=[("OT", c)],
                               start=(kb == 0), stop=(kb == last))
                            MM(SM[c][:, c0:BLK], onesb[:], pt[:, c, c0:BLK], r=["onesb"] + rkeys, w=[("SM", c)],
                               start=(kb == 0), stop=(kb == last))
                        if kb == last:
                            rs1, rs1k = epf.next()
                            rs2, rs2k = epf.next()
                            t_, tk_ = epf.next()
                            u_, uk_ = epf.next()
                            DV("reciprocal", rs1[:], SM[0][:, :], r=[("SM", 0)], w=[rs1k])
                            DV("reciprocal", rs2[:], SM[1][:, :], r=[("SM", 1)], w=[rs2k])
                            DV("tensor_tensor", t_[:], OT[0][:, :], rs1[:], ALU.mult, r=[("OT", 0), rs1k], w=[tk_])
                            DV("tensor_tensor", u_[:], OT[1][:, :], rs2[:], ALU.mult, r=[("OT", 1), rs2k], w=[uk_])
                            DV("scalar_tensor_tensor", t_[:], u_[:], lamt[:, 1:2], t_[:], ALU.mult, ALU.add,
                               r=[uk_, tk_, "lam"], w=[tk_])
                            AC(u_[:], t_[:], AF.Square, r=[tk_], w=[uk_])
                            MM(SM[0][:, :], ones32[:], u_[:], r=["ones32", uk_], w=[("SM", 0)], start=True, stop=True)
                            AC(rs1[:], SM[0][:, :], AF.Ln, r=[("SM", 0)], w=[rs1k], scale=1.0 / 128.0, bias=RMS_EPS)
                            AC(rs1[:], rs1[:], AF.Exp, r=[rs1k], w=[rs1k], scale=-0.5)
                            DV("scalar_tensor_tensor", mixT[:, h, sbk * BLK:(sbk + 1) * BLK], t_[:], pv[:, 152:153], rs1[:],
                               ALU.mult, ALU.mult, r=[tk_, rs1k, "pvg"], w=[("mix", h, 6 + sbk)])

                    load_head(0)
                    emit_qk(0)
                    for si in range(len(steps)):
                        h, sbk, kb, first_kb = steps[si]
                        if sbk == 0 and kb == 0 and h + 1 < 8:
                            load_head(h + 1)
                        if si + 1 < len(steps):
                            emit_qk(si + 1)
                        emit_exp(si)
                        emit_pv(si)
                    CHK("attn")

                    def load_cache(j):
                        bi, h = j // 8, j % 8
                        DMA("pool", ckt[j % 2][:], ckT[bi, h, :, :], w=[("ckt", j % 2)])
                        DMA("pool", cvt[j % 2][:, :, 0:128], cvd[bi, h, :, :, :], r=[("cvt", j % 2)], w=[("cvt", j % 2)])

                    ssteps = [(j, kb) for j in range(16) for kb in range(17)]
                    bankO = psf[4]
                    sst = {}

                    def s_qk(si):
                        j, kb = ssteps[si]
                        bi, h = j // 8, j % 8
                        q0 = TOWN + bi * 16
                        ck = ckt[j % 2]
                        pss = psS[si % 2]
                        np_ = 128 if kb < 16 else 16
                        for c in range(2):
                            if kb < 16:
                                lhs = ck[64 * c:64 * c + 64, kb * 128:(kb + 1) * 128]
                                rk_ = [("ckt", j % 2)]
                            else:
                                lhs = ksmp[64 * c:64 * c + 64, h, bi * 16:(bi + 1) * 16]
                                rk_ = []
                            MM(pss[c][0:np_, 0:16], lhs, qT[64 * c:64 * c + 64, h, q0:q0 + 16],
                               r=rk_, w=[("psS", si % 2, c)], start=True, stop=True)

                    def s_exp(si):
                        j, kb = ssteps[si]
                        pss = psS[si % 2]
                        np_ = 128 if kb < 16 else 16
                        pt, ptk = ptr.next()
                        sst[si] = (pt, ptk)
                        for c in range(2):
                            AC(pt[0:np_, c, 0:16], pss[c][0:np_, 0:16], AF.Exp, r=[("psS", si % 2, c)], w=[ptk + (c,)], scale=QSCALE)

                    def s_pv(si):
                        j, kb = ssteps[si]
                        bi, h = j // 8, j % 8
                        q0 = TOWN + bi * 16
                        cv = cvt[j % 2]
                        np_ = 128 if kb < 16 else 16
                        pt, ptk = sst.pop(si)
                        for c in range(2):
                            if kb < 16:
                                rhs = cv[:, kb, :]
                                rk_ = [("cvt", j % 2)]
                            else:
                                rhs = vsmp[0:16, bi, h, :]
                                rk_ = []
                            MM(bankO[0:16, c * 129:(c + 1) * 129], pt[0:np_, c, 0:16], rhs,
                               r=rk_ + [ptk + (c,)], w=[("psO", 0, c)],
                               start=(kb == 0 and c == 0), stop=(kb == 16), skip_group_check=True)
                        if kb == 16:
                            epilogue(16, bankO, 0, bankO, 129, [("psO", 0, 0), ("psO", 0, 1)],
                                     mixT[:, h, q0:q0 + 16], ("mix", h, 8, bi))

                    load_cache(0)
                    s_qk(0)
                    for si in range(len(ssteps)):
                        j, kb = ssteps[si]
                        if kb == 0 and j + 1 < 16:
                            load_cache(j + 1)
                        if si + 1 < len(ssteps):
                            s_qk(si + 1)
                        s_exp(si)
                        s_pv(si)
                    CHK("sattn")
                    P.barrier()

            with ExitStack() as es2:
                psf = [es2.enter_context(nc.psum_tensor("pm%d" % i, [128, 512], F32)) for i in range(6)]
                R = sb("R", [128, NKC, T2], scope=es2)
                XB = sb("XB", [128, NKC, T2], BF16, scope=es2)
                wtr = Rot("WT", [sb("WT%d" % i, [128, NKC, 512], BF16, scope=es2) for i in range(2)])
                sqr = Rot("sq", [sb("sq%d" % i, [128, TB], scope=es2) for i in range(4)])
                relr = Rot("rel", [sb("rel%d" % i, [128, TB], scope=es2) for i in range(4)])
                mean_t = sb("mean_t", [128, TB], scope=es2)
                rstd_t = sb("rstd_t", [128, TB], scope=es2)
                msq_t = sb("msq_t", [128, TB], scope=es2)
                lnr = Rot("lnt", [sb("lnt%d" % i, [128, TB], scope=es2) for i in range(4)])
                psR = Rot("psR", [psf[0], psf[1], psf[2], psf[3]])
                Hq = mixT

                for j in range(4):
                    DMA("sp", R[:, 4 * j:4 * j + 4, :], xown[512 * j:512 * (j + 1), :].rearrange("(c p) t -> p c t", p=128),
                        w=[("R", m) for m in range(4 * j, 4 * j + 4)])

                def load_wt(src, r0, c0):
                    wt, wtk = wtr.next()
                    DMA("pool", wt[:], src[r0:r0 + 2048, c0:c0 + 512].rearrange("(c p) n -> p c n", p=128), w=[wtk])
                    return wt, wtk

                def mm_group(wt, wtk, mp, act, akey, tb):
                    ps, psk = psR.next()
                    for kc in range(NKC):
                        MM(ps[:, 0:TB], wt[:, kc, mp * 128:(mp + 1) * 128], act[:, kc, tb * TB:(tb + 1) * TB],
                           r=[wtk, (akey, kc)], w=[psk], start=(kc == 0), stop=(kc == NKC - 1))
                    return ps, psk

                def layer_norm(gcol, bcol, write_bf):
                    for tb in range(NTB):
                        cs = slice(tb * TB, (tb + 1) * TB)
                        p1, p1k = psf[4], ("psLN", 0)
                        p2, p2k = psf[5], ("psLN", 1)
                        for m in range(NKC):
                            sq, sqk = sqr.next()
                            AC(sq[:], R[:, m, cs], AF.Square, r=[("R", m)], w=[sqk])
                            MM(p1[:, 0:TB], ones32[:], R[:, m, cs], r=["ones32", ("R", m)], w=[p1k],
                               start=(m == 0), stop=(m == NKC - 1))
                            MM(p2[:, 0:TB], ones32[:], sq[:], r=["ones32", sqk], w=[p2k],
                               start=(m == 0), stop=(m == NKC - 1))
                        DV("tensor_scalar", mean_t[:], p1[:, 0:TB], 1.0 / D, None, ALU.mult, r=[p1k], w=["mean"])
                        DV("tensor_tensor", msq_t[:], mean_t[:], mean_t[:], ALU.mult, r=["mean"], w=["msq"])
                        DV("scalar_tensor_tensor", rstd_t[:], p2[:, 0:TB], 1.0 / D, msq_t[:], ALU.mult, ALU.subtract,
                           r=[p2k, "msq"], w=["rstd"])
                        DV("tensor_scalar", rstd_t[:], rstd_t[:], LN_EPS, None, ALU.add, r=["rstd"], w=["rstd"])
                        AC(rstd_t[:], rstd_t[:], AF.Ln, r=["rstd"], w=["rstd"])
                        AC(rstd_t[:], rstd_t[:], AF.Exp, r=["rstd"], w=["rstd"], scale=-0.5)
                        for m in range(NKC):
                            t_, tk_ = lnr.next()
                            DV("tensor_tensor", t_[:], R[:, m, cs], mean_t[:], ALU.subtract, r=[("R", m), "mean"], w=[tk_])
                            DV("tensor_tensor", t_[:], t_[:], rstd_t[:], ALU.mult, r=[tk_, "rstd"], w=[tk_])
                            DV("tensor_scalar", R[:, m, cs], t_[:], pv[:, gcol + m:gcol + m + 1], pv[:, bcol + m:bcol + m + 1],
                               ALU.mult, ALU.add, r=[tk_, "pv"], w=[("R", m)])
                            if write_bf:
                                AC(XB[:, m, cs], R[:, m, cs], AF.Copy, r=[("R", m)], w=[("XB", m)])

                for j in range(4):
                    wt, wtk = load_wt(w_out, 0, 512 * j)
                    for mp in range(4):
                        m = 4 * j + mp
                        for tb in range(NTB):
                            ps, psk = mm_group(wt, wtk, mp, mixT, "H", tb)
                            cs = slice(tb * TB, (tb + 1) * TB)
                            DV("scalar_tensor_tensor", R[:, m, cs], R[:, m, cs], ALPHA, ps[:, 0:TB], ALU.mult, ALU.add,
                               r=[("R", m), psk], w=[("R", m)])
                layer_norm(PV_G1, PV_B1, True)

                for Q in range(4):
                    for j in range(4):
                        wt, wtk = load_wt(w_up, 0, Q * 2048 + 512 * j)
                        for mp in range(4):
                            fc = 4 * j + mp
                            for tb in range(NTB):
                                ps, psk = mm_group(wt, wtk, mp, XB, "XB", tb)
                                cs = slice(tb * TB, (tb + 1) * TB)
                                rl, rlk = relr.next()
                                AC(rl[:], ps[:, 0:TB], AF.Relu, r=[psk], w=[rlk])
                                DV("tensor_tensor", Hq[:, fc, cs], rl[:], rl[:], ALU.mult, r=[rlk], w=[("H", fc)])
                    for j in range(4):
                        wt, wtk = load_wt(w_down, Q * 2048, 512 * j)
                        for mp in range(4):
                            m = 4 * j + mp
                            for tb in range(NTB):
                                ps, psk = mm_group(wt, wtk, mp, Hq, "H", tb)
                                cs = slice(tb * TB, (tb + 1) * TB)
                                if Q == 0:
                                    DV("scalar_tensor_tensor", R[:, m, cs], R[:, m, cs], ALPHA, ps[:, 0:TB], ALU.mult, ALU.add,
                                       r=[("R", m), psk], w=[("R", m)])
                                else:
                                    DV("tensor_tensor", R[:, m, cs], R[:, m, cs], ps[:, 0:TB], ALU.add,
                                       r=[("R", m), psk], w=[("R", m)])
                layer_norm(PV_G2, PV_B2, False)
                for j in range(4):
                    DMA("sp", yT[512 * j:512 * (j + 1), :].rearrange("(c p) t -> p c t", p=128), R[:, 4 * j:4 * j + 4, :],
                        r=[("R", m) for m in range(4 * j, 4 * j + 4)])
                P.barrier()

        except _Stop:
            pass

        P.emit()
    return nc


def _perm_matrix():
    m = np.arange(128)
    d = m % 64
    partner = np.where(d < 32, m + 32, m - 32)
    P = np.zeros((128, 128), np.float32)
    P[partner, m] = 1.0
    return P


def _rope_tables(pos):
    inv = (np.float32(10000.0) ** (-(np.arange(0, 64, 2, dtype=np.float32) / np.float32(64)))).astype(np.float32)
    ang = pos.astype(np.float32)[:, None] * inv[None, :]
    cos = np.cos(ang).astype(np.float32)
    sin = np.sin(ang).astype(np.float32)
    d = np.arange(128) % 64
    f = d % 32
    sgn = np.where(d < 32, -1.0, 1.0).astype(np.float32)
    cosT = np.ascontiguousarray(cos[:, f].T)
    sinT = np.ascontiguousarray((sin[:, f] * sgn[None, :]).T)
    return cosT, sinT


_NC_CACHE = {}
import os as _os
_DBG = _os.environ.get('KDBG', '')


def _prep(x_prompt, x_sample, cache_k, cache_v, state_h, state_conv,
           w_in, lambda_q1, lambda_k1, lambda_q2, lambda_k2, subln_g,
           conv_w, conv_b, w_rg_a, b_rg_a, w_rg_i, b_rg_i, lru_lambda,
           w_out, ln1_g, ln1_b, w_up, w_down, ln2_g, ln2_b):
    f32 = np.float32
    x_prompt = np.asarray(x_prompt, f32); x_sample = np.asarray(x_sample, f32)
    cache_k = np.asarray(cache_k, f32); cache_v = np.asarray(cache_v, f32)
    w_in0 = np.asarray(w_in, f32)[0]
    wq_ = w_in0[:, 0:1024]; wk_ = w_in0[:, 1024:2048]; wv_ = w_in0[:, 2048:3072]
    wxb_ = w_in0[:, 3072:4096]; wg_ = w_in0[:, 4096:5120]
    col = np.arange(1024)
    dd = col % 64
    partner = np.where(dd < 32, col + 32, col - 32)
    shared = {
        "wq": np.ascontiguousarray(wq_), "wqp": np.ascontiguousarray(wq_[:, partner]),
        "wk": np.ascontiguousarray(wk_), "wkp": np.ascontiguousarray(wk_[:, partner]),
        "wv": np.ascontiguousarray(wv_), "wxb": np.ascontiguousarray(wxb_), "wg": np.ascontiguousarray(wg_),
        "w_out": np.ascontiguousarray(np.asarray(w_out, f32)[0]),
        "w_up": np.ascontiguousarray(np.asarray(w_up, f32)[0]),
        "w_down": np.ascontiguousarray(np.asarray(w_down, f32)[0]),
        "identd": np.eye(128, dtype=f32),
        "permd": _perm_matrix(),
        "subgd": np.ascontiguousarray(np.broadcast_to(np.asarray(subln_g, f32)[0][None, :], (128, 128))),
    }
    lam4 = np.stack([np.asarray(v, f32)[0] for v in (lambda_q1, lambda_k1, lambda_q2, lambda_k2)], 0)
    shared["lamd"] = np.ascontiguousarray(np.broadcast_to(lam4[None], (128, 4, 64)))
    wa = np.asarray(w_rg_a, f32)[0]; wi = np.asarray(w_rg_i, f32)[0]
    wabd = np.zeros((128, 8, 128), f32); wibd = np.zeros((128, 8, 128), f32)
    for ch in range(8):
        for t in range(2):
            wabd[64 * t:64 * t + 64, ch, 64 * t:64 * t + 64] = wa[2 * ch + t]
            wibd[64 * t:64 * t + 64, ch, 64 * t:64 * t + 64] = wi[2 * ch + t]
    shared["wabd"] = wabd; shared["wibd"] = wibd
    pvd = np.zeros((128, 160), f32)

    def pc(v, n):
        return np.asarray(v, f32).reshape(n, 128).T

    cw = np.asarray(conv_w, f32)[0]
    for ch in range(8):
        for j in range(4):
            pvd[:, ch * 4 + j] = cw[j, ch * 128:(ch + 1) * 128]
    pvd[:, 32:40] = pc(np.asarray(conv_b)[0], 8)
    pvd[:, 40:48] = pc(np.asarray(b_rg_a)[0], 8)
    pvd[:, 48:56] = pc(np.asarray(b_rg_i)[0], 8)
    pvd[:, 56:64] = pc(np.asarray(lru_lambda)[0], 8)
    pvd[:, 64:80] = pc(np.asarray(ln1_g)[0], 16)
    pvd[:, 80:96] = pc(np.asarray(ln1_b)[0], 16)
    pvd[:, 96:112] = pc(np.asarray(ln2_g)[0], 16)
    pvd[:, 112:128] = pc(np.asarray(ln2_b)[0], 16)
    pvd[:, 152] = np.asarray(subln_g, f32)[0]
    shared["pvd"] = pvd

    in_maps = []
    for c in range(8):
        b, qi = c // 4, c % 4
        shift = 3072 - 1024 * qi
        m = dict(shared)
        xr = np.zeros((D, NSLOT), f32)
        xr[:, shift:] = x_prompt[b, :1024 * (qi + 1), :].T
        m["xrot"] = xr
        xo = np.empty((D, T2), f32)
        xo[:, :TOWN] = x_prompt[b, 1024 * qi:1024 * (qi + 1), :].T
        xo[:, TOWN:] = x_sample[2 * c:2 * c + 2].reshape(NSMP, D).T
        m["xown"] = xo
        pos = np.concatenate([np.maximum(np.arange(NSLOT) - shift, 0), 2048 + np.arange(16), 2048 + np.arange(16)])
        cosT, sinT = _rope_tables(pos)
        m["cosd"] = cosT; m["sind"] = sinT
        valid = np.concatenate([(np.arange(NSLOT) >= shift).astype(f32), np.ones(NSMP, f32)])
        m["vald"] = np.ascontiguousarray(np.broadcast_to(valid[None, :], (128, NSLOT + NSMP)))
        kb_valid = (np.arange(32) * 128 >= shift)
        m["kbiasd"] = np.ascontiguousarray(np.broadcast_to(np.where(kb_valid, 0.0, NEG).astype(f32)[None, :], (128, 32)))
        ck = cache_k[0, 2 * c:2 * c + 2]
        m["ckT"] = np.ascontiguousarray(ck.transpose(0, 2, 3, 1))
        cv = cache_v[0, 2 * c:2 * c + 2].reshape(2, 16, 128, 8, 128)
        m["cvd"] = np.ascontiguousarray(cv.transpose(0, 3, 2, 1, 4))
        sh = np.asarray(state_h, f32)[0, 2 * c:2 * c + 2]
        m["shd"] = np.ascontiguousarray(sh.reshape(2, 8, 128).transpose(2, 1, 0))
        sc = np.asarray(state_conv, f32)[0, 2 * c:2 * c + 2]
        m["scd"] = np.ascontiguousarray(sc.reshape(2, 3, 8, 128).transpose(3, 2, 0, 1))
        in_maps.append(m)

    return in_maps


def _assemble(R):
    f32 = np.float32
    y_prompt = np.empty((2, 4096, D), f32); y_sample = np.empty((16, 16, D), f32)
    k_prompt = np.empty((1, 2, 4096, 8, 128), f32); v_prompt = np.empty((1, 2, 4096, 8, 128), f32)
    h_prompt = np.empty((1, 2, 1024), f32); conv_prompt = np.empty((1, 2, 3, 1024), f32)
    k_sample = np.empty((1, 16, 16, 8, 128), f32); v_sample = np.empty((1, 16, 16, 8, 128), f32)
    h_sample = np.empty((1, 16, 1024), f32); conv_sample = np.empty((1, 16, 3, 1024), f32)
    for c in range(8):
        b, qi = c // 4, c % 4
        r = R[c]
        yT = np.asarray(r["yT"], f32)
        y_prompt[b, 1024 * qi:1024 * (qi + 1)] = yT[:, :TOWN].T
        y_sample[2 * c:2 * c + 2] = yT[:, TOWN:].T.reshape(2, 16, D)
        kT = np.asarray(r["koutT"], f32); vT = np.asarray(r["voutT"], f32)
        k_prompt[0, b, 1024 * qi:1024 * (qi + 1)] = kT[:, :, :TOWN].transpose(2, 0, 1)
        v_prompt[0, b, 1024 * qi:1024 * (qi + 1)] = vT[:, :, :TOWN].transpose(2, 0, 1)
        k_sample[0, 2 * c:2 * c + 2] = kT[:, :, TOWN:].transpose(2, 0, 1).reshape(2, 16, 8, 128)
        v_sample[0, 2 * c:2 * c + 2] = vT[:, :, TOWN:].transpose(2, 0, 1).reshape(2, 16, 8, 128)
        ho = np.asarray(r["hout"], f32)
        co = np.asarray(r["cout"], f32)
        if qi == 3:
            h_prompt[0, b] = ho[:, :, 0].T.reshape(1024)
            conv_prompt[0, b] = co[:, :, 0, :].transpose(2, 1, 0).reshape(3, 1024)
        for bi in range(2):
            h_sample[0, 2 * c + bi] = ho[:, :, 1 + bi].T.reshape(1024)
            conv_sample[0, 2 * c + bi] = co[:, :, 1 + bi, :].transpose(2, 1, 0).reshape(3, 1024)
    return (y_prompt, y_sample, k_prompt, v_prompt, h_prompt, conv_prompt,
            k_sample, v_sample, h_sample, conv_sample)


def kernel(x_prompt, x_sample, cache_k, cache_v, state_h, state_conv,
           w_in, lambda_q1, lambda_k1, lambda_q2, lambda_k2, subln_g,
           conv_w, conv_b, w_rg_a, b_rg_a, w_rg_i, b_rg_i, lru_lambda,
           w_out, ln1_g, ln1_b, w_up, w_down, ln2_g, ln2_b):
    in_maps = _prep(x_prompt, x_sample, cache_k, cache_v, state_h, state_conv,
                    w_in, lambda_q1, lambda_k1, lambda_q2, lambda_k2, subln_g,
                    conv_w, conv_b, w_rg_a, b_rg_a, w_rg_i, b_rg_i, lru_lambda,
                    w_out, ln1_g, ln1_b, w_up, w_down, ln2_g, ln2_b)
    if "nc" not in _NC_CACHE:
        _NC_CACHE["nc"] = build_nc()
    nc = _NC_CACHE["nc"]
    res = run_bass_kernel_spmd(nc, in_maps, core_ids=list(range(8)))
    return _assemble(res.results)
```

```python
import numpy as np
from contextlib import ExitStack
import concourse.bass as bass
import concourse.mybir as mybir
from concourse.bass_utils import run_bass_kernel_spmd

F32 = mybir.dt.float32
BF16 = mybir.dt.bfloat16
ALU = mybir.AluOpType
AF = mybir.ActivationFunctionType

D = 2048
NKC = 16
NSLOT = 4096
BLK = 512
NBLK = 8
OWN0 = 3072
TOWN = 1024
NSMP = 32
T2 = TOWN + NSMP
TB = 352
NTB = 3
LAM_INIT = 0.8 - 0.6 * 1.0
ALPHA = 2.0 ** 0.25
LN_EPS = 1e-5
RMS_EPS = 1e-5
QSCALE = 64 ** -0.5
NEG = -30000.0


class Op:
    __slots__ = ("eng", "fn", "dma", "deps", "signal", "semi", "count")


class Prog:
    CE = ("pe", "act", "dve", "pool")
    DQ = ("sp", "pool", "act")

    def __init__(self, nc, es, ndma=6):
        self.nc = nc
        self.ops = []
        self.lw = {}
        self.rd = {}
        self.sems = []
        self.esem = {}
        for e in self.CE:
            self.esem[e] = len(self.sems)
            self.sems.append(es.enter_context(nc.semaphore("sem_" + e)))
        self.ndma = ndma
        self.dsem = {}
        self.dcnt = {}
        self.dlast = {}
        self.dnext = {}
        for q in self.DQ:
            self.dsem[q] = []
            for i in range(ndma):
                self.dsem[q].append(len(self.sems))
                self.sems.append(es.enter_context(nc.semaphore("dq_%s_%d" % (q, i))))
            self.dcnt[q] = [0] * ndma
            self.dlast[q] = [None] * ndma
            self.dnext[q] = 0
        self.last_of = {e: None for e in self.CE}
        self.stopped = False

    def add(self, eng, meth, args=(), kw=None, reads=(), writes=(), dma=False):
        op = Op()
        op.eng = eng
        op.fn = (meth, tuple(args), dict(kw or {}))
        op.dma = dma
        op.signal = False
        op.semi = None
        op.count = 0
        op.deps = []
        if self.stopped:
            return op
        def _expand(keys):
            out = []
            for k in keys:
                out.append(k)
                if isinstance(k, tuple) and len(k) > 2:
                    out.append(k[:2])
            return out
        reads = _expand(reads)
        writes = _expand(writes)
        banks = set()
        for a in list(args) + list((kw or {}).values()):
            t = getattr(a, "tensor", None)
            if t is not None and type(t).__name__ == "PSumTensorHandle":
                banks.add(("BANK", t.name))
        if banks:
            writes = list(writes) + sorted(banks)
        raw = []
        other = []
        for k in reads:
            w = self.lw.get(k)
            if w is not None:
                raw.append(w)
        for k in writes:
            w = self.lw.get(k)
            if w is not None:
                other.append(w)
            r = self.rd.get(k)
            if r:
                other.extend(r[0].values())
                other.extend(r[1])
        deps = []
        seen = set()
        for lst, is_raw in ((raw, True), (other, False)):
            for d in lst:
                if id(d) in seen:
                    continue
                if (not dma) and (not d.dma) and d.eng == eng:
                    if eng == "pe":
                        continue
                seen.add(id(d))
                deps.append(d)
        if dma:
            i = self.dnext[eng]
            self.dnext[eng] = (i + 1) % self.ndma
            prev = self.dlast[eng][i]
            if prev is not None and id(prev) not in seen:
                deps.append(prev)
            self.dcnt[eng][i] += 16
            op.semi = self.dsem[eng][i]
            op.count = self.dcnt[eng][i]
            self.dlast[eng][i] = op
        for d in deps:
            d.signal = True
        op.deps = deps
        for k in writes:
            self.lw[k] = op
            self.rd[k] = [{}, []]
        for k in reads:
            if k in writes:
                continue
            r = self.rd.setdefault(k, [{}, []])
            if dma:
                r[1].append(op)
            else:
                r[0][eng] = op
        if not dma:
            self.last_of[eng] = op
        self.ops.append(op)
        return op

    def barrier(self):
        if self.stopped:
            return
        deps = [o for o in self.last_of.values() if o is not None]
        for q in self.DQ:
            deps.extend(o for o in self.dlast[q] if o is not None)
        for d in deps:
            d.signal = True
        for e in ("pe", "act", "dve", "pool", "sp"):
            op = Op()
            op.eng = e
            op.fn = None
            op.dma = False
            op.signal = False
            op.semi = None
            op.count = 0
            op.deps = list(deps)
            self.ops.append(op)
        self.lw = {}
        self.rd = {}

    def emit(self):
        nc = self.nc
        cnt = {e: 0 for e in self.CE}
        for op in self.ops:
            if (not op.dma) and op.signal and op.fn is not None:
                cnt[op.eng] += 1
                op.semi = self.esem[op.eng]
                op.count = cnt[op.eng]
        byeng = {e: [] for e in ("pe", "act", "dve", "pool", "sp")}
        for op in self.ops:
            byeng[op.eng].append(op)
        sems = self.sems

        def run(name, h):
            waited = {}
            for op in byeng[name]:
                for d in op.deps:
                    if d.semi is None:
                        continue
                    if waited.get(d.semi, 0) >= d.count:
                        continue
                    h.wait_ge(sems[d.semi], d.count)
                    waited[d.semi] = d.count
                if op.fn is None:
                    continue
                meth, args, kw = op.fn
                ins = getattr(h, meth)(*args, **kw)
                if op.dma:
                    ins.then_inc(sems[op.semi], 16)
                elif op.signal:
                    ins.then_inc(sems[op.semi], 1)
            if name == "sp":
                for q in self.DQ:
                    for i in range(self.ndma):
                        c = self.dcnt[q][i]
                        if c > 0 and waited.get(self.dsem[q][i], 0) < c:
                            h.wait_ge(sems[self.dsem[q][i]], c)

        with nc.Block() as block:
            @block.tensor
            def _(h):
                run("pe", h)

            @block.scalar
            def _(h):
                run("act", h)

            @block.vector
            def _(h):
                run("dve", h)

            @block.gpsimd
            def _(h):
                run("pool", h)

            @block.sync
            def _(h):
                run("sp", h)


class _CompView:
    def __init__(self, t, c):
        self.t = t
        self.c = c

    def __getitem__(self, idx):
        rows, cols = idx
        return self.t[rows, self.c, cols]


class Rot:
    def __init__(self, name, tiles, keys=None):
        self.name = name
        self.tiles = tiles
        self.keys = keys if keys is not None else [(name, j) for j in range(len(tiles))]
        self.i = 0

    def next(self):
        j = self.i % len(self.tiles)
        self.i += 1
        return self.tiles[j], self.keys[j]


_PROG = [None]


class _Stop(Exception):
    pass


def build_nc(stop=None):
    nc = bass.Bass("TRN2", target_bir_lowering=False)

    def din(name, shape, dt=F32):
        return nc.dram_tensor(name, list(shape), dt, kind="ExternalInput").ap()

    def dout(name, shape, dt=F32):
        return nc.dram_tensor(name, list(shape), dt, kind="ExternalOutput").ap()

    xrot = din("xrot", [D, NSLOT])
    xown = din("xown", [D, T2])
    wq = din("wq", [D, 1024]); wqp = din("wqp", [D, 1024])
    wk = din("wk", [D, 1024]); wkp = din("wkp", [D, 1024])
    wv = din("wv", [D, 1024]); wxb = din("wxb", [D, 1024]); wg = din("wg", [D, 1024])
    w_out = din("w_out", [D, D]); w_up = din("w_up", [D, 4 * D]); w_down = din("w_down", [4 * D, D])
    cosd = din("cosd", [128, NSLOT + NSMP]); sind = din("sind", [128, NSLOT + NSMP])
    vald = din("vald", [128, NSLOT + NSMP])
    kbiasd = din("kbiasd", [128, 32])
    wabd = din("wabd", [128, 8, 128]); wibd = din("wibd", [128, 8, 128])
    pvd = din("pvd", [128, 160])
    lamd = din("lamd", [128, 4, 64])
    subgd = din("subgd", [128, 128])
    identd = din("identd", [128, 128])
    permd = din("permd", [128, 128])
    ckT = din("ckT", [2, 8, 128, 2048])
    cvd = din("cvd", [2, 8, 128, 16, 128])
    shd = din("shd", [128, 8, 2])
    scd = din("scd", [128, 8, 2, 3])

    yT = dout("yT", [D, T2])
    koutT = dout("koutT", [8, 128, T2])
    voutT = dout("voutT", [8, 128, T2])
    hout = dout("hout", [128, 8, 3])
    cout = dout("cout", [128, 8, 3, 3])

    scr_k = nc.dram_tensor("scr_k", [8, 128, NSLOT], BF16).ap()
    scr_v = nc.dram_tensor("scr_v", [8, 128, 32, 129], BF16).ap()

    PV_CW = 0; PV_CB = 32; PV_BA = 40; PV_BI = 48; PV_LL = 56
    PV_G1 = 64; PV_B1 = 80; PV_G2 = 96; PV_B2 = 112
    PV_HBA = 128; PV_HBI = 136; PV_HC = 144
    AX = mybir.AxisListType.X

    with ExitStack() as es:
        P = Prog(nc, es)
        _PROG[0] = P

        def CHK(name):
            if stop == name:
                P.barrier()
                P.stopped = True

        def sb(name, shape, dt=F32, scope=es):
            return scope.enter_context(nc.sbuf_tensor(name, list(shape), dt))

        def OP(eng, meth, *args, r=(), w=(), **kw):
            return P.add(eng, meth, args, kw, list(r), list(w), False)

        def DV(meth, *args, r=(), w=(), **kw):
            return P.add("dve", meth, args, kw, list(r), list(w), False)

        def AC(out, in_, func, r=(), w=(), **kw):
            return P.add("act", "activation", (), dict(out=out, in_=in_, func=func, **kw), list(r), list(w), False)

        def PL(meth, *args, r=(), w=(), **kw):
            return P.add("pool", meth, args, kw, list(r), list(w), False)

        def MM(out, lhsT, rhs, r=(), w=(), **kw):
            return P.add("pe", "matmul", (out, lhsT, rhs), kw, list(r), list(w), False)

        def TR(out, in_, idn, r=(), w=()):
            return P.add("pe", "transpose", (out, in_, idn), {}, list(r), list(w), False)

        def DMA(q, out, in_, r=(), w=()):
            return P.add(q, "dma_start", (), dict(out=out, in_=in_), list(r), list(w), True)

        try:
            ident32 = sb("ident32", [128, 128])
            permT = sb("permT", [128, 128])
            mixT = sb("mixT", [128, 16, T2], BF16)
            pv = sb("pv", [128, 160])
            ident = sb("ident", [128, 128], BF16)
            ones32 = sb("ones32", [128, 128])
            lamt = sb("lamt", [128, 4])
            gsub = sb("gsub", [128, 128])
            zero_t = sb("zero_t", [128, 64])

            DMA("sp", pv[:, :], pvd[:, :], w=["pv"])
            DMA("pool", ident[:], identd[:, :], w=["ident"])
            DMA("sp", ident32[:], identd[:, :], w=["ident32"])
            DMA("sp", permT[:], permd[:, :], w=["permT"])
            PL("memset", ones32[:], 1.0, w=["ones32"])
            PL("memset", zero_t[:], 0.0, w=["zero"])
            DMA("sp", gsub[:], subgd[:, :], w=["gsub"])
            with ExitStack() as es0:
                lamin = sb("lamin", [128, 4, 64], scope=es0)
                lamp = sb("lamp", [128, 2, 64], scope=es0)
                lams = sb("lams", [128, 4], scope=es0)
                sp_t = sb("sp_t", [128, 16], scope=es0)
                DMA("sp", lamin[:], lamd[:, :, :], w=["lamin"])
                DV("tensor_tensor", lamp[:, 0, :], lamin[:, 0, :], lamin[:, 1, :], ALU.mult, r=["lamin"], w=["lamp0"])
                DV("tensor_tensor", lamp[:, 1, :], lamin[:, 2, :], lamin[:, 3, :], ALU.mult, r=["lamin"], w=["lamp1"])
                DV("reduce_sum", lams[:, 0:1], lamp[:, 0, :], AX, r=["lamp0"], w=["lams0"])
                DV("reduce_sum", lams[:, 1:2], lamp[:, 1, :], AX, r=["lamp1"], w=["lams1"])
                AC(lams[:, 2:4], lams[:, 0:2], AF.Exp, r=["lams0", "lams1"], w=["lams23"])
                DV("scalar_tensor_tensor", lamt[:, 0:1], lams[:, 2:3], LAM_INIT, lams[:, 3:4], ALU.add, ALU.subtract,
                   r=["lams23"], w=["lam0"])
                DV("tensor_scalar", lamt[:, 1:2], lamt[:, 0:1], -1.0, None, ALU.mult, r=["lam0"], w=["lam"])
                DV("tensor_scalar", gsub[:], gsub[:], 1.0 - LAM_INIT, None, ALU.mult, r=["gsub"], w=["gsub"])
                DV("tensor_scalar", pv[:, 152:153], pv[:, 152:153], 1.0 - LAM_INIT, None, ALU.mult, r=["pv"], w=["pvg"])
                DV("tensor_scalar", pv[:, PV_HBA:PV_HBA + 16], pv[:, PV_BA:PV_BA + 16], -1.0, None, ALU.mult,
                   r=["pv"], w=["pvh"])
                AC(sp_t[:, 0:8], pv[:, PV_LL:PV_LL + 8], AF.Exp, r=["pv"], w=["sp0"], scale=-1.0)
                DV("tensor_scalar", sp_t[:, 0:8], sp_t[:, 0:8], 1.0, None, ALU.add, r=["sp0"], w=["sp1"])
                AC(sp_t[:, 8:16], sp_t[:, 0:8], AF.Ln, r=["sp1"], w=["sp2"])
                DV("tensor_scalar", pv[:, PV_HC:PV_HC + 8], sp_t[:, 8:16], -8.0, None, ALU.mult, r=["sp2"], w=["pvc"])
                P.barrier()

            with ExitStack() as es1:
                ksmp = sb("ksmp", [128, 8, NSMP], BF16, scope=es1)
                vsmp = sb("vsmp", [16, 2, 8, 129], BF16, scope=es1)

                def run_passes(stage):
                    with ExitStack() as esp:
                        psf = [esp.enter_context(nc.psum_tensor("ps%d%s" % (i, stage), [128, 512], F32)) for i in range(7)]
                        pst = esp.enter_context(nc.psum_tensor("pst" + stage, [128, 4, 128], BF16))
                        wbuf = sb("wbuf" + stage, [128, NKC, 2048], BF16, scope=esp)
                        xrotr = Rot("xblk", [sb("xblk%d%s" % (i, stage), [128, NKC, BLK], BF16, scope=esp) for i in range(2)])
                        tabr = Rot("tab", [sb("tab%d%s" % (i, stage), [128, 3, BLK], scope=esp) for i in range(2)])
                        wab = sb("wab" + stage, [128, 8, 128], BF16, scope=esp)
                        wib = sb("wib" + stage, [128, 8, 128], BF16, scope=esp)
                        hcar = sb("hcar" + stage, [128, 8, 3], scope=esp)
                        xcar = sb("xcar" + stage, [128, 8, 3, 3], scope=esp)
                        _ft = [sb("f32t%d%s" % (i, stage), [128, BLK], scope=esp) for i in range(18 if stage == "A" else 12)]
                        _fk = [("f32t", i) for i in range(18 if stage == "A" else 12)]
                        _nw = 8 if stage == "A" else 6
                        f32r = Rot("f32t", _ft[0:_nw], _fk[0:_nw])
                        stgr = Rot("stg", _ft[_nw:], _fk[_nw:])
                        rXC = Rot("rXC", _ft[0:3], _fk[0:3])
                        rTR = Rot("rTR", _ft[3:6], _fk[3:6])
                        rTI = Rot("rTI", _ft[6:9], _fk[6:9])
                        rB2 = Rot("rB2", _ft[9:12], _fk[9:12])
                        rHB = Rot("rHB", _ft[12:15], _fk[12:15])
                        rG2 = Rot("rG2", _ft[15:18], _fk[15:18])
                        bfr = Rot("bft", [sb("bft%d%s" % (i, stage), [128, BLK], BF16, scope=esp) for i in range(4)])
                        xbtr = Rot("xbt", [sb("xbt%d%s" % (i, stage), [128, BLK + 8], scope=esp) for i in range(3)])
                        vaugr = Rot("vaug", [sb("vaug%d%s" % (i, stage), [128, 4, 129], BF16, scope=esp) for i in range(2)])
                        psA = Rot("psA", [psf[0], psf[1]])
                        psB = Rot("psB", [psf[2], psf[3]])
                        psG = Rot("psG", [psf[4], psf[5], psf[6]])

                        if stage == "A":
                            DMA("pool", wab[:], wabd[:, :, :], w=["wab"])
                            DMA("pool", wib[:], wibd[:, :, :], w=["wib"])
                            PL("memset", hcar[:], 0.0, w=["hcar"])
                            PL("memset", xcar[:], 0.0, w=["xcar"])
                            DMA("sp", hcar[:, :, 1:3], shd[:, :, :], r=["hcar"], w=["hcar"])
                            DMA("sp", xcar[:, :, 1:3, :], scd[:, :, :, :], r=["xcar"], w=["xcar"])
                            for vi, vt_ in enumerate(vaugr.tiles):
                                PL("memset", vt_[:, :, 128:129], 1.0, w=[("vaug", vi)])
                            PL("memset", vsmp[:, :, :, 128:129], 1.0, w=["vsmp"])

                        def load_w(half, src):
                            DMA("pool", wbuf[:, :, half * 1024:(half + 1) * 1024],
                                src.rearrange("(c p) n -> p c n", p=128), w=[("W", half)])

                        def blocks(own_only):
                            lst = []
                            for b in range(NBLK):
                                if own_only and b < 6:
                                    continue
                                lst.append((b, b * BLK, BLK))
                            lst.append((8, NSLOT, NSMP))
                            return lst

                        def load_x(b, c0, n, fast=True):
                            xt, xk = xrotr.next()
                            tt, tk = tabr.next()
                            if False:
                                for j in range(NKC):
                                    stg, sk = stgr.next()
                                    DMA("sp", stg[:, 0:n], xrot[128 * j:128 * (j + 1), c0:c0 + n], w=[sk])
                                    AC(xt[:, j, 0:n], stg[:, 0:n], AF.Copy, r=[sk], w=[xk + (j,)])
                            else:
                                src = xrot[:, c0:c0 + n] if b < 8 else xown[:, TOWN:T2]
                                DMA("pool", xt[:, :, 0:n], src.rearrange("(c p) s -> p c s", p=128), w=[xk])
                            DMA("sp", tt[:, 0, 0:n], cosd[:, c0:c0 + n], w=[tk + (0,)])
                            DMA("sp", tt[:, 1, 0:n], sind[:, c0:c0 + n], w=[tk + (1,)])
                            DMA("sp", tt[:, 2, 0:n], vald[:, c0:c0 + n], w=[tk + (2,)])
                            return xt, xk, tt, tk

                        def proj(ps, psk, half, chunk, xt, xk, n):
                            for kc in range(NKC):
                                MM(ps[:, 0:n], wbuf[:, kc, half * 1024 + chunk * 128: half * 1024 + (chunk + 1) * 128],
                                   xt[:, kc, 0:n], r=[("W", half), xk], w=[psk], start=(kc == 0), stop=(kc == NKC - 1))

                        def own_col(b):
                            return {6: 0, 7: 512, 8: 1024}.get(b)

                        def rope_pass(w_a, w_b, is_q):
                            load_w(0, w_a)
                            for (b, c0, n) in blocks(own_only=is_q):
                                xt, xk, tt, tk = load_x(b, c0, n)
                                oc = own_col(b)
                                pend = None
                                for h in range(9):
                                    cur = None
                                    if h < 8:
                                        pa, pak = psA.next()
                                        proj(pa, pak, 0, h, xt, xk, n)
                                        ks, ksk = f32r.next()
                                        AC(ks[:, 0:n], pa[:, 0:n], AF.Copy, r=[pak], w=[ksk])
                                        cur = (h, ks, ksk)
                                    if pend is not None:
                                        rope_tail(pend[0], pend[1], pend[2], b, c0, n, oc, tt, tk, is_q)
                                    pend = cur

                        def rope_tail(h, ks, ksk, b, c0, n, oc, tt, tk, is_q):
                            pb, pbk = psB.next()
                            MM(pb[:, 0:n], permT[:], ks[:, 0:n], r=["permT", ksk], w=[pbk], start=True, stop=True)
                            t1, t1k = f32r.next()
                            t2, t2k = f32r.next()
                            DV("tensor_tensor", t1[:, 0:n], ks[:, 0:n], tt[:, 0, 0:n], ALU.mult, r=[ksk, tk + (0,)], w=[t1k])
                            DV("tensor_tensor", t2[:, 0:n], pb[:, 0:n], tt[:, 1, 0:n], ALU.mult, r=[pbk, tk + (1,)], w=[t2k])
                            if is_q:
                                DV("tensor_tensor", qT[:, h, oc:oc + n], t1[:, 0:n], t2[:, 0:n], ALU.add,
                                   r=[t1k, t2k], w=[("qT", h, b)])
                            else:
                                kf, kfk = f32r.next()
                                DV("tensor_tensor", kf[:, 0:n], t1[:, 0:n], t2[:, 0:n], ALU.add, r=[t1k, t2k], w=[kfk])
                                if b < 8:
                                    kb_, kbk = bfr.next()
                                    AC(kb_[:, 0:n], kf[:, 0:n], AF.Copy, r=[kfk], w=[kbk])
                                    DMA("sp", scr_k[h, :, c0:c0 + n], kb_[:, 0:n], r=[kbk], w=[("scrk", h)])
                                else:
                                    AC(ksmp[:, h, :], kf[:, 0:n], AF.Copy, r=[kfk], w=[("ksmp", h)])
                                if oc is not None:
                                    DMA("sp", koutT[h, :, oc:oc + n], kf[:, 0:n], r=[kfk])

                        if stage == "B":
                            rope_pass(wq, wqp, True)
                            CHK("p3")
                            P.barrier()
                            return
                        rope_pass(wk, wkp, False)
                        CHK("p1")

                        load_w(0, wv)
                        for (b, c0, n) in blocks(own_only=False):
                            xt, xk, tt, tk = load_x(b, c0, n)
                            oc = own_col(b)
                            pend = None
                            for h in range(9):
                                cur = None
                                if h < 8:
                                    pa, pak = psA.next()
                                    proj(pa, pak, 0, h, xt, xk, n)
                                    vb, vbk = bfr.next()
                                    AC(vb[:, 0:n], pa[:, 0:n], AF.Copy, r=[pak], w=[vbk])
                                    if oc is not None:
                                        vf, vfk = f32r.next()
                                        DV("tensor_copy", vf[:, 0:n], pa[:, 0:n], r=[pak], w=[vfk])
                                        DMA("sp", voutT[h, :, oc:oc + n], vf[:, 0:n], r=[vfk])
                                    cur = (h, vb, vbk)
                                if pend is not None:
                                    ph, pvb, pvbk = pend
                                    if b < 8:
                                        va, vak = vaugr.next()
                                        for s4 in range(4):
                                            TR(pst[:, s4, :], pvb[:, s4 * 128:(s4 + 1) * 128], ident[:], r=[pvbk, "ident"], w=["pst"])
                                        DV("tensor_copy", va[:, :, 0:128], pst[:, :, :], r=["pst"], w=[vak])
                                        DMA("sp", scr_v[ph, :, b * 4:(b + 1) * 4, :], va[:, :, :], r=[vak], w=[("scrv", ph)])
                                    else:
                                        for bi in range(2):
                                            TR(pst[0:16, bi, :], pvb[:, bi * 16:(bi + 1) * 16], ident[:], r=[pvbk, "ident"], w=["pst"])
                                        DV("tensor_copy", vsmp[:, :, ph, 0:128], pst[0:16, 0:2, :], r=["pst"], w=["vsmp"])
                                pend = cur
                            CHK("p2a_b%d" % b)
                        CHK("p2a")

                        load_w(0, wxb)
                        load_w(1, wg)
                        def lru_chain(b, n, ch, xt, xk, tt, tk, oc):
                            pa, pak = psA.next()
                            proj(pa, pak, 0, ch, xt, xk, n)
                            segs = [(0, 0, n)] if b < 8 else [(1, 0, 16), (2, 16, 16)]
                            xbt, xbk = xbtr.next()
                            xc, xck = rXC.next()
                            cwc = PV_CW + ch * 4
                            for (ci, s0, sn) in segs:
                                off = s0 + (3 if ci == 2 else 0)
                                kh = xbk + ("h", ci)
                                kd = xbk + ("d", ci)
                                DV("tensor_copy", xbt[:, off:off + 3], xcar[:, ch, ci, :], r=["xcar"], w=[kh])
                                AC(xbt[:, off + 3:off + 3 + sn], pa[:, s0:s0 + sn], AF.Copy, r=[pak], w=[kd])
                            yield
                            for (ci, s0, sn) in segs:
                                off = s0 + (3 if ci == 2 else 0)
                                kh = xbk + ("h", ci)
                                kd = xbk + ("d", ci)
                                DV("tensor_copy", xcar[:, ch, ci, :], xbt[:, off + sn:off + sn + 3], r=[kd, kh], w=["xcar"])
                                AC(xc[:, s0:s0 + sn], xbt[:, off + 3:off + 3 + sn], AF.Identity,
                                   r=[kd, kh, "pv"], w=[xck + (ci,)],
                                   scale=pv[:, cwc + 3:cwc + 4], bias=pv[:, PV_CB + ch:PV_CB + ch + 1])
                                yield
                                for j in range(3):
                                    DV("scalar_tensor_tensor", xc[:, s0:s0 + sn], xbt[:, off + j:off + j + sn],
                                       pv[:, cwc + j:cwc + j + 1], xc[:, s0:s0 + sn], ALU.mult, ALU.add,
                                       r=[kd, kh, xck + (ci,)], w=[xck + (ci,)])
                                    yield
                            xck_all = [xck + (ci,) for (ci, _, _) in segs]
                            xcb, xcbk = bfr.next()
                            AC(xcb[:, 0:n], xc[:, 0:n], AF.Copy, r=xck_all, w=[xcbk])
                            yield
                            pr, prk = psG.next()
                            pi, pik = psG.next()
                            MM(pr[:, 0:n], wab[:, ch, :], xcb[:, 0:n], r=["wab", xcbk], w=[prk], start=True, stop=True)
                            MM(pi[:, 0:n], wib[:, ch, :], xcb[:, 0:n], r=["wib", xcbk], w=[pik], start=True, stop=True)
                            tr, trk = rTR.next()
                            ti, tik = rTI.next()
                            AC(tr[:, 0:n], pr[:, 0:n], AF.Exp, r=[prk, "pvh"], w=[trk],
                               bias=pv[:, PV_HBA + ch:PV_HBA + ch + 1], scale=-1.0)
                            AC(ti[:, 0:n], pi[:, 0:n], AF.Exp, r=[pik, "pvh"], w=[tik],
                               bias=pv[:, PV_HBI + ch:PV_HBI + ch + 1], scale=-1.0)
                            yield
                            AC(tr[:, 0:n], tr[:, 0:n], AF.Ln, r=[trk], w=[trk], bias=1.0)
                            AC(ti[:, 0:n], ti[:, 0:n], AF.Ln, r=[tik], w=[tik], bias=1.0)
                            yield
                            AC(tr[:, 0:n], tr[:, 0:n], AF.Exp, r=[trk], w=[trk], scale=-1.0)
                            AC(ti[:, 0:n], ti[:, 0:n], AF.Exp, r=[tik], w=[tik], scale=-1.0)
                            yield
                            AC(tr[:, 0:n], tr[:, 0:n], AF.Exp, r=[trk, "pvc"], w=[trk],
                               scale=pv[:, PV_HC + ch:PV_HC + ch + 1])
                            DV("tensor_tensor", ti[:, 0:n], ti[:, 0:n], xc[:, 0:n], ALU.mult, r=[tik] + xck_all, w=[tik])
                            yield
                            b2, b2k = rB2.next()
                            AC(b2[:, 0:n], tr[:, 0:n], AF.Square, r=[trk], w=[b2k])
                            if b < 6:
                                DV("tensor_tensor", ti[:, 0:n], ti[:, 0:n], tt[:, 2, 0:n], ALU.mult, r=[tik, tk + (2,)], w=[tik])
                            yield
                            AC(b2[:, 0:n], b2[:, 0:n], AF.Ln, r=[b2k], w=[b2k], scale=-1.0, bias=1.0)
                            yield
                            AC(b2[:, 0:n], b2[:, 0:n], AF.Exp, r=[b2k], w=[b2k], scale=0.5)
                            yield
                            DV("tensor_tensor", b2[:, 0:n], b2[:, 0:n], ti[:, 0:n], ALU.mult, r=[b2k, tik], w=[b2k])
                            yield
                            hb, hbk = rHB.next()
                            for (ci, s0, sn) in segs:
                                DV("tensor_tensor_scan", hb[:, s0:s0 + sn], tr[:, s0:s0 + sn], b2[:, s0:s0 + sn],
                                   hcar[:, ch, ci:ci + 1], ALU.mult, ALU.add, r=[trk, b2k, "hcar"], w=[hbk + (ci,)])
                                yield
                                DV("tensor_copy", hcar[:, ch, ci:ci + 1], hb[:, s0 + sn - 1:s0 + sn], r=[hbk + (ci,)], w=["hcar"])
                            if oc is not None:
                                pg, pgk = psB.next()
                                proj(pg, pgk, 1, ch, xt, xk, n)
                                gs = xc
                                AC(gs[:, 0:n], pg[:, 0:n], AF.Copy, r=[pgk], w=[xck])
                                yield
                                g2, g2k = rG2.next()
                                AC(g2[:, 0:n], gs[:, 0:n], AF.Square, r=[xck], w=[g2k])
                                yield
                                DV("tensor_scalar", g2[:, 0:n], g2[:, 0:n], 0.044715, 1.0, ALU.mult, ALU.add, r=[g2k], w=[g2k])
                                yield
                                DV("tensor_tensor", g2[:, 0:n], g2[:, 0:n], gs[:, 0:n], ALU.mult, r=[g2k, xck], w=[g2k])
                                yield
                                AC(g2[:, 0:n], g2[:, 0:n], AF.Exp, r=[g2k], w=[g2k], scale=-1.5957691216057308)
                                yield
                                DV("tensor_scalar", g2[:, 0:n], g2[:, 0:n], 1.0, None, ALU.add, r=[g2k], w=[g2k])
                                yield
                                DV("reciprocal", g2[:, 0:n], g2[:, 0:n], r=[g2k], w=[g2k])
                                yield
                                DV("tensor_tensor", g2[:, 0:n], g2[:, 0:n], gs[:, 0:n], ALU.mult, r=[g2k, xck], w=[g2k])
                                yield
                                DV("tensor_tensor", mixT[:, 8 + ch, oc:oc + n], g2[:, 0:n], hb[:, 0:n], ALU.mult,
                                   r=[g2k] + [hbk + (ci,) for (ci, _, _) in segs], w=[("mix", 8 + ch, b)])

                        IL = 3
                        for (b, c0, n) in blocks(own_only=False):
                            xt, xk, tt, tk = load_x(b, c0, n, fast=False)
                            oc = own_col(b)
                            for g0 in range(0, 8, IL):
                                gens = [lru_chain(b, n, ch, xt, xk, tt, tk, oc) for ch in range(g0, min(8, g0 + IL))]
                                while gens:
                                    for g in list(gens):
                                        try:
                                            next(g)
                                        except StopIteration:
                                            gens.remove(g)
                        DMA("sp", hout[:, :, :], hcar[:], r=["hcar"])
                        DMA("sp", cout[:, :, :, :], xcar[:], r=["xcar"])
                        CHK("p2b")

                        P.barrier()

                qT = None
                run_passes("A")
                qT = sb("qT", [128, 8, T2], BF16, scope=es1)
                run_passes("B")

                with ExitStack() as esa:
                    psf = [esa.enter_context(nc.psum_tensor("pa%d" % i, [128, 512], F32)) for i in range(8)]
                    kTh = [sb("kTh%d" % i, [128, NSLOT], BF16, scope=esa) for i in range(2)]
                    Vh = [sb("Vh%d" % i, [128, 32, 129], BF16, scope=esa) for i in range(2)]
                    ptr = Rot("PT", [sb("PT%d" % i, [128, 2, BLK], BF16, scope=esa) for i in range(3)])
                    kbias = sb("kbias", [128, 32], scope=esa)
                    ckt = [sb("ckt%d" % i, [128, 2048], BF16, scope=esa) for i in range(2)]
                    cvt = [sb("cvt%d" % i, [128, 16, 129], BF16, scope=esa) for i in range(2)]
                    ep = Rot("ep", [sb("ep%d" % i, [128, 128], scope=esa) for i in range(4)])
                    epb = Rot("epb", [sb("epb%d" % i, [128, 128], scope=esa) for i in range(2)])
                    sm = Rot("sm", [sb("sm%d" % i, [128, 8], scope=esa) for i in range(4)])
                    junk = sb("junk", [128, 128], scope=esa)
                    onesb = sb("onesb", [128, 128], BF16, scope=esa)
                    epf = Rot("epf", [sb("epf%d" % i, [128, BLK], scope=esa) for i in range(7)])
                    PL("memset", onesb[:], 1.0, w=["onesb"])
                    DMA("sp", kbias[:], kbiasd[:, :], w=["kbias"])
                    for i in range(2):
                        PL("memset", cvt[i][:, :, 128:129], 1.0, w=[("cvt", i)])
                    psS = [[psf[0], psf[1]], [psf[2], psf[3]]]
                    accs = {}
                    order = [(s, c) for s in range(4) for c in range(2)]
                    for idx, sc in enumerate(order):
                        accs[sc] = (psf[4 + sc[0]], sc[1] * 129, 4 + sc[0])

                    def epilogue(npart, bank1, o1, bank2, o2, okeys, dst, dkey):
                        ps_ = slice(0, npart)
                        s_, sk = sm.next()
                        t_, tk_ = ep.next()
                        o_, ok_ = ep.next()
                        ob, obk = epb.next()
                        DV("reciprocal", s_[ps_, 0:1], bank1[ps_, o1 + 128:o1 + 129], r=okeys, w=[sk + (0,)])
                        DV("reciprocal", s_[ps_, 1:2], bank2[ps_, o2 + 128:o2 + 129], r=okeys, w=[sk + (1,)])
                        DV("tensor_tensor", s_[ps_, 2:3], s_[ps_, 1:2], lamt[ps_, 1:2], ALU.mult, r=[sk + (1,), "lam"], w=[sk + (2,)])
                        DV("tensor_scalar", t_[ps_, :], bank1[ps_, o1:o1 + 128], s_[ps_, 0:1], None, ALU.mult,
                           r=list(okeys) + [sk + (0,)], w=[tk_])
                        DV("scalar_tensor_tensor", o_[ps_, :], bank2[ps_, o2:o2 + 128], s_[ps_, 2:3], t_[ps_, :],
                           ALU.mult, ALU.add, r=list(okeys) + [sk + (2,), tk_], w=[ok_])
                        AC(junk[ps_, :], o_[ps_, :], AF.Square, r=[ok_], w=["junk", sk + (3,)], accum_out=s_[ps_, 3:4])
                        DV("tensor_scalar", s_[ps_, 4:5], s_[ps_, 3:4], 1.0 / 128.0, RMS_EPS, ALU.mult, ALU.add,
                           r=[sk + (3,)], w=[sk + (4,)])
                        AC(s_[ps_, 6:7], s_[ps_, 4:5], AF.Ln, r=[sk + (4,)], w=[sk + (6,)])
                        AC(s_[ps_, 5:6], s_[ps_, 6:7], AF.Exp, r=[sk + (6,)], w=[sk + (5,)], scale=-0.5)
                        DV("scalar_tensor_tensor", ob[ps_, :], o_[ps_, :], s_[ps_, 5:6], gsub[ps_, :], ALU.mult, ALU.mult,
                           r=[ok_, sk + (5,), "gsub"], w=[obk])
                        TR(bank1[:, 0:npart], ob[ps_, :], ident32[ps_, 0:npart], r=[obk, "ident32"] + list(okeys), w=list(okeys))
                        AC(dst, bank1[:, 0:npart], AF.Copy, r=list(okeys), w=[dkey])

                    def load_head(h):
                        DMA("sp", kTh[h % 2][:], scr_k[h, :, :], w=[("kTh", h % 2)])
                        DMA("sp", Vh[h % 2][:], scr_v[h, :, :, :], w=[("Vh", h % 2)])

                    steps = []
                    for h in range(8):
                        for sbk in range(2):
                            first_kb = 24 + 4 * sbk
                            for kb in range(first_kb + 4):
                                steps.append((h, sbk, kb, first_kb))
                    st_state = {}

                    def emit_qk(si):
                        h, sbk, kb, first_kb = steps[si]
                        kt = kTh[h % 2]
                        i = kb - first_kb
                        c0 = max(0, i) * 128
                        pss = psS[si % 2]
                        psk = [("psS", si % 2, 0), ("psS", si % 2, 1)]
                        for c in range(2):
                            MM(pss[c][:, c0:BLK], kt[64 * c:64 * c + 64, kb * 128:(kb + 1) * 128],
                               qT[64 * c:64 * c + 64, h, sbk * BLK + c0:(sbk + 1) * BLK],
                               r=[("kTh", h % 2)], w=[psk[c]], start=True, stop=True)

                    def emit_exp(si):
                        h, sbk, kb, first_kb = steps[si]
                        i = kb - first_kb
                        c0 = max(0, i) * 128
                        pss = psS[si % 2]
                        psk = [("psS", si % 2, 0), ("psS", si % 2, 1)]
                        pt, ptk = ptr.next()
                        st_state[si] = (pt, ptk)
                        if i < 0:
                            for c in range(2):
                                if kb < 24:
                                    AC(pt[:, c, :], pss[c][:, :], AF.Exp, r=[psk[c], "kbias"], w=[ptk + (c,)],
                                       bias=kbias[:, kb:kb + 1], scale=QSCALE)
                                else:
                                    AC(pt[:, c, :], pss[c][:, :], AF.Exp, r=[psk[c]], w=[ptk + (c,)], scale=QSCALE)
                        else:
                            for c in range(2):
                                AC(pt[0:64, c, c0:BLK], pss[c][0:64, c0:BLK], AF.Exp, r=[psk[c]], w=[ptk + (c, "a")], scale=QSCALE)
                                AC(pt[64:128, c, c0 + 64:BLK], pss[c][64:128, c0 + 64:BLK], AF.Exp, r=[psk[c]],
                                   w=[ptk + (c, "b")], scale=QSCALE)
                                AC(pt[64:128, c, c0:c0 + 64], zero_t[64:128, 0:64], AF.Copy, r=["zero"], w=[ptk + (c, "z")])

                    OT = [psf[4], psf[5]]
                    SM = [psf[6], psf[7]]

                    def emit_pv(si):
                        h, sbk, kb, first_kb = steps[si]
                        vt = Vh[h % 2]
                        vk = ("Vh", h % 2)
                        i = kb - first_kb
                        c0 = max(0, i) * 128
                        last = first_kb + 3
                        pt, ptk = st_state.pop(si)
                        for c in range(2):
                            rkeys = [ptk + (c,)] if i < 0 else [ptk + (c, "a"), ptk + (c, "b"), ptk + (c, "z")]
                            MM(OT[c][:, c0:BLK], vt[:, kb, 0:128], pt[:, c, c0:BLK], r=[vk] + rkeys, w=[("OT", c)],
                               start=(kb == 0), stop=(kb == last))
                            MM(SM[c][:, c0:BLK], onesb[:], pt[:, c, c0:BLK], r=["onesb"] + rkeys, w=[("SM", c)],
                               start=(kb == 0), stop=(kb == last))
                        if kb == last:
                            rs1, rs1k = epf.next()
                            rs2, rs2k = epf.next()
                            t_, tk_ = epf.next()
                            u_, uk_ = epf.next()
                            DV("reciprocal", rs1[:], SM[0][:, :], r=[("SM", 0)], w=[rs1k])
                            DV("reciprocal", rs2[:], SM[1][:, :], r=[("SM", 1)], w=[rs2k])
                            DV("tensor_tensor", t_[:], OT[0][:, :], rs1[:], ALU.mult, r=[("OT", 0), rs1k], w=[tk_])
                            DV("tensor_tensor", u_[:], OT[1][:, :], rs2[:], ALU.mult, r=[("OT", 1), rs2k], w=[uk_])
                            DV("scalar_tensor_tensor", t_[:], u_[:], lamt[:, 1:2], t_[:], ALU.mult, ALU.add,
                               r=[uk_, tk_, "lam"], w=[tk_])
                            AC(u_[:], t_[:], AF.Square, r=[tk_], w=[uk_])
                            MM(SM[0][:, :], ones32[:], u_[:], r=["ones32", uk_], w=[("SM", 0)], start=True, stop=True)
                            AC(rs1[:], SM[0][:, :], AF.Ln, r=[("SM", 0)], w=[rs1k], scale=1.0 / 128.0, bias=RMS_EPS)
                            AC(rs1[:], rs1[:], AF.Exp, r=[rs1k], w=[rs1k], scale=-0.5)
                            DV("scalar_tensor_tensor", mixT[:, h, sbk * BLK:(sbk + 1) * BLK], t_[:], pv[:, 152:153], rs1[:],
                               ALU.mult, ALU.mult, r=[tk_, rs1k, "pvg"], w=[("mix", h, 6 + sbk)])

                    load_head(0)
                    emit_qk(0)
                    for si in range(len(steps)):
                        h, sbk, kb, first_kb = steps[si]
                        if sbk == 0 and kb == 0 and h + 1 < 8:
                            load_head(h + 1)
                        if si + 1 < len(steps):
                            emit_qk(si + 1)
                        emit_exp(si)
                        emit_pv(si)
                    CHK("attn")

                    def load_cache(j):
                        bi, h = j // 8, j % 8
                        DMA("pool", ckt[j % 2][:], ckT[bi, h, :, :], w=[("ckt", j % 2)])
                        DMA("pool", cvt[j % 2][:, :, 0:128], cvd[bi, h, :, :, :], r=[("cvt", j % 2)], w=[("cvt", j % 2)])

                    ssteps = [(j, kb) for j in range(16) for kb in range(17)]
                    bankO = psf[4]
                    sst = {}

                    def s_qk(si):
                        j, kb = ssteps[si]
                        bi, h = j // 8, j % 8
                        q0 = TOWN + bi * 16
                        ck = ckt[j % 2]
                        pss = psS[si % 2]
                        np_ = 128 if kb < 16 else 16
                        for c in range(2):
                            if kb < 16:
                                lhs = ck[64 * c:64 * c + 64, kb * 128:(kb + 1) * 128]
                                rk_ = [("ckt", j % 2)]
                            else:
                                lhs = ksmp[64 * c:64 * c + 64, h, bi * 16:(bi + 1) * 16]
                                rk_ = []
                            MM(pss[c][0:np_, 0:16], lhs, qT[64 * c:64 * c + 64, h, q0:q0 + 16],
                               r=rk_, w=[("psS", si % 2, c)], start=True, stop=True)

                    def s_exp(si):
                        j, kb = ssteps[si]
                        pss = psS[si % 2]
                        np_ = 128 if kb < 16 else 16
                        pt, ptk = ptr.next()
                        sst[si] = (pt, ptk)
                        for c in range(2):
                            AC(pt[0:np_, c, 0:16], pss[c][0:np_, 0:16], AF.Exp, r=[("psS", si % 2, c)], w=[ptk + (c,)], scale=QSCALE)

                    def s_pv(si):
                        j, kb = ssteps[si]
                        bi, h = j // 8, j % 8
                        q0 = TOWN + bi * 16
                        cv = cvt[j % 2]
                        np_ = 128 if kb < 16 else 16
                        pt, ptk = sst.pop(si)
                        for c in range(2):
                            if kb < 16:
                                rhs = cv[:, kb, :]
                                rk_ = [("cvt", j % 2)]
                            else:
                                rhs = vsmp[0:16, bi, h, :]
                                rk_ = []
                            MM(bankO[0:16, c * 129:(c + 1) * 129], pt[0:np_, c, 0:16], rhs,
                               r=rk_ + [ptk + (c,)], w=[("psO", 0, c)],
                               start=(kb == 0 and c == 0), stop=(kb == 16), skip_group_check=True)
                        if kb == 16:
                            epilogue(16, bankO, 0, bankO, 129, [("psO", 0, 0), ("psO", 0, 1)],
                                     mixT[:, h, q0:q0 + 16], ("mix", h, 8, bi))

                    load_cache(0)
                    s_qk(0)
                    for si in range(len(ssteps)):
                        j, kb = ssteps[si]
                        if kb == 0 and j + 1 < 16:
                            load_cache(j + 1)
                        if si + 1 < len(ssteps):
                            s_qk(si + 1)
                        s_exp(si)
                        s_pv(si)
                    CHK("sattn")
                    P.barrier()

            with ExitStack() as es2:
                psf = [es2.enter_context(nc.psum_tensor("pm%d" % i, [128, 512], F32)) for i in range(6)]
                R = sb("R", [128, NKC, T2], scope=es2)
                XB = sb("XB", [128, NKC, T2], BF16, scope=es2)
                wtr = Rot("WT", [sb("WT%d" % i, [128, NKC, 512], BF16, scope=es2) for i in range(2)])
                sqr = Rot("sq", [sb("sq%d" % i, [128, TB], scope=es2) for i in range(4)])
                relr = Rot("rel", [sb("rel%d" % i, [128, TB], scope=es2) for i in range(4)])
                mean_t = sb("mean_t", [128, TB], scope=es2)
                rstd_t = sb("rstd_t", [128, TB], scope=es2)
                msq_t = sb("msq_t", [128, TB], scope=es2)
                lnr = Rot("lnt", [sb("lnt%d" % i, [128, TB], scope=es2) for i in range(4)])
                psR = Rot("psR", [psf[0], psf[1], psf[2], psf[3]])
                Hq = mixT

                for j in range(4):
                    DMA("sp", R[:, 4 * j:4 * j + 4, :], xown[512 * j:512 * (j + 1), :].rearrange("(c p) t -> p c t", p=128),
                        w=[("R", m) for m in range(4 * j, 4 * j + 4)])

                def load_wt(src, r0, c0):
                    wt, wtk = wtr.next()
                    DMA("pool", wt[:], src[r0:r0 + 2048, c0:c0 + 512].rearrange("(c p) n -> p c n", p=128), w=[wtk])
                    return wt, wtk

                def mm_group(wt, wtk, mp, act, akey, tb):
                    ps, psk = psR.next()
                    for kc in range(NKC):
                        MM(ps[:, 0:TB], wt[:, kc, mp * 128:(mp + 1) * 128], act[:, kc, tb * TB:(tb + 1) * TB],
                           r=[wtk, (akey, kc)], w=[psk], start=(kc == 0), stop=(kc == NKC - 1))
                    return ps, psk

                def layer_norm(gcol, bcol, write_bf):
                    for tb in range(NTB):
                        cs = slice(tb * TB, (tb + 1) * TB)
                        p1, p1k = psf[4], ("psLN", 0)
                        p2, p2k = psf[5], ("psLN", 1)
                        for m in range(NKC):
                            sq, sqk = sqr.next()
                            AC(sq[:], R[:, m, cs], AF.Square, r=[("R", m)], w=[sqk])
                            MM(p1[:, 0:TB], ones32[:], R[:, m, cs], r=["ones32", ("R", m)], w=[p1k],
                               start=(m == 0), stop=(m == NKC - 1))
                            MM(p2[:, 0:TB], ones32[:], sq[:], r=["ones32", sqk], w=[p2k],
                               start=(m == 0), stop=(m == NKC - 1))
                        DV("tensor_scalar", mean_t[:], p1[:, 0:TB], 1.0 / D, None, ALU.mult, r=[p1k], w=["mean"])
                        DV("tensor_tensor", msq_t[:], mean_t[:], mean_t[:], ALU.mult, r=["mean"], w=["msq"])
                        DV("scalar_tensor_tensor", rstd_t[:], p2[:, 0:TB], 1.0 / D, msq_t[:], ALU.mult, ALU.subtract,
                           r=[p2k, "msq"], w=["rstd"])
                        DV("tensor_scalar", rstd_t[:], rstd_t[:], LN_EPS, None, ALU.add, r=["rstd"], w=["rstd"])
                        AC(rstd_t[:], rstd_t[:], AF.Ln, r=["rstd"], w=["rstd"])
                        AC(rstd_t[:], rstd_t[:], AF.Exp, r=["rstd"], w=["rstd"], scale=-0.5)
                        for m in range(NKC):
                            t_, tk_ = lnr.next()
                            DV("tensor_tensor", t_[:], R[:, m, cs], mean_t[:], ALU.subtract, r=[("R", m), "mean"], w=[tk_])
                            DV("tensor_tensor", t_[:], t_[:], rstd_t[:], ALU.mult, r=[tk_, "rstd"], w=[tk_])
                            DV("tensor_scalar", R[:, m, cs], t_[:], pv[:, gcol + m:gcol + m + 1], pv[:, bcol + m:bcol + m + 1],
                               ALU.mult, ALU.add, r=[tk_, "pv"], w=[("R", m)])
                            if write_bf:
                                AC(XB[:, m, cs], R[:, m, cs], AF.Copy, r=[("R", m)], w=[("XB", m)])

                for j in range(4):
                    wt, wtk = load_wt(w_out, 0, 512 * j)
                    for mp in range(4):
                        m = 4 * j + mp
                        for tb in range(NTB):
                            ps, psk = mm_group(wt, wtk, mp, mixT, "H", tb)
                            cs = slice(tb * TB, (tb + 1) * TB)
                            DV("scalar_tensor_tensor", R[:, m, cs], R[:, m, cs], ALPHA, ps[:, 0:TB], ALU.mult, ALU.add,
                               r=[("R", m), psk], w=[("R", m)])
                layer_norm(PV_G1, PV_B1, True)

                for Q in range(4):
                    for j in range(4):
                        wt, wtk = load_wt(w_up, 0, Q * 2048 + 512 * j)
                        for mp in range(4):
                            fc = 4 * j + mp
                            for tb in range(NTB):
                                ps, psk = mm_group(wt, wtk, mp, XB, "XB", tb)
                                cs = slice(tb * TB, (tb + 1) * TB)
                                rl, rlk = relr.next()
                                AC(rl[:], ps[:, 0:TB], AF.Relu, r=[psk], w=[rlk])
                                DV("tensor_tensor", Hq[:, fc, cs], rl[:], rl[:], ALU.mult, r=[rlk], w=[("H", fc)])
                    for j in range(4):
                        wt, wtk = load_wt(w_down, Q * 2048, 512 * j)
                        for mp in range(4):
                            m = 4 * j + mp
                            for tb in range(NTB):
                                ps, psk = mm_group(wt, wtk, mp, Hq, "H", tb)
                                cs = slice(tb * TB, (tb + 1) * TB)
                                if Q == 0:
                                    DV("scalar_tensor_tensor", R[:, m, cs], R[:, m, cs], ALPHA, ps[:, 0:TB], ALU.mult, ALU.add,
                                       r=[("R", m), psk], w=[("R", m)])
                                else:
                                    DV("tensor_tensor", R[:, m, cs], R[:, m, cs], ps[:, 0:TB], ALU.add,
                                       r=[("R", m), psk], w=[("R", m)])
                layer_norm(PV_G2, PV_B2, False)
                for j in range(4):
                    DMA("sp", yT[512 * j:512 * (j + 1), :].rearrange("(c p) t -> p c t", p=128), R[:, 4 * j:4 * j + 4, :],
                        r=[("R", m) for m in range(4 * j, 4 * j + 4)])
                P.barrier()

        except _Stop:
            pass

        P.emit()
    return nc


def _perm_matrix():
    m = np.arange(128)
    d = m % 64
    partner = np.where(d < 32, m + 32, m - 32)
    P = np.zeros((128, 128), np.float32)
    P[partner, m] = 1.0
    return P


def _rope_tables(pos):
    inv = (np.float32(10000.0) ** (-(np.arange(0, 64, 2, dtype=np.float32) / np.float32(64)))).astype(np.float32)
    ang = pos.astype(np.float32)[:, None] * inv[None, :]
    cos = np.cos(ang).astype(np.float32)
    sin = np.sin(ang).astype(np.float32)
    d = np.arange(128) % 64
    f = d % 32
    sgn = np.where(d < 32, -1.0, 1.0).astype(np.float32)
    cosT = np.ascontiguousarray(cos[:, f].T)
    sinT = np.ascontiguousarray((sin[:, f] * sgn[None, :]).T)
    return cosT, sinT


_NC_CACHE = {}
import os as _os
_DBG = _os.environ.get('KDBG', '')


def _prep(x_prompt, x_sample, cache_k, cache_v, state_h, state_conv,
           w_in, lambda_q1, lambda_k1, lambda_q2, lambda_k2, subln_g,
           conv_w, conv_b, w_rg_a, b_rg_a, w_rg_i, b_rg_i, lru_lambda,
           w_out, ln1_g, ln1_b, w_up, w_down, ln2_g, ln2_b):
    f32 = np.float32
    x_prompt = np.asarray(x_prompt, f32); x_sample = np.asarray(x_sample, f32)
    cache_k = np.asarray(cache_k, f32); cache_v = np.asarray(cache_v, f32)
    w_in0 = np.asarray(w_in, f32)[0]
    wq_ = w_in0[:, 0:1024]; wk_ = w_in0[:, 1024:2048]; wv_ = w_in0[:, 2048:3072]
    wxb_ = w_in0[:, 3072:4096]; wg_ = w_in0[:, 4096:5120]
    col = np.arange(1024)
    dd = col % 64
    partner = np.where(dd < 32, col + 32, col - 32)
    shared = {
        "wq": np.ascontiguousarray(wq_), "wqp": np.ascontiguousarray(wq_[:, partner]),
        "wk": np.ascontiguousarray(wk_), "wkp": np.ascontiguousarray(wk_[:, partner]),
        "wv": np.ascontiguousarray(wv_), "wxb": np.ascontiguousarray(wxb_), "wg": np.ascontiguousarray(wg_),
        "w_out": np.ascontiguousarray(np.asarray(w_out, f32)[0]),
        "w_up": np.ascontiguousarray(np.asarray(w_up, f32)[0]),
        "w_down": np.ascontiguousarray(np.asarray(w_down, f32)[0]),
        "identd": np.eye(128, dtype=f32),
        "permd": _perm_matrix(),
        "subgd": np.ascontiguousarray(np.broadcast_to(np.asarray(subln_g, f32)[0][None, :], (128, 128))),
    }
    lam4 = np.stack([np.asarray(v, f32)[0] for v in (lambda_q1, lambda_k1, lambda_q2, lambda_k2)], 0)
    shared["lamd"] = np.ascontiguousarray(np.broadcast_to(lam4[None], (128, 4, 64)))
    wa = np.asarray(w_rg_a, f32)[0]; wi = np.asarray(w_rg_i, f32)[0]
    wabd = np.zeros((128, 8, 128), f32); wibd = np.zeros((128, 8, 128), f32)
    for ch in range(8):
        for t in range(2):
            wabd[64 * t:64 * t + 64, ch, 64 * t:64 * t + 64] = wa[2 * ch + t]
            wibd[64 * t:64 * t + 64, ch, 64 * t:64 * t + 64] = wi[2 * ch + t]
    shared["wabd"] = wabd; shared["wibd"] = wibd
    pvd = np.zeros((128, 160), f32)

    def pc(v, n):
        return np.asarray(v, f32).reshape(n, 128).T

    cw = np.asarray(conv_w, f32)[0]
    for ch in range(8):
        for j in range(4):
            pvd[:, ch * 4 + j] = cw[j, ch * 128:(ch + 1) * 128]
    pvd[:, 32:40] = pc(np.asarray(conv_b)[0], 8)
    pvd[:, 40:48] = pc(np.asarray(b_rg_a)[0], 8)
    pvd[:, 48:56] = pc(np.asarray(b_rg_i)[0], 8)
    pvd[:, 56:64] = pc(np.asarray(lru_lambda)[0], 8)
    pvd[:, 64:80] = pc(np.asarray(ln1_g)[0], 16)
    pvd[:, 80:96] = pc(np.asarray(ln1_b)[0], 16)
    pvd[:, 96:112] = pc(np.asarray(ln2_g)[0], 16)
    pvd[:, 112:128] = pc(np.asarray(ln2_b)[0], 16)
    pvd[:, 152] = np.asarray(subln_g, f32)[0]
    shared["pvd"] = pvd

    in_maps = []
    for c in range(8):
        b, qi = c // 4, c % 4
        shift = 3072 - 1024 * qi
        m = dict(shared)
        xr = np.zeros((D, NSLOT), f32)
        xr[:, shift:] = x_prompt[b, :1024 * (qi + 1), :].T
        m["xrot"] = xr
        xo = np.empty((D, T2), f32)
        xo[:, :TOWN] = x_prompt[b, 1024 * qi:1024 * (qi + 1), :].T
        xo[:, TOWN:] = x_sample[2 * c:2 * c + 2].reshape(NSMP, D).T
        m["xown"] = xo
        pos = np.concatenate([np.maximum(np.arange(NSLOT) - shift, 0), 2048 + np.arange(16), 2048 + np.arange(16)])
        cosT, sinT = _rope_tables(pos)
        m["cosd"] = cosT; m["sind"] = sinT
        valid = np.concatenate([(np.arange(NSLOT) >= shift).astype(f32), np.ones(NSMP, f32)])
        m["vald"] = np.ascontiguousarray(np.broadcast_to(valid[None, :], (128, NSLOT + NSMP)))
        kb_valid = (np.arange(32) * 128 >= shift)
        m["kbiasd"] = np.ascontiguousarray(np.broadcast_to(np.where(kb_valid, 0.0, NEG).astype(f32)[None, :], (128, 32)))
        ck = cache_k[0, 2 * c:2 * c + 2]
        m["ckT"] = np.ascontiguousarray(ck.transpose(0, 2, 3, 1))
        cv = cache_v[0, 2 * c:2 * c + 2].reshape(2, 16, 128, 8, 128)
        m["cvd"] = np.ascontiguousarray(cv.transpose(0, 3, 2, 1, 4))
        sh = np.asarray(state_h, f32)[0, 2 * c:2 * c + 2]
        m["shd"] = np.ascontiguousarray(sh.reshape(2, 8, 128).transpose(2, 1, 0))
        sc = np.asarray(state_conv, f32)[0, 2 * c:2 * c + 2]
        m["scd"] = np.ascontiguousarray(sc.reshape(2, 3, 8, 128).transpose(3, 2, 0, 1))
        in_maps.append(m)

    return in_maps


def _assemble(R):
    f32 = np.float32
    y_prompt = np.empty((2, 4096, D), f32); y_sample = np.empty((16, 16, D), f32)
    k_prompt = np.empty((1, 2, 4096, 8, 128), f32); v_prompt = np.empty((1, 2, 4096, 8, 128), f32)
    h_prompt = np.empty((1, 2, 1024), f32); conv_prompt = np.empty((1, 2, 3, 1024), f32)
    k_sample = np.empty((1, 16, 16, 8, 128), f32); v_sample = np.empty((1, 16, 16, 8, 128), f32)
    h_sample = np.empty((1, 16, 1024), f32); conv_sample = np.empty((1, 16, 3, 1024), f32)
    for c in range(8):
        b, qi = c // 4, c % 4
        r = R[c]
        yT = np.asarray(r["yT"], f32)
        y_prompt[b, 1024 * qi:1024 * (qi + 1)] = yT[:, :TOWN].T
        y_sample[2 * c:2 * c + 2] = yT[:, TOWN:].T.reshape(2, 16, D)
        kT = np.asarray(r["koutT"], f32); vT = np.asarray(r["voutT"], f32)
        k_prompt[0, b, 1024 * qi:1024 * (qi + 1)] = kT[:, :, :TOWN].transpose(2, 0, 1)
        v_prompt[0, b, 1024 * qi:1024 * (qi + 1)] = vT[:, :, :TOWN].transpose(2, 0, 1)
        k_sample[0, 2 * c:2 * c + 2] = kT[:, :, TOWN:].transpose(2, 0, 1).reshape(2, 16, 8, 128)
        v_sample[0, 2 * c:2 * c + 2] = vT[:, :, TOWN:].transpose(2, 0, 1).reshape(2, 16, 8, 128)
        ho = np.asarray(r["hout"], f32)
        co = np.asarray(r["cout"], f32)
        if qi == 3:
            h_prompt[0, b] = ho[:, :, 0].T.reshape(1024)
            conv_prompt[0, b] = co[:, :, 0, :].transpose(2, 1, 0).reshape(3, 1024)
        for bi in range(2):
            h_sample[0, 2 * c + bi] = ho[:, :, 1 + bi].T.reshape(1024)
            conv_sample[0, 2 * c + bi] = co[:, :, 1 + bi, :].transpose(2, 1, 0).reshape(3, 1024)
    return (y_prompt, y_sample, k_prompt, v_prompt, h_prompt, conv_prompt,
            k_sample, v_sample, h_sample, conv_sample)


def kernel(x_prompt, x_sample, cache_k, cache_v, state_h, state_conv,
           w_in, lambda_q1, lambda_k1, lambda_q2, lambda_k2, subln_g,
           conv_w, conv_b, w_rg_a, b_rg_a, w_rg_i, b_rg_i, lru_lambda,
           w_out, ln1_g, ln1_b, w_up, w_down, ln2_g, ln2_b):
    in_maps = _prep(x_prompt, x_sample, cache_k, cache_v, state_h, state_conv,
                    w_in, lambda_q1, lambda_k1, lambda_q2, lambda_k2, subln_g,
                    conv_w, conv_b, w_rg_a, b_rg_a, w_rg_i, b_rg_i, lru_lambda,
                    w_out, ln1_g, ln1_b, w_up, w_down, ln2_g, ln2_b)
    if "nc" not in _NC_CACHE:
        _NC_CACHE["nc"] = build_nc()
    nc = _NC_CACHE["nc"]
    res = run_bass_kernel_spmd(nc, in_maps, core_ids=list(range(8)))
    return _assemble(res.results)
```
